# Optimizing a Trainium2 kernel written in Bass

```python
import math
import numpy as np
import jax
import jax.numpy as jnp
from jax import lax

D_MODEL = 1024
BATCH = 8
SEQ = 2048
DEPTH = 4

MEM_LEN = 256
MIX_W = D_MODEL // 2
NSA_DH = 64
NSA_HEADS = MIX_W // NSA_DH
NSA_KV = 2
NSA_R = NSA_HEADS // NSA_KV
NSA_ROT = NSA_DH // 4
CMP_L = 32
CMP_D = 16
CMP_HID = 256
SEL_L = 64
N_SEL = 8
WINDOW = 256
QB = 128
HB = 4
HK = 128
HV = MIX_W // HB
HGRN_CHUNK = 64
HC = 8
NOPE = 64
ROPE_D = 32
VD = MIX_W // HC
Q_RANK = 384
KV_RANK = 256
XA_HEADS = 4
XA_DH = D_MODEL // XA_HEADS
D_FF = 256 * ((8 * D_MODEL // 3 + 255) // 256)
ROPE_THETA = 500000.0
LN_EPS = 1e-5
RMS_EPS = 1e-6
NEG = -1e30
BIG = 1e9
F_MIN = 1e-20
F32 = jnp.float32

IN_SIZES = (
    NSA_HEADS * NSA_DH,
    NSA_KV * NSA_DH, NSA_KV * NSA_DH,
    NSA_KV * NSA_DH, NSA_KV * NSA_DH,
    NSA_KV * NSA_DH, NSA_KV * NSA_DH,
    NSA_HEADS * 3,
    HB * HK, HB * HK,
    HB * HV, HB * HV,
    Q_RANK, KV_RANK, ROPE_D,
    3 * D_MODEL,
)
N_IN = sum(IN_SIZES)

kernel_name = 'hybrid_nsa_hgrn2_mla_deepnorm_macaron'


def layer_norm(x, g, b):
    xf = x.astype(F32)
    mu = jnp.mean(xf, -1, keepdims=True)
    var = jnp.mean(jnp.square(xf - mu), -1, keepdims=True)
    return ((xf - mu) * lax.rsqrt(var + LN_EPS) * g + b).astype(x.dtype)


def rms_norm(x, g):
    xf = x.astype(F32)
    return (xf * lax.rsqrt(jnp.mean(jnp.square(xf), -1, keepdims=True) + RMS_EPS) * g).astype(x.dtype)


def rope(x, pos, rot_dim):
    half = rot_dim // 2
    inv = ROPE_THETA ** (-jnp.arange(half, dtype=F32) / half)
    ang = pos.astype(F32)[..., None] * inv
    cos = jnp.cos(ang)[:, :, None, :]
    sin = jnp.sin(ang)[:, :, None, :]
    xr = x[..., :rot_dim].astype(F32)
    x1, x2 = xr[..., :half], xr[..., half:]
    rot = jnp.concatenate([x1 * cos - x2 * sin, x2 * cos + x1 * sin], -1).astype(x.dtype)
    return jnp.concatenate([rot, x[..., rot_dim:]], -1)


def masked_softmax(s, mask, axis=-1):
    p = jax.nn.softmax(jnp.where(mask, s, NEG), axis=axis)
    return jnp.where(mask, p, 0.0)


def swiglu(h, w1, w3, w2):
    return (jax.nn.silu(h @ w1) * (h @ w3)) @ w2


def nsa_mixer(q, kc, vc, ks, vs, kw, vw, gates, pos, cmp_pos, cmp_w1, cmp_w2):
    B, S = q.shape[:2]
    G, R, dh = NSA_KV, NSA_R, NSA_DH
    dt = q.dtype
    scale = dh ** -0.5
    t = jnp.arange(S)
    nq = S // QB
    qg = q.reshape(B, S, G, R, dh)
    qr = rope(q, pos, NSA_ROT).reshape(B, S, G, R, dh)
    ks = rope(ks, pos, NSA_ROT)
    kw = rope(kw, pos, NSA_ROT)

    n_cmp = (S - CMP_L) // CMP_D + 1
    start = np.arange(n_cmp) * CMP_D
    blk_idx = start[:, None] + np.arange(CMP_L)[None, :]

    def compress(z, pe, w1, w2):
        blocks = z[:, blk_idx] + pe[None, None, :, None, :]
        flat = blocks.transpose(0, 1, 3, 2, 4).reshape(B, n_cmp, G, CMP_L * dh)
        return jax.nn.gelu(flat @ w1) @ w2

    k_cmp = compress(kc, cmp_pos[0], cmp_w1[0], cmp_w2[0])
    v_cmp = compress(vc, cmp_pos[1], cmp_w1[1], cmp_w2[1])
    s_cmp = jnp.einsum('bsgrd,bngd->bgrsn', qg, k_cmp).astype(F32) * scale
    cmask = jnp.asarray(start + CMP_L - 1)[None, :] <= t[:, None]
    p_cmp = masked_softmax(s_cmp, cmask)
    o_cmp = jnp.einsum('bgrsn,bngd->bsgrd', p_cmp.astype(dt), v_cmp)

    n_slc = S // SEL_L
    j_np = np.arange(n_slc)
    overlap = np.clip(np.minimum(start[:, None] + CMP_L, (j_np[None, :] + 1) * SEL_L)
                      - np.maximum(start[:, None], j_np[None, :] * SEL_L), 0, None) / CMP_L
    imp = jnp.einsum('bgsn,nj->bgsj', p_cmp.sum(2), jnp.asarray(overlap, dtype=F32))
    j = jnp.arange(n_slc)
    cur = (t // SEL_L)[:, None]
    blk_valid = j[None, :] * SEL_L <= t[:, None]
    forced = (j[None, :] == 0) | (j[None, :] == cur) | (j[None, :] == cur - 1)
    score = jnp.where(blk_valid & forced, BIG, jnp.where(blk_valid, imp, -BIG))
    k_top = min(N_SEL, n_slc)
    top_val, top_idx = lax.top_k(score, k_top)
    top_ok = top_val > -0.5 * BIG

    k_blk = ks.reshape(B, n_slc, SEL_L, G, dh).transpose(0, 3, 1, 2, 4)
    v_blk = vs.reshape(B, n_slc, SEL_L, G, dh).transpose(0, 3, 1, 2, 4)
    bi = jnp.arange(B)[:, None, None, None]
    gi = jnp.arange(G)[None, :, None, None]
    q_ch = qr.reshape(B, nq, QB, G, R, dh).transpose(1, 0, 3, 2, 4, 5)
    idx_ch = top_idx.reshape(B, G, nq, QB, k_top).transpose(2, 0, 1, 3, 4)
    ok_ch = top_ok.reshape(B, G, nq, QB, k_top).transpose(2, 0, 1, 3, 4)
    t_ch = t.reshape(nq, QB)

    def sel_block(args):
        qc, ic, okc, tc = args
        kg = k_blk[bi, gi, ic]
        vg = v_blk[bi, gi, ic]
        s = jnp.einsum('bgtrd,bgtkld->bgtrkl', qc, kg).astype(F32) * scale
        kpos = ic[..., None] * SEL_L + jnp.arange(SEL_L)
        m = (kpos <= tc[None, None, :, None, None]) & okc[..., None]
        m = m[:, :, :, None].reshape(B, G, QB, 1, k_top * SEL_L)
        p = masked_softmax(s.reshape(B, G, QB, R, k_top * SEL_L), m).reshape(s.shape)
        return jnp.einsum('bgtrkl,bgtkld->bgtrd', p.astype(dt), vg)

    o_sel = lax.map(sel_block, (q_ch, idx_ch, ok_ch, t_ch))
    o_sel = o_sel.transpose(1, 0, 3, 2, 4, 5).reshape(B, S, G, R, dh)

    nwb = WINDOW // QB

    def band(z):
        zb = jnp.pad(z.reshape(B, nq, QB, G, dh), ((0, 0), (nwb, 0), (0, 0), (0, 0), (0, 0)))
        return jnp.concatenate([zb[:, o:o + nq] for o in range(nwb + 1)], axis=2)

    kwb, vwb = band(kw), band(vw)
    qpos = t.reshape(nq, QB)
    kpos = (jnp.arange(nq)[:, None] - nwb) * QB + jnp.arange((nwb + 1) * QB)[None, :]
    wmask = ((kpos[:, None, :] <= qpos[:, :, None]) & (kpos[:, None, :] > qpos[:, :, None] - WINDOW)
             & (kpos[:, None, :] >= 0))
    s_w = jnp.einsum('bnqgrd,bnkgd->bngrqk', qr.reshape(B, nq, QB, G, R, dh), kwb).astype(F32) * scale
    p_w = masked_softmax(s_w, wmask[None, :, None, None])
    o_win = jnp.einsum('bngrqk,bnkgd->bnqgrd', p_w.astype(dt), vwb).reshape(B, S, G, R, dh)

    gt = jax.nn.sigmoid(gates.astype(F32).reshape(B, S, G, R, 3)).astype(dt)
    o = gt[..., 0:1] * o_cmp + gt[..., 1:2] * o_sel + gt[..., 2:3] * o_win
    return o.reshape(B, S, G * R * dh)


def hgrn2_mixer(q, fz, i_in, g_out, lb, norm_g):
    B, S = q.shape[:2]
    H, dk, dv, C = HB, HK, HV, HGRN_CHUNK
    nc = S // C
    lb = lb.reshape(H, dk).astype(F32)
    zf = fz.reshape(B, S, H, dk).astype(F32)
    f = lb + (1.0 - lb) * jax.nn.sigmoid(zf)
    log_f = jnp.log(jnp.maximum(f, F_MIN))
    k = (1.0 - lb) * jax.nn.sigmoid(-zf)

    def chunks(z, d):
        return z.reshape(B, nc, C, H, d).transpose(1, 0, 3, 2, 4)

    qc = chunks(q.reshape(B, S, H, dk).astype(F32), dk)
    kc = chunks(k, dk)
    vc = chunks(i_in.reshape(B, S, H, dv).astype(F32), dv)
    lfc = chunks(log_f, dk)
    causal = jnp.tril(jnp.ones((C, C), dtype=bool))

    def step(state, inp):
        qt, kt, vt, lf = inp
        b = jnp.cumsum(lf, axis=2)
        inter = jnp.einsum('bhtd,bhde->bhte', qt * jnp.exp(b), state)
        diff = b[:, :, :, None, :] - b[:, :, None, :, :]
        decay = jnp.exp(jnp.where(causal[:, :, None], diff, NEG))
        a = jnp.einsum('bhtd,bhtsd,bhsd->bhts', qt, decay, kt)
        intra = jnp.einsum('bhts,bhse->bhte', a, vt)
        b_last = b[:, :, -1:, :]
        new_state = (jnp.exp(b_last[:, :, 0, :, None]) * state
                     + jnp.einsum('bhsd,bhse->bhde', kt * jnp.exp(b_last - b), vt))
        return new_state, inter + intra

    state0 = jnp.zeros((B, H, dk, dv), F32)
    _, o = lax.scan(step, state0, (qc, kc, vc, lfc))
    o = o.transpose(1, 0, 3, 2, 4).reshape(B, S, H, dv)
    o = rms_norm(o, norm_g) * jax.nn.silu(g_out.reshape(B, S, H, dv).astype(F32))
    return o.reshape(B, S, H * dv).astype(q.dtype)


def mla_mixer(cq, ckv, kr, pos, qn_g, w_uq, kvn_g, w_ukv):
    B, S = cq.shape[:2]
    q = (rms_norm(cq, qn_g) @ w_uq).reshape(B, S, HC, NOPE + ROPE_D)
    q_nope = q[..., :NOPE]
    q_pe = rope(q[..., NOPE:], pos, ROPE_D)
    kv = (rms_norm(ckv, kvn_g) @ w_ukv).reshape(B, S, HC, NOPE + VD)
    k_nope, v = kv[..., :NOPE], kv[..., NOPE:]
    k_pe = rope(kr[:, :, None, :], pos, ROPE_D)[:, :, 0]
    scale = (NOPE + ROPE_D) ** -0.5
    nq = S // QB
    t = jnp.arange(S)
    qn_ch = q_nope.reshape(B, nq, QB, HC, NOPE).transpose(1, 0, 2, 3, 4)
    qp_ch = q_pe.reshape(B, nq, QB, HC, ROPE_D).transpose(1, 0, 2, 3, 4)

    def attn_block(args):
        qn, qp, tc = args
        s = (jnp.einsum('bqhd,bkhd->bhqk', qn, k_nope)
             + jnp.einsum('bqhr,bkr->bhqk', qp, k_pe)).astype(F32) * scale
        p = masked_softmax(s, t[None, :] <= tc[:, None])
        return jnp.einsum('bhqk,bkhd->bqhd', p.astype(v.dtype), v)

    o = lax.map(attn_block, (qn_ch, qp_ch, t.reshape(nq, QB)))
    return o.transpose(1, 0, 2, 3, 4).reshape(B, S, HC * VD)


def memory_cross_attention(h, mem, wq, wk, wv, wo):
    B, S = h.shape[:2]
    q = (h @ wq).reshape(B, S, XA_HEADS, XA_DH)
    k = (mem @ wk).reshape(B, -1, XA_HEADS, XA_DH)
    v = (mem @ wv).reshape(B, -1, XA_HEADS, XA_DH)
    s = jnp.einsum('bshd,bmhd->bhsm', q, k).astype(F32) * (XA_DH ** -0.5)
    p = jax.nn.softmax(s, axis=-1).astype(h.dtype)
    return jnp.einsum('bhsm,bmhd->bshd', p, v).reshape(B, S, D_MODEL) @ wo


def setup_inputs(seed: int = 0) -> dict:
    key = jax.random.key(seed)
    ks = jax.random.split(key, 26)
    beta = (8.0 * DEPTH) ** -0.25

    def w(k, shape, fan_in, s=1.0):
        return jax.random.normal(k, shape, F32) * (s * fan_in ** -0.5)

    def gain(k, shape):
        return 1.0 + 0.02 * jax.random.normal(k, shape, F32)

    return {
        'x': jax.random.normal(ks[0], (BATCH, SEQ, D_MODEL), F32),
        'mem': jax.random.normal(ks[1], (BATCH, MEM_LEN, D_MODEL), F32),
        'positions': jnp.broadcast_to(jnp.arange(SEQ, dtype=jnp.int32), (BATCH, SEQ)),
        'ln_g': gain(ks[2], (DEPTH, 4, D_MODEL)),
        'ln_b': 0.02 * jax.random.normal(ks[3], (DEPTH, 4, D_MODEL), F32),
        'ffn_w1': w(ks[4], (DEPTH, 2, D_MODEL, D_FF), D_MODEL),
        'ffn_w3': w(ks[5], (DEPTH, 2, D_MODEL, D_FF), D_MODEL),
        'ffn_w2': w(ks[6], (DEPTH, 2, D_FF, D_MODEL), D_FF, beta),
        'w_in': w(ks[7], (DEPTH, D_MODEL, N_IN), D_MODEL),
        'nsa_cmp_pos': 0.1 * jax.random.normal(ks[8], (DEPTH, 2, CMP_L, NSA_DH), F32),
        'nsa_cmp_w1': w(ks[9], (DEPTH, 2, CMP_L * NSA_DH, CMP_HID), CMP_L * NSA_DH),
        'nsa_cmp_w2': w(ks[10], (DEPTH, 2, CMP_HID, NSA_DH), CMP_HID),
        'hgrn_lb_logits': 0.1 * jax.random.normal(ks[11], (DEPTH, HB * HK), F32),
        'hgrn_norm_g': gain(ks[12], (DEPTH, HV)),
        'mla_q_norm_g': gain(ks[13], (DEPTH, Q_RANK)),
        'mla_w_uq': w(ks[14], (DEPTH, Q_RANK, HC * (NOPE + ROPE_D)), Q_RANK),
        'mla_kv_norm_g': gain(ks[15], (DEPTH, KV_RANK)),
        'mla_w_ukv': w(ks[16], (DEPTH, KV_RANK, HC * (NOPE + VD)), KV_RANK),
        'w_branch': w(ks[17], (DEPTH, 3, MIX_W, D_MODEL), MIX_W, beta),
        'w_out': w(ks[18], (DEPTH, D_MODEL, D_MODEL), D_MODEL, beta),
        'xa_wq': w(ks[19], (DEPTH, D_MODEL, D_MODEL), D_MODEL),
        'xa_wk': w(ks[20], (DEPTH, D_MODEL, D_MODEL), D_MODEL),
        'xa_wv': w(ks[21], (DEPTH, D_MODEL, D_MODEL), D_MODEL),
        'xa_wo': w(ks[22], (DEPTH, D_MODEL, D_MODEL), D_MODEL, beta),
    }


def reference(x, mem, positions, ln_g, ln_b, ffn_w1, ffn_w3, ffn_w2, w_in, nsa_cmp_pos,
              nsa_cmp_w1, nsa_cmp_w2, hgrn_lb_logits, hgrn_norm_g, mla_q_norm_g, mla_w_uq,
              mla_kv_norm_g, mla_w_ukv, w_branch, w_out, xa_wq, xa_wk, xa_wv, xa_wo):
    B, S, _ = x.shape
    alpha = (2.0 * DEPTH) ** 0.25
    split_points = np.cumsum(IN_SIZES)[:-1].tolist()
    p_lb = jax.nn.softmax(hgrn_lb_logits.astype(F32), axis=0)
    lower_bounds = jnp.cumsum(p_lb, axis=0) - p_lb[0:1]
    for l in range(DEPTH):
        x = layer_norm(alpha * x + 0.5 * swiglu(x, ffn_w1[l, 0], ffn_w3[l, 0], ffn_w2[l, 0]),
                       ln_g[l, 0], ln_b[l, 0])
        (a_q, a_kc, a_vc, a_ks, a_vs, a_kw, a_vw, a_gate,
         b_q, b_f, b_i, b_g, c_q, c_kv, c_kr, merge) = jnp.split(x @ w_in[l], split_points, axis=-1)
        kvs = lambda z: z.reshape(B, S, NSA_KV, NSA_DH)
        y_a = nsa_mixer(a_q.reshape(B, S, NSA_HEADS, NSA_DH), kvs(a_kc), kvs(a_vc), kvs(a_ks),
                        kvs(a_vs), kvs(a_kw), kvs(a_vw), a_gate, positions,
                        nsa_cmp_pos[l], nsa_cmp_w1[l], nsa_cmp_w2[l])
        y_b = hgrn2_mixer(b_q, b_f, b_i, b_g, lower_bounds[l], hgrn_norm_g[l])
        y_c = mla_mixer(c_q, c_kv, c_kr, positions, mla_q_norm_g[l], mla_w_uq[l],
                        mla_kv_norm_g[l], mla_w_ukv[l])
        gates = jax.nn.sigmoid(merge.astype(F32).reshape(B, S, 3, D_MODEL)).astype(x.dtype)
        mixed = (gates[:, :, 0] * (y_a @ w_branch[l, 0]) + gates[:, :, 1] * (y_b @ w_branch[l, 1])
                 + gates[:, :, 2] * (y_c @ w_branch[l, 2]))
        x = layer_norm(alpha * x + mixed @ w_out[l], ln_g[l, 1], ln_b[l, 1])
        x = layer_norm(alpha * x + memory_cross_attention(x, mem, xa_wq[l], xa_wk[l], xa_wv[l], xa_wo[l]),
                       ln_g[l, 2], ln_b[l, 2])
        x = layer_norm(alpha * x + 0.5 * swiglu(x, ffn_w1[l, 1], ffn_w3[l, 1], ffn_w2[l, 1]),
                       ln_g[l, 3], ln_b[l, 3])
    return x
```

```python
import contextlib
import os
import numpy as np
import concourse.bass as bass
import concourse.mybir as mybir
from concourse.bass_utils import run_bass_kernel_spmd

F32 = mybir.dt.float32
BF16 = mybir.dt.bfloat16
I32 = mybir.dt.int32
AF = mybir.ActivationFunctionType
ALU = mybir.AluOpType

ENGS = ("pe", "act", "dve", "pool", "sp")
EPOCH = 60000


class V:
    __slots__ = ("buf", "ap")

    def __init__(self, buf, ap):
        self.buf = buf
        self.ap = ap

    def __getitem__(self, k):
        return V(self.buf, self.ap[k])

    def rearrange(self, s, **kw):
        return V(self.buf, self.ap.rearrange(s, **kw))

    def bc(self, shape):
        return V(self.buf, self.ap.to_broadcast(shape))


class Buf:
    def __init__(self, name, ap):
        self.name = name
        self.ap = ap
        self.w = []
        self.r = {}
        self.wpe = False

    def __getitem__(self, k):
        return V(self, self.ap[k])

    @property
    def v(self):
        return V(self, self.ap)


class MK:
    def __init__(self, nc, n_dma_sems=8):
        self.nc = nc
        self.es = contextlib.ExitStack()
        self.q = {e: [] for e in ENGS}
        self.cnt = {e: 0 for e in ENGS}
        self.cursem = {}
        self.waited = {e: {} for e in ENGS}
        self.sems = {}
        self.nsem = 0
        self.pesems = set()
        for e in ("pe", "act", "dve", "pool"):
            self.cursem[e] = self._newsem(e)
        self.pesems.add(self.cursem["pe"])
        self.dsem = {}
        self.dcnt = {}
        for qn in ("sp", "pool"):
            self.dsem[qn] = [self._newsem("d" + qn) for _ in range(n_dma_sems)]
            self.dcnt[qn] = 0
        self.nbuf = 0

    def _newsem(self, tag):
        sid = self.nsem
        self.nsem += 1
        h = self.es.enter_context(self.nc.semaphore(f"s{sid}_{tag}"))
        self.sems[sid] = h
        return sid

    def sb(self, name, shape, dtype=F32):
        self.nbuf += 1
        t = self.nc.alloc_sbuf_tensor(f"{name}_{self.nbuf}", list(shape), dtype)
        return Buf(name, t.ap())

    def ps(self, name, shape, dtype=F32):
        self.nbuf += 1
        t = self.nc.alloc_psum_tensor(f"{name}_{self.nbuf}", list(shape), dtype)
        return Buf(name, t.ap())

    def dram(self, name, shape, dtype=F32, kind="Internal"):
        t = self.nc.dram_tensor(name, list(shape), dtype, kind=kind)
        return Buf(name, t.ap())

    def sub(self, buf, key, name=None):
        return Buf(name or buf.name, buf.ap[key])

    def alias(self, name, ap, olds):
        b = Buf(name, ap)
        for o in olds:
            for ev in o.w:
                b.r[ev[0]] = max(b.r.get(ev[0], 0), ev[1])
            for s, v in o.r.items():
                b.r[s] = max(b.r.get(s, 0), v)
        return b

    def _collect(self, eng, reads, writes, pe_acc=False):
        need = {}

        def add(s, v):
            if need.get(s, 0) < v:
                need[s] = v
        for b in reads:
            for ev in b.w:
                add(*ev)
        for b in writes:
            if not (pe_acc and b.wpe and eng == "pe"):
                for ev in b.w:
                    add(*ev)
            for s, v in b.r.items():
                add(s, v)
        wl = []
        wd = self.waited[eng]
        for s, v in need.items():
            if eng == "pe" and s in self.pesems:
                continue
            if wd.get(s, 0) >= v:
                continue
            wd[s] = v
            wl.append((s, v))
        return wl

    def _record(self, ev, reads, writes, eng):
        for b in writes:
            b.w = [ev]
            b.r = {}
            b.wpe = (eng == "pe")
        for b in reads:
            if b in writes:
                continue
            if b.r.get(ev[0], 0) < ev[1]:
                b.r[ev[0]] = ev[1]

    def op(self, eng, fn, reads, writes, pe_acc=False):
        reads = [x.buf if isinstance(x, V) else x for x in reads]
        writes = [x.buf if isinstance(x, V) else x for x in writes]
        wl = self._collect(eng, reads, writes, pe_acc)
        if self.cnt[eng] >= EPOCH:
            self.cursem[eng] = self._newsem(eng)
            if eng == "pe":
                self.pesems.add(self.cursem[eng])
            self.cnt[eng] = 0
        self.cnt[eng] += 1
        ev = (self.cursem[eng], self.cnt[eng])
        self.q[eng].append((wl, fn, ev[0], 1))
        self._record(ev, reads, writes, eng)
        return ev

    def dma(self, out, in_, queue="sp", **kw):
        eng = queue
        reads = [in_.buf]
        writes = [out.buf]
        i = self.dcnt[queue]
        self.dcnt[queue] += 1
        sems = self.dsem[queue]
        sid = sems[i % len(sems)]
        val = 16 * (i // len(sems) + 1)
        wl = self._collect(eng, reads, writes)
        if val > 16:
            wd = self.waited[eng]
            if wd.get(sid, 0) < val - 16:
                wd[sid] = val - 16
                wl.append((sid, val - 16))
        oap, iap = out.ap, in_.ap
        self.q[eng].append((wl, lambda e: e.dma_start(out=oap, in_=iap, **kw), sid, 16))
        ev = (sid, val)
        self._record(ev, reads, writes, eng)
        return ev

    @staticmethod
    def _a(x):
        return x.ap if isinstance(x, V) else x

    def mm(self, out, lhsT, rhs, start=True, stop=True):
        o, l, r = out.ap, lhsT.ap, rhs.ap
        return self.op("pe", lambda e: e.matmul(o, l, r, start=start, stop=stop),
                       [lhsT, rhs], [out], pe_acc=not start)

    def transpose(self, out, in_, ident):
        o, i, d = out.ap, in_.ap, ident.ap
        return self.op("pe", lambda e: e.transpose(o, i, d), [in_, ident], [out])

    def act(self, out, in_, func, bias=None, scale=None, accum_out=None):
        kw = {}
        reads = [in_]
        writes = [out]
        if bias is not None:
            kw["bias"] = self._a(bias)
            if isinstance(bias, V):
                reads.append(bias)
        if scale is not None:
            kw["scale"] = self._a(scale)
            if isinstance(scale, V):
                reads.append(scale)
        if accum_out is not None:
            kw["accum_out"] = accum_out.ap
            writes.append(accum_out)
        o, i = out.ap, in_.ap
        return self.op("act", lambda e: e.activation(o, i, func, **kw), reads, writes)

    def tt(self, out, in0, in1, op, eng="dve", deps=()):
        o, a, b = out.ap, in0.ap, in1.ap
        return self.op(eng, lambda e: e.tensor_tensor(o, a, b, op), [in0, in1] + list(deps), [out])

    def ts(self, out, in0, s1, op0, s2=None, op1=None, eng="dve", deps=()):
        reads = [in0] + [s for s in (s1, s2) if isinstance(s, V)] + list(deps)
        o, a = out.ap, in0.ap
        a1, a2 = self._a(s1), self._a(s2)
        kw = {}
        if op1 is not None:
            kw["op1"] = op1
        return self.op(eng, lambda e: e.tensor_scalar(o, a, a1, a2, op0, **kw), reads, [out])

    def stt(self, out, in0, scalar, in1, op0, op1, eng="dve"):
        reads = [in0, in1] + ([scalar] if isinstance(scalar, V) else [])
        o, a, b, s = out.ap, in0.ap, in1.ap, self._a(scalar)
        return self.op(eng, lambda e: e.scalar_tensor_tensor(o, a, s, b, op0, op1), reads, [out])

    def copy(self, out, in_, eng="dve"):
        o, i = out.ap, in_.ap
        if eng == "act":
            return self.op(eng, lambda e: e.copy(o, i), [in_], [out])
        return self.op(eng, lambda e: e.tensor_copy(o, i), [in_], [out])

    def memset(self, out, val, eng="dve"):
        o = out.ap
        return self.op(eng, lambda e: e.memset(o, val), [], [out])

    def recip(self, out, in_):
        o, i = out.ap, in_.ap
        return self.op("dve", lambda e: e.reciprocal(o, i), [in_], [out])

    def max8(self, out, in_):
        o, i = out.ap, in_.ap
        return self.op("dve", lambda e: e.max(o, i), [in_], [out])

    def emit(self, final_events):
        nc = self.nc
        sems = self.sems
        q = self.q

        def run(e, items, extra=None):
            for wl, fn, sid, inc in items:
                for s, v in wl:
                    e.wait_ge(sems[s], v)
                ins = fn(e)
                if inc:
                    ins.then_inc(sems[sid], inc)
            if extra:
                for s, v in extra:
                    e.wait_ge(sems[s], v)

        with nc.Block() as block:
            @block.tensor
            def _(e):
                run(e, q["pe"])

            @block.vector
            def _(e):
                run(e, q["dve"])

            @block.scalar
            def _(e):
                run(e, q["act"])

            @block.gpsimd
            def _(e):
                run(e, q["pool"])

            @block.sync
            def _(e):
                run(e, q["sp"], list(final_events))


D = 1024
S = 2048
DEPTH = 4
NCH = 8
NG = 4
GW = 512
DFF = 2816
NJ = 22
ALPHA = (2.0 * DEPTH) ** 0.25
LN_EPS = 1e-5
RMS_EPS = 1e-6
N_IN = 7096
O_Q, O_KC, O_VC, O_KS, O_VS, O_KW, O_VW, O_GATE = 0, 512, 640, 768, 896, 1024, 1152, 1280
O_BQ, O_BF, O_BI, O_BG = 1304, 1816, 2328, 2840
O_CQ, O_CKV, O_KR, O_MERGE = 3352, 3736, 3992, 4024
NEGM = -30000.0
NSA_ENABLED = True


def _fm_tile(W, cols=None):
    K = W.shape[0]
    if cols is None:
        cols = np.arange(W.shape[1])
    cols = np.asarray(cols, dtype=np.int64)
    t = np.zeros((1024, 128), np.float32)
    ok = cols >= 0
    idxs = np.nonzero(ok)[0]
    t[:K, idxs] = W[:, cols[ok]]
    return t.reshape(8, 128, 128).transpose(1, 0, 2).reshape(128, 1024)


def _tm_tile(W, cols):
    K = W.shape[0]
    cols = np.asarray(cols, dtype=np.int64)
    t = np.zeros((1024, 512), np.float32)
    t[:K, :len(cols)] = W[:, cols]
    return t.reshape(8, 128, 512).transpose(1, 0, 2).reshape(128, 4096)


def _rows64_tile(W, c):
    t = np.zeros((128, 8, 128), np.float32)
    t[:64] = W[:, c * 128:(c + 1) * 128].reshape(8, 64, 128).transpose(1, 0, 2)
    return t.reshape(128, 1024)


def layer_tiles(inp, l):
    r = np.arange
    fm, tm = [], []
    win = inp["w_in"][l]
    for i in range(2):
        for j in range(NJ):
            fm.append((f"w1_{i}_{j}", _fm_tile(inp["ffn_w1"][l, i], r(128 * j, 128 * j + 128))))
            fm.append((f"w3_{i}_{j}", _fm_tile(inp["ffn_w3"][l, i], r(128 * j, 128 * j + 128))))
    for b in range(3):
        for c in range(NCH):
            fm.append((f"merge_{b}_{c}", _fm_tile(win, O_MERGE + 1024 * b + 128 * c + r(128))))
    for c in range(NCH):
        fm.append((f"wbr_0_{c}", _rows64_tile(inp["w_branch"][l, 0], c)))
        fm.append((f"wbr_1_{c}", _fm_tile(inp["w_branch"][l, 1], 128 * c + r(128))))
        fm.append((f"wbr_2_{c}", _rows64_tile(inp["w_branch"][l, 2], c)))
        fm.append((f"wout_{c}", _fm_tile(inp["w_out"][l], 128 * c + r(128))))
        fm.append((f"xq_{c}", _fm_tile(inp["xa_wq"][l], 128 * c + r(128))))
        fm.append((f"xk_{c}", _fm_tile(inp["xa_wk"][l], 128 * c + r(128))))
        fm.append((f"xo_{c}", _fm_tile(inp["xa_wo"][l], 128 * c + r(128))))
    for hv in range(2):
        tm.append((f"xv_{hv}", _tm_tile(inp["xa_wv"][l], 512 * hv + r(512))))
    EXTRA_TILES(inp, l, fm, tm)
    return fm, tm


def EXTRA_TILES(inp, l, fm, tm):
    r = np.arange
    win = inp["w_in"][l]
    for k in range(3):
        fm.append((f"cq_{k}", _fm_tile(win, O_CQ + 128 * k + r(128))))
    for k in range(2):
        fm.append((f"ckv_{k}", _fm_tile(win, O_CKV + 128 * k + r(128))))
    pad64 = -np.ones(64, np.int64)
    fm.append(("kr96", _fm_tile(win, np.concatenate([pad64, O_KR + r(32)]))))
    fm.append(("kr96p", _fm_tile(win, np.concatenate([pad64, O_KR + 16 + r(16), O_KR + r(16)]))))
    uq = inp["mla_w_uq"][l]
    ukv = inp["mla_w_ukv"][l]
    for h in range(8):
        fm.append((f"uq_{h}", _fm_tile(uq, 96 * h + r(96))))
        fm.append((f"uqp_{h}", _fm_tile(uq, np.concatenate([pad64, 96 * h + 80 + r(16), 96 * h + 64 + r(16)]))))
        fm.append((f"ukn_{h}", _fm_tile(ukv, 128 * h + r(64))))
    def ropeperm(base, n):
        cols = -np.ones(n, np.int64)
        for j in range(n):
            d = j % 64
            if d < 8:
                cols[j] = base + j + 8
            elif d < 16:
                cols[j] = base + j - 8
        return cols
    for c in range(4):
        fm.append((f"q_{c}", _fm_tile(win, O_Q + 128 * c + r(128))))
        fm.append((f"qp_{c}", _fm_tile(win, ropeperm(O_Q + 128 * c, 128))))
    for g in range(2):
        dup = np.concatenate([64 * g + r(64), 64 * g + r(64)])
        pd = np.concatenate([ropeperm(64 * g, 64), ropeperm(64 * g, 64)])
        fm.append((f"kcd_{g}", _fm_tile(win, O_KC + dup)))
        fm.append((f"vcd_{g}", _fm_tile(win, O_VC + dup)))
        fm.append((f"ksd_{g}", _fm_tile(win, O_KS + dup)))
        fm.append((f"ksdp_{g}", _fm_tile(win, np.where(pd >= 0, O_KS + pd, -1))))
        fm.append((f"kwd_{g}", _fm_tile(win, O_KW + dup)))
        fm.append((f"kwdp_{g}", _fm_tile(win, np.where(pd >= 0, O_KW + pd, -1))))
        tm.append((f"vsw_{g}", _tm_tile(win, np.concatenate([O_VS + 64 * g + r(64), O_VW + 64 * g + r(64)]))))
    fm.append(("gate", _fm_tile(win, O_GATE + r(24))))
    for kv in range(2):
        w1 = inp["nsa_cmp_w1"][l, kv]
        t = w1.reshape(16, 128, 256).transpose(1, 0, 2).reshape(128, 4096)
        tm.append((f"cw1_{kv}", np.ascontiguousarray(t)))
    w2 = inp["nsa_cmp_w2"][l]
    fm.append(("cw2k", _fm_tile(w2[0], np.concatenate([r(64), r(64)]))))
    fm.append(("cw2v", _fm_tile(w2[1], r(64))))
    for h in range(4):
        fm.append((f"bq_{h}", _fm_tile(win, O_BQ + 128 * h + r(128))))
        fm.append((f"bf_{h}", _fm_tile(win, O_BF + 128 * h + r(128))))
        fm.append((f"bg_{h}", _fm_tile(win, O_BG + 128 * h + r(128))))
    tm.append(("bf_tm", _tm_tile(win, O_BF + r(512))))
    tm.append(("bi_tm", _tm_tile(win, O_BI + r(512))))
    tm.append(("ukv_v", _tm_tile(ukv, np.concatenate([128 * h + 64 + r(64) for h in range(8)]))))


def prep_inputs(inputs):
    f = lambda k: np.asarray(inputs[k], dtype=np.float32)
    inp = {k: f(k) for k in inputs if k != "positions"}
    shared = {}
    fms, tms = [], []
    for l in range(DEPTH):
        fm, tm = layer_tiles(inp, l)
        if l == 0:
            idx = {n: i for i, (n, _) in enumerate(fm)}
            idx.update({n: i for i, (n, _) in enumerate(tm)})
            nt = (len(fm), len(tm))
        fms.extend(a for _, a in fm)
        tms.extend(a for _, a in tm)
    shared["wfm"] = np.stack(fms)
    shared["wtm"] = np.stack(tms)
    w2 = inp["ffn_w2"].reshape(DEPTH, 2, NJ, 128, NCH, 128).transpose(0, 1, 4, 3, 2, 5)
    shared["w2"] = np.ascontiguousarray(w2).reshape(DEPTH * 2 * NCH, 128, NJ * 128)
    shared["lng"] = np.ascontiguousarray(inp["ln_g"].reshape(DEPTH, 4, NCH, 128).transpose(3, 0, 1, 2)).reshape(128, DEPTH * 4 * NCH)
    shared["lnb"] = np.ascontiguousarray(inp["ln_b"].reshape(DEPTH, 4, NCH, 128).transpose(3, 0, 1, 2)).reshape(128, DEPTH * 4 * NCH)
    shared["ones32"] = np.ones((128, 128), np.float32)
    EXTRA_SHARED(inp, shared)
    x = inp["x"]
    mem = inp["mem"]
    per_core = []
    for b in range(x.shape[0]):
        per_core.append({"xT": np.ascontiguousarray(x[b].T),
                         "posr": np.ascontiguousarray(np.broadcast_to(np.asarray(inputs["positions"])[b].astype(np.int32)[None, :], (128, S))),
                         "memT": np.ascontiguousarray(mem[b].T.reshape(8, 128, 256).transpose(1, 0, 2)).reshape(128, 2048)})
    return shared, per_core, idx, nt


def EXTRA_SHARED(inp, shared):
    shared["ident"] = np.eye(128, dtype=np.float32)
    k = np.arange(128)[:, None]
    q = np.arange(512)[None, :]
    shared["causneg"] = np.concatenate([np.where(128 * rel + k <= q, 0.0, NEGM) for rel in range(4)], axis=1).astype(np.float32)
    inv96 = np.zeros((128, 1), np.float32)
    sgn96 = np.zeros((128, 1), np.float32)
    fr = (500000.0 ** (-np.arange(16, dtype=np.float32) / 16)).astype(np.float32)
    inv96[64:80, 0] = fr
    inv96[80:96, 0] = fr
    sgn96[64:80, 0] = -1.0
    sgn96[80:96, 0] = 1.0
    shared["ropec"] = np.concatenate([inv96, sgn96], axis=1)
    n = np.arange(128)[:, None]
    t = np.arange(S)[None, :]
    shared["cmpneg"] = np.where((16 * n + 31 <= t) & (n < 127), 0.0, NEGM).astype(np.float32)
    kk_ = np.arange(128)[:, None]
    qq_ = np.arange(512)[None, :]
    wn = []
    for rel in range(-2, 4):
        kp = 128 * rel + kk_
        wn.append(np.where((kp <= qq_) & (kp > qq_ - 256), 0.0, NEGM))
    shared["winneg"] = np.concatenate(wn, axis=1).astype(np.float32)
    es = np.zeros((128, 16 * 128), np.float32)
    for kt in range(16):
        for k_ in range(128):
            es[2 * kt + k_ // 64, kt * 128 + k_] = 1.0
    shared["esel"] = es
    eg = np.zeros((128, 24 * 64), np.float32)
    for j in range(24):
        eg[j, j * 64:(j + 1) * 64] = 1.0
    shared["egate"] = eg
    start = np.arange(127) * 16
    jn = np.arange(32)
    ovl = np.clip(np.minimum(start[:, None] + 32, (jn[None, :] + 1) * 64) - np.maximum(start[:, None], jn[None, :] * 64), 0, None) / 32.0
    o128 = np.zeros((128, 32), np.float32)
    o128[:127] = ovl
    shared["ovl"] = o128
    tt_ = np.arange(S)[:, None]
    valid = jn[None, :] * 64 <= tt_
    cur = tt_ // 64
    forced = (jn[None, :] == 0) | (jn[None, :] == cur) | (jn[None, :] == cur - 1)
    vnf = (valid & ~forced).astype(np.float32)
    addm = np.where(valid & forced, 1e9, np.where(valid, 0.0, -1e9)).astype(np.float32)
    shared["vnf"] = np.ascontiguousarray(vnf.reshape(16, 128, 32).transpose(1, 0, 2)).reshape(128, 512)
    shared["addm"] = np.ascontiguousarray(addm.reshape(16, 128, 32).transpose(1, 0, 2)).reshape(128, 512)
    invn = np.zeros((128, 2), np.float32)
    frn = (500000.0 ** (-np.arange(8, dtype=np.float32) / 8)).astype(np.float32)
    for base in (0, 64):
        invn[base:base + 8, 0] = frn
        invn[base + 8:base + 16, 0] = frn
        invn[base:base + 8, 1] = -1.0
        invn[base + 8:base + 16, 1] = 1.0
    shared["ropen"] = invn
    pe = inp["nsa_cmp_pos"]
    shared["cpe"] = np.ascontiguousarray(pe.reshape(DEPTH, 2, 16, 128).transpose(3, 0, 1, 2)).reshape(128, DEPTH * 2 * 16)
    lbl = inp["hgrn_lb_logits"]
    shared["lblfm"] = np.ascontiguousarray(lbl.reshape(DEPTH, 4, 128).transpose(2, 0, 1)).reshape(128, 16)
    shared["lblrep"] = np.ascontiguousarray(np.broadcast_to(lbl.reshape(1, DEPTH * 512), (128, DEPTH * 512)))
    shared["hng"] = np.ascontiguousarray(inp["hgrn_norm_g"].T)
    ii = np.arange(128)
    shared["ublk"] = ((ii[:, None] <= ii[None, :]) & (ii[:, None] // 64 == ii[None, :] // 64)).astype(np.float32)
    shared["qng"] = np.ascontiguousarray(inp["mla_q_norm_g"].reshape(DEPTH, 3, 128).transpose(2, 0, 1)).reshape(128, DEPTH * 3)
    shared["kvng"] = np.ascontiguousarray(inp["mla_kv_norm_g"].reshape(DEPTH, 2, 128).transpose(2, 0, 1)).reshape(128, DEPTH * 2)


class Prog:
    AW = 23808

    def __init__(self, shapes, idx, nt, n_layers=DEPTH, stop=None, ext_y=False):
        self.idx = idx
        self.nt = nt
        self.n_layers = n_layers
        self.stop = stop
        self.ext_y = ext_y
        nc = bass.Bass("TRN2", target_bir_lowering=False)
        self.nc = nc
        m = MK(nc)
        self.m = m
        self.din = {k: m.dram(k, list(s), I32 if k == "posr" else F32, kind="ExternalInput") for k, s in shapes.items()}
        self.out = m.dram("out", [D, S], F32, kind="ExternalOutput")
        self.xres = m.dram("xres", [D, S], F32)
        yk = lambda nm: "ExternalInput" if (ext_y is True or (ext_y and ext_y != nm)) else ("ExternalOutput" if ext_y == nm else "Internal")
        self.ya_d = m.dram("ya_d", [8, 64, S], BF16, kind=yk("nsa"))
        self.yb_d = m.dram("yb_d", [4, 128, S], BF16, kind=yk("hgrn"))
        self.yc_d = m.dram("yc_d", [8, 64, S], BF16, kind=yk("mla"))
        self.xb_t = m.sb("xb", [128, NCH, S], BF16)
        self.xb = [[m.sub(self.xb_t, (slice(None), c, slice(g * GW, (g + 1) * GW)), f"xb{c}_{g}") for g in range(NG)] for c in range(NCH)]
        self.ones32 = m.sb("ones32", [128, 128])
        self.onesb = m.sb("onesb", [128, 128], BF16)
        self.lng = m.sb("lng", [128, DEPTH * 4 * NCH])
        self.lnb = m.sb("lnb", [128, DEPTH * 4 * NCH])
        m.dma(self.ones32.v, self.din["ones32"].v)
        m.dma(self.lng.v, self.din["lng"].v)
        m.dma(self.lnb.v, self.din["lnb"].v)
        m.dma(self.onesb.v, self.din["ones32"].v, "pool")
        self.pb = [m.ps(f"pb{i}", [128, GW]) for i in range(8)]
        self.nslot = 4
        self.wslot = [m.sb(f"wslot{i}", [128, 8, 128], BF16) for i in range(self.nslot)]
        self.wslot_i = 0
        self.tslot = [m.sb(f"tslot{i}", [128, 8, 512], BF16) for i in range(2)]
        self.tslot_i = 0
        self.xstage = [m.sb(f"xstage{i}", [128, GW]) for i in range(2)]
        self.ybuf = [m.sb(f"y{c}", [128, GW]) for c in range(NCH)]
        self.sq = [m.sb("sq0", [128, GW])] * 2
        self.mean = m.sb("mean", [128, GW])
        self.rstd = m.sb("rstd", [128, GW])
        self.tmpa = [m.sb(f"tmpa{i}", [128, GW]) for i in range(2)]
        self.xn = [m.sb(f"xn{i}", [128, GW]) for i in range(2)]
        self.arena = m.sb("arena", [128, self.AW])
        self.aoff = 0
        self.barrier = {}
        self.cnt_stage = 0
        self.final_events = []
        self.setup_consts()

    def ld_const(self, name, shape, dtype=F32, key=None):
        m = self.m
        b = m.sb(name, shape, dtype)
        m.dma(b.v, self.din[key or name].v, "pool" if dtype != F32 else "sp")
        return b

    def rope_tables(self, inv_sgn, rows, name):
        m = self.m
        C = m.sb(name + "C", [128, S], BF16)
        Sn = m.sb(name + "S", [128, S], BF16)
        TWO_PI = 2.0 * np.pi
        for r0, r1 in rows:
            for g in range(NG):
                gs = slice(g * GW, (g + 1) * GW)
                pi_ = self.tmpa[0]
                pf = self.tmpa[1]
                m.dma(V(pi_, pi_.ap.bitcast(I32))[r0:r1, :], self.din["posr"][r0:r1, gs])
                for which, dst in ((0, Sn), (1, C)):
                    ang = self.xn[0]
                    kk = self.xn[1]
                    m.copy(pf[r0:r1, :], V(pi_, pi_.ap.bitcast(I32))[r0:r1, :])
                    m.ts(ang[r0:r1, :], pf[r0:r1, :], inv_sgn[r0:r1, 0:1], ALU.mult, (np.pi / 2 if which else 0.0), ALU.add)
                    m.ts(kk[r0:r1, :], ang[r0:r1, :], 1.0 / TWO_PI, ALU.mult)
                    ki = self.sq[0]
                    m.copy(V(ki, ki.ap.bitcast(I32))[r0:r1, :], kk[r0:r1, :])
                    m.copy(kk[r0:r1, :], V(ki, ki.ap.bitcast(I32))[r0:r1, :])
                    m.stt(ang[r0:r1, :], kk[r0:r1, :], -TWO_PI, ang[r0:r1, :], ALU.mult, ALU.add)
                    m.ts(kk[r0:r1, :], ang[r0:r1, :], np.pi, ALU.is_gt)
                    m.stt(ang[r0:r1, :], kk[r0:r1, :], -TWO_PI, ang[r0:r1, :], ALU.mult, ALU.add)
                    m.ts(kk[r0:r1, :], ang[r0:r1, :], -np.pi, ALU.is_lt)
                    m.stt(ang[r0:r1, :], kk[r0:r1, :], TWO_PI, ang[r0:r1, :], ALU.mult, ALU.add)
                    m.act(kk[r0:r1, :], ang[r0:r1, :], AF.Sin)
                    if which == 0:
                        m.ts(dst[r0:r1, gs], kk[r0:r1, :], inv_sgn[r0:r1, 1:2], ALU.mult)
                    else:
                        m.copy(dst[r0:r1, gs], kk[r0:r1, :])
        return C, Sn

    def setup_consts(self):
        m = self.m
        self.identb = self.ld_const("identb", [128, 128], BF16, "ident")
        self.causneg = self.ld_const("causneg", [128, 4 * GW], BF16)
        self.ropec96 = self.ld_const("ropec", [128, 2])
        self.qng = self.ld_const("qng", [128, DEPTH * 3])
        self.kvng = self.ld_const("kvng", [128, DEPTH * 2])
        if not self.ext_y or self.ext_y == "mla":
            self.C96, self.S96 = self.rope_tables(self.ropec96, [(64, 96)], "r96")
        if not self.ext_y or self.ext_y == "hgrn":
            self.hgrn_consts()
        if self.ext_y == "nsa" or (not self.ext_y and NSA_ENABLED):
            self.ropen = self.ld_const("ropen", [128, 2])
            self.Cn, self.Sn = self.rope_tables(self.ropen, [(0, 16), (64, 80)], "rn")

    def nsa(self, l):
        m = self.m
        self.phase()
        cv = self.carve
        scale = 64.0 ** -0.5

        def ldc(name, shape, dtype=F32):
            b = cv(name, shape, dtype)
            m.dma(b.v, self.din[name].v, "pool" if dtype != F32 else "sp")
            return b
        cmpneg = ldc("cmpneg", [128, S], BF16)
        winneg = ldc("winneg", [128, 6 * GW], BF16)
        esel = ldc("esel", [128, 16 * 128], BF16)
        egate = ldc("egate", [128, 24 * 64], BF16)
        ovl = ldc("ovl", [128, 32])
        vnf = ldc("vnf", [128, 512])
        addm = ldc("addm", [128, 512])
        cpe = cv("cpe", [128, 32], BF16)
        m.dma(cpe.v, self.din["cpe"][:, l * 32:(l + 1) * 32], "pool")
        sg = cv("sg", [128, S], BF16)
        cbias = cv("cbias", [128, 2])
        slb = cv("nslb", [128, 32], BF16)
        PT = [cv(f"nPT{k}", [128, GW], BF16) for k in range(2)]
        P32 = cv("nP32", [128, GW])
        pn = cv("npn", [128, GW])
        pns = cv("npns", [128, GW])
        rr = cv("nrr", [128, GW])
        t1 = cv("nt1", [128, GW])
        t2 = cv("nt2", [128, GW])
        sc = cv("nsc", [128, 32])
        sl = cv("nsl", [128, 32])
        m8 = cv("nm8", [128, 8])
        hT = [cv(f"nhT{k}", [128, 128], BF16) for k in range(2)]
        kcmp = cv("nkcmp", [128, 128], BF16)
        vcmp = cv("nvcmp", [128, 64])
        yo = [cv(f"nyo{k}", [64, GW], BF16) for k in range(2)]
        yc_ = [cv(f"nyc{k}", [64, GW], BF16) for k in range(2)]
        qT = [cv(f"nq{k}", [128, S], BF16) for k in range(2)]
        qr = [cv(f"nqr{k}", [128, S], BF16) for k in range(2)]
        ksd = cv("nksd", [128, S], BF16)
        kwd = cv("nkwd", [128, S], BF16)
        zk = cv("nzk", [128, S + 16], BF16)
        zv = cv("nzv", [128, S + 16], BF16)
        vsw = [cv(f"nvsw{t}", [128, 128], BF16) for t in range(16)]
        selT = cv("nselT", [128, S], BF16)
        Cn, Sn = self.Cn, self.Sn
        m.memset(zk.v, 0.0)
        m.memset(zv.v, 0.0)
        m.memset(hT[0].v, 0.0)
        m.memset(selT.v, 0.0)
        m.memset(hT[1].v, 0.0)
        ROPE_ROWS = (slice(0, 16), slice(64, 80))
        u = 0
        w = self.load_fm(l, "gate")
        for g4 in range(NG):
            gs = slice(g4 * GW, (g4 + 1) * GW)
            ps = self.pb[g4 % 2]
            for kc in range(NCH):
                m.mm(ps.v, w[:, kc, :], self.xb[kc][g4].v, start=(kc == 0), stop=(kc == NCH - 1))
            m.act(sg[0:32, gs], ps[0:32, :], AF.Sigmoid)
        if os.environ.get("NSA_STOP") == "1":
            return

        def proj(name, g4):
            w_ = self.load_fm(l, name)
            nonlocal u
            ps_ = self.pb[u % 2]
            u += 1
            for kc in range(NCH):
                m.mm(ps_.v, w_[:, kc, :], self.xb[kc][g4].v, start=(kc == 0), stop=(kc == NCH - 1))
            return ps_

        def roped(dst, name, pname, g4):
            gs = slice(g4 * GW, (g4 + 1) * GW)
            ps_ = proj(name, g4)
            if os.environ.get("NSA_STOP") == "2a":
                raise StopIteration
            pp_ = proj(pname, g4)
            if os.environ.get("NSA_STOP") == "2b":
                raise StopIteration
            m.copy(dst[:, gs], ps_.v, "act")
            if os.environ.get("NSA_STOP") == "2c":
                raise StopIteration
            for rws in (ROPE_ROWS if os.environ.get("NSA_ROPE", "1") == "1" else ()):
                m.tt(t1[rws, :], ps_[rws, :], Cn[rws, gs], ALU.mult, deps=[dst])
                m.tt(t2[rws, :], pp_[rws, :], Sn[rws, gs], ALU.mult)
                m.tt(dst[rws, gs], t1[rws, :], t2[rws, :], ALU.add)
            return ps_

        def gate_mul(h, br, R_, O_, rows, qs, dst, add_to=None):
            m.ts(rr[rows, :], R_[rows, :], 1e-30, ALU.max)
            m.recip(rr[rows, :], rr[rows, :])
            pg = self.pb[7]
            j = h * 3 + br
            m.mm(pg[0:64, :], egate[0:32, j * 64:(j + 1) * 64], sg[0:32, qs])
            m.tt(t1[0:64, :], pg[0:64, :], rr[0:64, :], ALU.mult)
            if add_to is None:
                m.tt(dst, O_[0:64, :], t1[0:64, :], ALU.mult)
            else:
                m.tt(t2[0:64, :], O_[0:64, :], t1[0:64, :], ALU.mult)
                m.tt(dst, t2[0:64, :], add_to, ALU.add)

        for g in range(2):
            for g4 in range(NG):
                gs = slice(g4 * GW, (g4 + 1) * GW)
                for k in range(2):
                    c = 2 * g + k
                    ps_ = roped(qr[k], f"q_{c}", f"qp_{c}", g4)
                    m.ts(qT[k][:, gs], ps_.v, 1.0, ALU.mult, deps=[qr[k]])
                    if os.environ.get("NSA_STOP") == "2d":
                        raise StopIteration
                if os.environ.get("NSA_STOP") == "2e":
                    raise StopIteration
                roped(ksd, f"ksd_{g}", f"ksdp_{g}", g4)
                if os.environ.get("NSA_STOP") == "2f":
                    raise StopIteration
                roped(kwd, f"kwd_{g}", f"kwdp_{g}", g4)
                if os.environ.get("NSA_STOP") == "2g":
                    raise StopIteration
                for nm, z in ((f"kcd_{g}", zk), (f"vcd_{g}", zv)):
                    ps_ = proj(nm, g4)
                    lo = g4 * GW
                    m.copy(z[0:64, lo:lo + GW], ps_[0:64, :], "act")
                    m.copy(z[64:128, lo:lo + GW - 1], ps_[64:128, 1:GW], "act")
                    if g4 > 0:
                        m.copy(z[64:128, lo - 1:lo], ps_[64:128, 0:1], "act")
            if os.environ.get("NSA_STOP") == "2":
                return
            wv = self.load_tm(l, f"vsw_{g}")
            for t in range(16):
                ps_ = self.pb[2 + t % 2]
                for kc in range(NCH):
                    m.mm(ps_[:, 0:128], self.xb[kc][t // 4][:, (t % 4) * 128:(t % 4 + 1) * 128], wv[:, kc, 0:128], start=(kc == 0), stop=(kc == NCH - 1))
                m.copy(vsw[t].v, ps_[:, 0:128], "act")
                if os.environ.get("NSA_STOP") == "3":
                    return
            if os.environ.get("NSA_STOP") == "4a":
                raise StopIteration
            for kv, z in ((0, zk), (1, zv)):
                w1 = self.load_tm(l, f"cw1_{kv}")
                w1v = w1.v.rearrange("p k n -> p (k n)").rearrange("p (a c) -> p a c", c=256)
                w2 = self.load_fm(l, "cw2k" if kv == 0 else "cw2v")
                for cc in range(2):
                    ps_ = self.pb[cc]
                    pbz = self.pb[2]
                    ccs = slice(cc * 128, (cc + 1) * 128)
                    for lp in range(16):
                        pc_ = kv * 16 + lp
                        m.mm(pbz[:, 0:1], w1v[:, lp, ccs], cpe[:, pc_:pc_ + 1], start=(lp == 0), stop=(lp == 15))
                    m.copy(cbias[:, cc:cc + 1], pbz[:, 0:1], "dve")
                    if os.environ.get("NSA_STOP") == "4b":
                        raise StopIteration
                    for lp in range(16):
                        m.mm(ps_[:, 0:128], w1v[:, lp, ccs], z[:, 2 * lp:2 * lp + 16 * 127 + 1:16], start=(lp == 0), stop=(lp == 15))
                    xg = t2
                    m.act(xg[:, 0:128], ps_[:, 0:128], AF.Identity, bias=cbias[:, cc:cc + 1])
                    m.act(t1[:, 0:128], xg[:, 0:128], AF.Square)
                    m.ts(t1[:, 0:128], t1[:, 0:128], 0.044715, ALU.mult, 1.0, ALU.add)
                    m.tt(t1[:, 0:128], t1[:, 0:128], xg[:, 0:128], ALU.mult)
                    m.act(t1[:, 0:128], t1[:, 0:128], AF.Sigmoid, scale=1.5957691216057308)
                    m.tt(hT[cc][:, 0:128], t1[:, 0:128], xg[:, 0:128], ALU.mult)
                    if os.environ.get("NSA_STOP") == "4c":
                        raise StopIteration
                ps_ = self.pb[2]
                if kv == 0:
                    for cc in range(2):
                        m.mm(ps_[:, 0:128], w2[:, cc, :], hT[cc][:, 0:128], start=(cc == 0), stop=(cc == 1))
                    m.copy(kcmp[:, 0:128], ps_[:, 0:128], "act")
                    if os.environ.get("NSA_STOP") == "4d":
                        raise StopIteration
                else:
                    for cc in range(2):
                        m.mm(ps_[0:128, 0:64], hT[cc][:, 0:128], w2[:, cc, 0:64], start=(cc == 0), stop=(cc == 1))
                    m.copy(vcmp[0:128, :], ps_[0:128, 0:64], "act")
            if os.environ.get("NSA_STOP") == "4":
                return
            for qg in range(NG):
                qs = slice(qg * GW, (qg + 1) * GW)
                for r_ in range(4):
                    h = 4 * g + r_
                    k = r_ // 2
                    P_ = slice(64 * (r_ % 2), 64 * (r_ % 2) + 64)
                    st = self.pb[2 + u % 2]
                    u += 1
                    m.mm(st[0:128, :], kcmp[P_, 0:128], qT[k][P_, qs], start=True, stop=False)
                    m.mm(st[0:128, :], self.identb[0:128, 0:128], cmpneg[0:128, qs], start=False, stop=True)
                    m.act(P32[0:128, :], st[0:128, :], AF.Exp, scale=scale)
                    O, R = self.pb[4], self.pb[5]
                    m.mm(O[0:64, :], vcmp[0:128, :], P32[0:128, :])
                    m.mm(R.v, self.ones32[0:128, :], P32[0:128, :])
                    y = yc_[(qg * 4 + r_) % 2]
                    gate_mul(h, 0, R, O, slice(0, 128), qs, y.v)
                    m.dma(self.ya_d[h][:, qs], y.v)
                    if r_ == 0:
                        m.tt(pns[0:128, :], P32[0:128, :], rr[0:128, :], ALU.mult)
                    else:
                        m.tt(pn[0:128, :], P32[0:128, :], rr[0:128, :], ALU.mult)
                        m.tt(pns[0:128, :], pns[0:128, :], pn[0:128, :], ALU.add, eng="pool")
                for t4 in range(4):
                    tt_ = qg * 4 + t4
                    pi_ = self.pb[6]
                    m.mm(pi_[:, 0:32], pns[0:128, t4 * 128:(t4 + 1) * 128], ovl[0:128, :])
                    m.tt(sc.v, pi_[:, 0:32], vnf[:, tt_ * 32:(tt_ + 1) * 32], ALU.mult)
                    m.tt(sc.v, sc.v, addm[:, tt_ * 32:(tt_ + 1) * 32], ALU.add)
                    m.max8(m8.v, sc.v)
                    m.ts(sl.v, sc.v, m8[:, 7:8], ALU.is_ge)
                    m.ts(sc.v, sc.v, -0.5e9, ALU.is_gt)
                    m.tt(sl.v, sl.v, sc.v, ALU.mult)
                    m.ts(slb.v, sl.v, -NEGM, ALU.mult, NEGM, ALU.add)
                    ptr = V(self.pb[7], self.pb[7].ap.bitcast(BF16))
                    m.transpose(ptr[0:32, 0:128], slb.v, self.identb.v)
                    m.copy(selT[0:32, tt_ * 128:(tt_ + 1) * 128], ptr[0:32, 0:128], "act")
            if os.environ.get("NSA_STOP") == "5":
                return
            for r_ in range(4):
                h = 4 * g + r_
                k = r_ // 2
                P_ = slice(64 * (r_ % 2), 64 * (r_ % 2) + 64)
                for qg in range(NG):
                    qs = slice(qg * GW, (qg + 1) * GW)
                    ycm = yc_[(qg + r_) % 2]
                    m.dma(ycm.v, self.ya_d[h][:, qs])
                    y = yo[(qg + r_) % 2]
                    for br in (1, 2):
                        O, R = self.pb[4], self.pb[5]
                        kts = list(range(0, 4 * qg + 4)) if br == 1 else list(range(max(0, 4 * qg - 2), 4 * qg + 4))
                        for ii, kt in enumerate(kts):
                            st = self.pb[2 + u % 2]
                            pt = PT[u % 2]
                            u += 1
                            ks_ = slice(kt * 128, (kt + 1) * 128)
                            rel = kt - 4 * qg
                            if br == 1:
                                m.mm(st.v, ksd[P_, ks_], qr[k][P_, qs], start=True, stop=False)
                                diag = rel >= 0
                                m.mm(st.v, esel[:, ks_], selT[:, qs], start=False, stop=not diag)
                                if diag:
                                    m.mm(st.v, self.identb.v, self.causneg[:, rel * GW:(rel + 1) * GW], start=False, stop=True)
                            else:
                                m.mm(st.v, kwd[P_, ks_], qr[k][P_, qs], start=True, stop=False)
                                m.mm(st.v, self.identb.v, winneg[:, (rel + 2) * GW:(rel + 3) * GW], start=False, stop=True)
                            m.act(pt.v, st.v, AF.Exp, scale=scale)
                            vcol = slice(0, 64) if br == 1 else slice(64, 128)
                            m.mm(O[0:64, :], vsw[kt][:, vcol], pt.v, start=(ii == 0), stop=(ii == len(kts) - 1))
                            m.mm(R[0:64, :], self.onesb[:, 0:64], pt.v, start=(ii == 0), stop=(ii == len(kts) - 1))
                        if br == 1:
                            gate_mul(h, 1, R, O, slice(0, 64), qs, pn[0:64, :], add_to=ycm.v)
                        else:
                            gate_mul(h, 2, R, O, slice(0, 64), qs, y.v, add_to=pn[0:64, :])
                    m.dma(self.ya_d[h][:, qs], y.v)

    def hgrn_consts(self):
        m = self.m
        self.hng = self.ld_const("hng", [128, DEPTH])
        self.ublk = self.ld_const("ublk", [128, 128])
        lf = self.ld_const("lblfm", [128, 16])
        e = m.sb("lbe", [128, 16])
        ssum = m.sb("lbs", [128, 4])
        self.lbfm = m.sb("lbfm", [128, 16])
        self.omlfm = m.sb("omlfm", [128, 16])
        m.act(e.v, lf.v, AF.Exp)
        m.tt(ssum.v, e[:, 0:4], e[:, 4:8], ALU.add)
        m.tt(ssum.v, ssum.v, e[:, 8:12], ALU.add)
        m.tt(ssum.v, ssum.v, e[:, 12:16], ALU.add)
        m.recip(ssum.v, ssum.v)
        m.memset(self.lbfm[:, 0:4], 0.0)
        for l in range(1, DEPTH):
            m.tt(e[:, 4 * l:4 * l + 4], e[:, 4 * l:4 * l + 4], ssum.v, ALU.mult)
            m.tt(self.lbfm[:, 4 * l:4 * l + 4], self.lbfm[:, 4 * l - 4:4 * l], e[:, 4 * l:4 * l + 4], ALU.add)
        m.ts(self.omlfm.v, self.lbfm.v, -1.0, ALU.mult, 1.0, ALU.add)

    def hgrn(self, l):
        m = self.m
        self.phase()
        cv = self.carve
        E = cv("lbE", [128, DEPTH * 512])
        lbr = cv("lbr", [128, 512])
        omlr = cv("omlr", [128, 512])
        tq = [cv(f"htq{k}", [128, 512]) for k in range(3)]
        vtm = [cv(f"hv{t}", [128, 512], BF16) for t in range(16)]
        bT = [cv(f"hb{h}", [128, S]) for h in range(4)]
        qi = cv("hqi", [128, S], BF16)
        qt = cv("hqt", [128, S], BF16)
        ks = cv("hks", [128, S], BF16)
        kstm = [cv(f"hkstm{t}", [128, 128], BF16) for t in range(16)]
        eb = cv("heb", [128, 32])
        S32 = cv("hS32", [128, 128])
        Sb = [cv(f"hSb{k}", [128, 128], BF16) for k in range(2)]
        ATm = [cv(f"hATm{k}", [128, 128], BF16) for k in range(2)]
        o32 = cv("ho32", [128, 512])
        yo = [cv(f"hyo{k}", [128, 512], BF16) for k in range(2)]
        m.dma(E.v, self.din["lblrep"].v)
        m.act(E.v, E.v, AF.Exp)
        m.tt(tq[0].v, E[:, 0:512], E[:, 512:1024], ALU.add)
        m.tt(tq[0].v, tq[0].v, E[:, 1024:1536], ALU.add)
        m.tt(tq[0].v, tq[0].v, E[:, 1536:2048], ALU.add)
        m.recip(tq[0].v, tq[0].v)
        m.memset(lbr.v, 0.0)
        for j in range(1, l + 1):
            m.tt(tq[1].v, E[:, 512 * j:512 * j + 512], tq[0].v, ALU.mult)
            m.tt(lbr.v, lbr.v, tq[1].v, ALU.add)
        m.ts(omlr.v, lbr.v, -1.0, ALU.mult, 1.0, ALU.add)
        wz = self.load_tm(l, "bf_tm")
        wi = self.load_tm(l, "bi_tm")
        for t in range(16):
            g, t4 = t // 4, t % 4
            tsl = slice(t4 * 128, (t4 + 1) * 128)
            pz, pi_ = self.pb[0], self.pb[1]
            for kc in range(NCH):
                m.mm(pz.v, self.xb[kc][g][:, tsl], wz[:, kc, :], start=(kc == 0), stop=(kc == NCH - 1))
            for kc in range(NCH):
                m.mm(pi_.v, self.xb[kc][g][:, tsl], wi[:, kc, :], start=(kc == 0), stop=(kc == NCH - 1))
            m.copy(vtm[t].v, pi_.v, "act")
            f = tq[t % 2]
            m.act(f.v, pz.v, AF.Sigmoid)
            m.tt(f.v, f.v, omlr.v, ALU.mult)
            m.tt(f.v, f.v, lbr.v, ALU.add)
            m.ts(f.v, f.v, 1e-20, ALU.max)
            m.act(f.v, f.v, AF.Ln)
            for h in range(4):
                pc = self.pb[2 + h % 2]
                m.mm(pc[:, 0:128], f[:, 128 * h:128 * h + 128], self.ublk.v)
                m.copy(bT[h][:, t * 128:(t + 1) * 128], pc[:, 0:128], "pool" if False else "dve")
        for h in range(4):
            wq = self.load_fm(l, f"bq_{h}")
            wf = self.load_fm(l, f"bf_{h}")
            wg = self.load_fm(l, f"bg_{h}")
            col = 4 * l + h
            for g in range(NG):
                gs = slice(g * GW, (g + 1) * GW)
                pq, pf = self.pb[0], self.pb[1]
                for kc in range(NCH):
                    m.mm(pq.v, wq[:, kc, :], self.xb[kc][g].v, start=(kc == 0), stop=(kc == NCH - 1))
                for kc in range(NCH):
                    m.mm(pf.v, wf[:, kc, :], self.xb[kc][g].v, start=(kc == 0), stop=(kc == NCH - 1))
                b3 = bT[h][:, gs].rearrange("p (c t) -> p c t", t=64)
                d, e1, kk = tq[0], tq[1], tq[2]
                d3 = d.v.rearrange("p (c t) -> p c t", t=64)
                m.tt(d3, b3[:, :, 63:64].bc([128, 8, 64]), b3, ALU.subtract)
                m.act(e1.v, bT[h][:, gs], AF.Exp)
                m.tt(qi[:, gs], pq.v, e1.v, ALU.mult)
                m.act(e1.v, d.v, AF.Exp)
                m.act(kk.v, pf.v, AF.Sigmoid, scale=-1.0)
                m.ts(kk.v, kk.v, self.omlfm[:, col:col + 1], ALU.mult)
                m.tt(ks[:, gs], kk.v, e1.v, ALU.mult)
                m.act(e1.v, d.v, AF.Exp, scale=-1.0)
                m.tt(qt[:, gs], pq.v, e1.v, ALU.mult)
            m.act(eb.v, bT[h].v.rearrange("p (c t) -> p c t", t=64)[:, :, 63], AF.Exp)
            for t in range(16):
                pt_ = V(self.pb[2 + t % 2], self.pb[2 + t % 2].ap.bitcast(BF16))
                m.transpose(pt_[:, 0:128], ks[:, t * 128:(t + 1) * 128], self.identb.v)
                m.copy(kstm[t].v, pt_[:, 0:128], "act")
            m.memset(S32.v, 0.0)
            m.memset(Sb[0].v, 0.0)
            sbi = 0
            for t in range(16):
                g, t4 = t // 4, t % 4
                tsl = slice(t * 128, (t + 1) * 128)
                pa = self.pb[2]
                po = self.pb[3]
                m.mm(pa[:, 0:128], ks[:, tsl], qt[:, tsl])
                am = ATm[t % 2]
                m.tt(am.v, pa[:, 0:128], self.ublk.v, ALU.mult)
                m.mm(po[:, 0:128], vtm[t][:, 128 * h:128 * h + 128], am.v, start=True, stop=False)
                for c2 in range(2):
                    j = 2 * t + c2
                    rows = slice(64 * c2, 64 * c2 + 64)
                    cols = slice(t * 128 + 64 * c2, t * 128 + 64 * c2 + 64)
                    m.mm(po[:, 64 * c2:64 * c2 + 64], Sb[sbi % 2].v, qi[:, cols], start=False, stop=(c2 == 1))
                    pS = self.pb[4 + c2]
                    m.mm(pS[:, 0:128], kstm[t][rows, :], vtm[t][rows, 128 * h:128 * h + 128])
                    m.stt(S32.v, S32.v, eb[:, j:j + 1], pS[:, 0:128], ALU.mult, ALU.add)
                    sbi += 1
                    m.copy(Sb[sbi % 2].v, S32.v, "pool")
                m.copy(o32[:, t4 * 128:(t4 + 1) * 128], po[:, 0:128], "act")
                if t4 == 3:
                    gs = slice(g * GW, (g + 1) * GW)
                    sq, rs = tq[0], tq[1]
                    A = self.pb[6]
                    m.act(sq.v, o32.v, AF.Square)
                    m.mm(A.v, self.ones32.v, sq.v)
                    m.ts(rs.v, A.v, 1.0 / 128.0, ALU.mult, RMS_EPS, ALU.add)
                    m.act(rs.v, rs.v, AF.Sqrt)
                    m.recip(rs.v, rs.v)
                    pg = self.pb[7]
                    for kc in range(NCH):
                        m.mm(pg.v, wg[:, kc, :], self.xb[kc][g].v, start=(kc == 0), stop=(kc == NCH - 1))
                    m.act(sq.v, pg.v, AF.Silu)
                    m.tt(rs.v, rs.v, sq.v, ALU.mult)
                    m.stt(rs.v, o32.v, self.hng[:, l:l + 1], rs.v, ALU.mult, ALU.mult)
                    y = yo[g % 2]
                    m.copy(y.v, rs.v, "pool")
                    m.dma(self.yb_d[h][:, gs], y.v)

    def mla(self, l):
        m = self.m
        self.phase()
        cv = self.carve
        cqg = [cv(f"cqg{k}", [128, S], BF16) for k in range(3)]
        ckvg = [cv(f"ckvg{k}", [128, S], BF16) for k in range(2)]
        rq = cv("rq", [128, S])
        rkv = cv("rkv", [128, S])
        rkt = cv("rkt", [128, 16])
        vtm = [cv(f"vtm{t}", [128, 512], BF16) for t in range(16)]
        kpe = cv("kpe", [128, S], BF16)
        qh = [cv(f"qh{k}", [128, S], BF16) for k in range(2)]
        kh = [cv(f"kh{k}", [128, S], BF16) for k in range(2)]
        PT = [cv(f"PT{k}", [128, GW], BF16) for k in range(2)]
        rr = cv("rr", [128, GW])
        sqt = [cv(f"sqt{k}", [128, GW]) for k in range(2)]
        t1 = cv("t1", [128, GW])
        t2 = cv("t2", [128, GW])
        yo = [cv(f"yo{k}", [64, GW], BF16) for k in range(2)]
        A = self.pb[6]
        Bt = self.pb[7]
        u = 0
        for (nm, nk, dst, gtab, rdst, dim) in (("cq", 3, cqg, self.qng, rq, 384.0), ("ckv", 2, ckvg, self.kvng, rkv, 256.0)):
            for g in range(NG):
                gs = slice(g * GW, (g + 1) * GW)
                for k in range(nk):
                    w = self.load_fm(l, f"{nm}_{k}")
                    ps = self.pb[u % 2]
                    sq = sqt[u % 2]
                    u += 1
                    for kc in range(NCH):
                        m.mm(ps.v, w[:, kc, :], self.xb[kc][g].v, start=(kc == 0), stop=(kc == NCH - 1))
                    col = l * nk + k
                    m.act(dst[k][:, gs], ps.v, AF.Copy, scale=gtab[:, col:col + 1])
                    m.act(sq.v, ps.v, AF.Square)
                    m.mm(A.v, self.ones32.v, sq.v, start=(k == 0), stop=(k == nk - 1))
                if nm == "ckv":
                    for t4 in range(4):
                        tt_ = g * 4 + t4
                        for k in range(2):
                            m.mm(Bt[:, tt_:tt_ + 1], sqt[(u - 2 + k) % 2][:, t4 * 128:(t4 + 1) * 128], self.ones32[:, 0:1], start=(k == 0), stop=(k == 1))
                m.ts(t1.v, A.v, 1.0 / dim, ALU.mult, RMS_EPS, ALU.add)
                m.act(t1.v, t1.v, AF.Sqrt)
                m.recip(rdst[:, gs], t1.v)
        m.ts(rkt.v, Bt[:, 0:16], 1.0 / 256.0, ALU.mult, RMS_EPS, ALU.add)
        m.act(rkt.v, rkt.v, AF.Sqrt)
        m.recip(rkt.v, rkt.v)
        wv = self.load_tm(l, "ukv_v")
        for t in range(16):
            ps = self.pb[2 + t % 2]
            ts_ = slice(t * 128, (t + 1) * 128)
            for k in range(2):
                m.mm(ps.v, ckvg[k][:, ts_], wv[:, k, :], start=(k == 0), stop=(k == 1))
            m.ts(vtm[t].v, ps.v, rkt[:, t:t + 1], ALU.mult)
        w = self.load_fm(l, "kr96")
        wp = self.load_fm(l, "kr96p")
        R6 = slice(64, 96)
        for g in range(NG):
            gs = slice(g * GW, (g + 1) * GW)
            ps, pp = self.pb[0], self.pb[1]
            for kc in range(NCH):
                m.mm(ps.v, w[:, kc, :], self.xb[kc][g].v, start=(kc == 0), stop=(kc == NCH - 1))
            for kc in range(NCH):
                m.mm(pp.v, wp[:, kc, :], self.xb[kc][g].v, start=(kc == 0), stop=(kc == NCH - 1))
            m.tt(t1[R6, :], ps[R6, :], self.C96[R6, gs], ALU.mult)
            m.tt(t2[R6, :], pp[R6, :], self.S96[R6, gs], ALU.mult)
            m.tt(kpe[R6, gs], t1[R6, :], t2[R6, :], ALU.add)
        scale = 96.0 ** -0.5
        R96 = slice(0, 96)
        R64 = slice(0, 64)
        for h in range(8):
            wq = self.load_fm(l, f"uq_{h}")
            wqp = self.load_fm(l, f"uqp_{h}")
            wk = self.load_fm(l, f"ukn_{h}")
            q_, k_ = qh[h % 2], kh[h % 2]
            for g in range(NG):
                gs = slice(g * GW, (g + 1) * GW)
                ps, pp, pk = self.pb[0], self.pb[1], self.pb[2]
                for k in range(3):
                    m.mm(ps[R96, :], wq[:, k, 0:96], cqg[k][:, gs], start=(k == 0), stop=(k == 2))
                for k in range(3):
                    m.mm(pp[R96, :], wqp[:, k, 0:96], cqg[k][:, gs], start=(k == 0), stop=(k == 2))
                for k in range(2):
                    m.mm(pk[R64, :], wk[:, k, 0:64], ckvg[k][:, gs], start=(k == 0), stop=(k == 1))
                m.copy(t1[R64, :], ps[R64, :], "act")
                m.tt(t1[R6, :], ps[R6, :], self.C96[R6, gs], ALU.mult)
                m.tt(t2[R6, :], pp[R6, :], self.S96[R6, gs], ALU.mult)
                m.tt(t1[R6, :], t1[R6, :], t2[R6, :], ALU.add)
                m.tt(q_[R96, gs], t1[R96, :], rq[R96, gs], ALU.mult)
                m.tt(k_[R64, gs], pk[R64, :], rkv[R64, gs], ALU.mult)
                m.copy(k_[R6, gs], kpe[R6, gs], "pool")
            for qg in range(NG):
                qs = slice(qg * GW, (qg + 1) * GW)
                O, R = self.pb[4], self.pb[5]
                nk = 4 * qg + 4
                for kt in range(nk):
                    st = self.pb[2 + u % 2]
                    pt = PT[u % 2]
                    u += 1
                    ks = slice(kt * 128, (kt + 1) * 128)
                    diag = kt >= 4 * qg
                    m.mm(st.v, k_[R96, ks], q_[R96, qs], start=True, stop=not diag)
                    if diag:
                        rel = kt - 4 * qg
                        m.mm(st.v, self.identb.v, self.causneg[:, rel * GW:(rel + 1) * GW], start=False, stop=True)
                    m.act(pt.v, st.v, AF.Exp, scale=scale)
                    m.mm(O[R64, :], vtm[kt][:, 64 * h:64 * h + 64], pt.v, start=(kt == 0), stop=(kt == nk - 1))
                    m.mm(R[R64, :], self.onesb[:, 0:64], pt.v, start=(kt == 0), stop=(kt == nk - 1))
                m.recip(rr[R64, :], R[R64, :])
                y = yo[(h * NG + qg) % 2]
                m.tt(y.v, O[R64, :], rr[R64, :], ALU.mult)
                m.dma(self.yc_d[h][:, qs], y.v)

    def phase(self):
        m = self.m
        self.aoff = 0
        bar = {}
        for e in ("pe", "act", "dve", "pool"):
            if m.cnt[e] > 0:
                bar[m.cursem[e]] = m.cnt[e]
        for qn in ("sp", "pool"):
            n = m.dcnt[qn]
            k = len(m.dsem[qn])
            for j, sid in enumerate(m.dsem[qn]):
                cntj = (n - j + k - 1) // k if n > j else 0
                if cntj > 0:
                    bar[sid] = 16 * cntj
        self.barrier = bar

    def carve(self, name, shape, dtype=F32):
        esz = 4 if dtype in (F32, I32) else 2
        free = 1
        for d_ in shape[1:]:
            free *= d_
        n32 = (free * esz + 3) // 4
        n32 = (n32 + 7) // 8 * 8
        assert self.aoff + n32 <= self.AW, (name, self.aoff, n32)
        ap = self.arena.ap[:, self.aoff:self.aoff + n32]
        self.aoff += n32
        if dtype != F32:
            ap = ap.bitcast(dtype)
        ap = ap[:, 0:free]
        if len(shape) == 3:
            ap = ap.rearrange("p (a b) -> p a b", a=shape[1])
        if shape[0] < 128:
            ap = ap[0:shape[0]]
        b = Buf(name, ap)
        b.r = dict(self.barrier)
        return b

    def load_fm(self, l, name, dst=None):
        m = self.m
        if dst is None:
            dst = self.wslot[self.wslot_i % self.nslot]
            self.wslot_i += 1
        tid = l * self.nt[0] + self.idx[name]
        m.dma(dst.v.rearrange("p k n -> p (k n)"), self.din["wfm"][tid], "pool")
        return dst

    def load_tm(self, l, name):
        m = self.m
        dst = self.tslot[self.tslot_i % 2]
        self.tslot_i += 1
        tid = l * self.nt[1] + self.idx[name]
        m.dma(dst.v.rearrange("p k n -> p (k n)"), self.din["wtm"][tid], "pool")
        return dst

    def resid_ln(self, l, i, g, delta_fn, final=False):
        m = self.m
        gs = slice(g * GW, (g + 1) * GW)
        A, B = self.pb[6], self.pb[7]
        for c in range(NCH):
            xs = self.xstage[self.cnt_stage % 2]
            self.cnt_stage += 1
            m.dma(xs.v, self.xres[c * 128:(c + 1) * 128, gs])
            ps = delta_fn(c)
            y = self.ybuf[c]
            m.stt(y.v, xs.v, ALPHA, ps, ALU.mult, ALU.add)
            sq = self.sq[c % 2]
            m.act(sq.v, y.v, AF.Square)
            m.mm(A.v, self.ones32.v, y.v, start=(c == 0), stop=(c == NCH - 1))
            m.mm(B.v, self.ones32.v, sq.v, start=(c == 0), stop=(c == NCH - 1))
        mean, rstd = self.mean, self.rstd
        t0 = self.tmpa[0]
        m.act(mean.v, A.v, AF.Copy, scale=1.0 / D)
        m.tt(t0.v, mean.v, mean.v, ALU.mult)
        m.stt(t0.v, B.v, 1.0 / D, t0.v, ALU.mult, ALU.subtract)
        m.ts(t0.v, t0.v, LN_EPS, ALU.add)
        m.act(t0.v, t0.v, AF.Sqrt)
        m.recip(rstd.v, t0.v)
        for c in range(NCH):
            y = self.ybuf[c]
            t = self.tmpa[c % 2]
            m.tt(t.v, y.v, mean.v, ALU.subtract)
            m.tt(t.v, t.v, rstd.v, ALU.mult)
            xn = self.xn[c % 2]
            col = (l * 4 + i) * NCH + c
            m.act(xn.v, t.v, AF.Identity, scale=self.lng[:, col:col + 1], bias=self.lnb[:, col:col + 1])
            if final:
                ev = m.dma(self.out[c * 128:(c + 1) * 128, gs], xn.v)
                self.final_events.append(ev)
            else:
                m.dma(self.xres[c * 128:(c + 1) * 128, gs], xn.v)
                m.copy(self.xb[c][g].v, xn.v, "pool")

    def init_stream(self):
        m = self.m
        for c in range(NCH):
            for g in range(NG):
                gs = slice(g * GW, (g + 1) * GW)
                xs = self.xstage[self.cnt_stage % 2]
                self.cnt_stage += 1
                m.dma(xs.v, self.din["xT"][c * 128:(c + 1) * 128, gs])
                m.dma(self.xres[c * 128:(c + 1) * 128, gs], xs.v)
                m.copy(self.xb[c][g].v, xs.v, "pool")

    def ffn(self, l, i, lni, final=False):
        m = self.m
        self.phase()
        w2res = [self.carve(f"w2res{c}", [128, NJ, 128], BF16) for c in range(NCH)]
        hT = [[self.carve(f"hT{gi}_{j}", [128, GW], BF16) for j in range(NJ)] for gi in range(2)]
        silu = [self.carve(f"silu{k}", [128, GW]) for k in range(2)]
        for c in range(NCH):
            tid = (l * 2 + i) * NCH + c
            m.dma(w2res[c].v.rearrange("p j n -> p (j n)"), self.din["w2"][tid], "pool")
        k = 0
        for half in range(2):
            for j in range(NJ):
                w1 = self.load_fm(l, f"w1_{i}_{j}")
                w3 = self.load_fm(l, f"w3_{i}_{j}")
                for gi in range(2):
                    g = 2 * half + gi
                    p1 = self.pb[(k % 2) * 2]
                    p3 = self.pb[(k % 2) * 2 + 1]
                    for kc in range(NCH):
                        m.mm(p1.v, w1[:, kc, :], self.xb[kc][g].v, start=(kc == 0), stop=(kc == NCH - 1))
                    for kc in range(NCH):
                        m.mm(p3.v, w3[:, kc, :], self.xb[kc][g].v, start=(kc == 0), stop=(kc == NCH - 1))
                    sl = silu[k % 2]
                    k += 1
                    m.act(sl.v, p1.v, AF.Silu)
                    m.stt(hT[gi][j].v, sl.v, 0.5, p3.v, ALU.mult, ALU.mult)
            for gi in range(2):
                g = 2 * half + gi

                def delta(c, gi=gi):
                    ps = self.pb[4 + (c % 2)]
                    for j in range(NJ):
                        m.mm(ps.v, w2res[c][:, j, :], hT[gi][j].v, start=(j == 0), stop=(j == NJ - 1))
                    return ps.v
                self.resid_ln(l, lni, g, delta, final=final)

    def merge(self, l):
        m = self.m
        self.phase()
        wres = [self.carve(f"woutres{c}", [128, 8, 128], BF16) for c in range(NCH)]
        ya = [self.carve(f"ya{h}", [64, GW], BF16) for h in range(8)]
        yb = [self.carve(f"yb{h}", [128, GW], BF16) for h in range(4)]
        yc = [self.carve(f"yc{h}", [64, GW], BF16) for h in range(8)]
        sgt = [self.carve(f"sgt{k}", [128, GW]) for k in range(2)]
        macc = self.carve("macc", [128, GW])
        mtmp = self.carve("mtmp", [128, GW])
        mixed = [self.carve(f"mixed{c}", [128, GW], BF16) for c in range(NCH)]
        for c in range(NCH):
            self.load_fm(l, f"wout_{c}", wres[c])
        k = 0
        for g in range(NG):
            gs = slice(g * GW, (g + 1) * GW)
            for h in range(8):
                m.dma(ya[h].v, self.ya_d[h][:, gs])
                m.dma(yc[h].v, self.yc_d[h][:, gs])
            for h in range(4):
                m.dma(yb[h].v, self.yb_d[h][:, gs])
            for c in range(NCH):
                for b in range(3):
                    wm = self.load_fm(l, f"merge_{b}_{c}")
                    wb = self.load_fm(l, f"wbr_{b}_{c}")
                    psm = self.pb[(k % 2) * 2]
                    psy = self.pb[(k % 2) * 2 + 1]
                    for kc in range(NCH):
                        m.mm(psm.v, wm[:, kc, :], self.xb[kc][g].v, start=(kc == 0), stop=(kc == NCH - 1))
                    if b == 1:
                        for h in range(4):
                            m.mm(psy.v, wb[:, h, :], yb[h].v, start=(h == 0), stop=(h == 3))
                    else:
                        ys = ya if b == 0 else yc
                        for h in range(8):
                            m.mm(psy.v, wb[0:64, h, :], ys[h].v, start=(h == 0), stop=(h == 7))
                    sg = sgt[k % 2]
                    k += 1
                    m.act(sg.v, psm.v, AF.Sigmoid)
                    if b == 0:
                        m.tt(macc.v, sg.v, psy.v, ALU.mult)
                    else:
                        m.tt(mtmp.v, sg.v, psy.v, ALU.mult)
                        m.tt(macc.v, macc.v, mtmp.v, ALU.add, eng="pool")
                m.copy(mixed[c].v, macc.v, "pool")

            def delta(c):
                ps = self.pb[4 + (c % 2)]
                for kc in range(NCH):
                    m.mm(ps.v, wres[c][:, kc, :], mixed[kc].v, start=(kc == 0), stop=(kc == NCH - 1))
                return ps.v
            self.resid_ln(l, 1, g, delta)

    def xattn(self, l):
        m = self.m
        self.phase()
        wres = [self.carve(f"wores{c}", [128, 8, 128], BF16) for c in range(NCH)]
        kT = [self.carve(f"xkT{c}", [128, 256], BF16) for c in range(NCH)]
        vtm = [self.carve(f"xv{k}", [128, 1024], BF16) for k in range(2)]
        qT = [self.carve(f"xq{c}", [128, GW], BF16) for c in range(NCH)]
        at = [self.carve(f"xat{c}", [128, GW], BF16) for c in range(NCH)]
        PT = [self.carve(f"xPT{k}", [128, GW], BF16) for k in range(2)]
        rr = self.carve("xrr", [128, GW])
        self.memT = self.carve("memT", [128, 8, 256], BF16)
        m.dma(self.memT.v.rearrange("p k n -> p (k n)"), self.din["memT"].v, "pool")
        for c in range(NCH):
            self.load_fm(l, f"xo_{c}", wres[c])
        for c in range(NCH):
            wk = self.load_fm(l, f"xk_{c}")
            ps = self.pb[c % 2]
            for kc in range(NCH):
                m.mm(ps[:, 0:256], wk[:, kc, :], self.memT[:, kc, :], start=(kc == 0), stop=(kc == NCH - 1))
            m.copy(kT[c].v, ps[:, 0:256], "act")
        for hv in range(2):
            wv = self.load_tm(l, f"xv_{hv}")
            for kt in range(2):
                ps = self.pb[2 + kt]
                for kc in range(NCH):
                    m.mm(ps.v, self.memT[:, kc, kt * 128:(kt + 1) * 128], wv[:, kc, :], start=(kc == 0), stop=(kc == NCH - 1))
                m.copy(vtm[kt][:, hv * 512:(hv + 1) * 512], ps.v, "act")
        u = 0
        for g in range(NG):
            for c in range(NCH):
                wq = self.load_fm(l, f"xq_{c}")
                ps = self.pb[c % 2]
                for kc in range(NCH):
                    m.mm(ps.v, wq[:, kc, :], self.xb[kc][g].v, start=(kc == 0), stop=(kc == NCH - 1))
                m.copy(qT[c].v, ps.v, "act")
            for h in range(4):
                O0, O1, R = self.pb[2], self.pb[3], self.pb[5]
                for kt in range(2):
                    st = self.pb[u % 2]
                    pt = PT[u % 2]
                    u += 1
                    ks = slice(kt * 128, (kt + 1) * 128)
                    m.mm(st.v, kT[2 * h][:, ks], qT[2 * h].v, start=True, stop=False)
                    m.mm(st.v, kT[2 * h + 1][:, ks], qT[2 * h + 1].v, start=False, stop=True)
                    m.act(pt.v, st.v, AF.Exp, scale=1.0 / 16.0)
                    m.mm(O0.v, vtm[kt][:, 256 * h:256 * h + 128], pt.v, start=(kt == 0), stop=(kt == 1))
                    m.mm(O1.v, vtm[kt][:, 256 * h + 128:256 * h + 256], pt.v, start=(kt == 0), stop=(kt == 1))
                    m.mm(R.v, self.onesb.v, pt.v, start=(kt == 0), stop=(kt == 1))
                m.recip(rr.v, R.v)
                m.tt(at[2 * h].v, O0.v, rr.v, ALU.mult)
                m.tt(at[2 * h + 1].v, O1.v, rr.v, ALU.mult)

            def delta(c):
                ps = self.pb[4] if c % 2 == 0 else self.pb[1]
                for kc in range(NCH):
                    m.mm(ps.v, wres[c][:, kc, :], at[kc].v, start=(kc == 0), stop=(kc == NCH - 1))
                return ps.v
            self.resid_ln(l, 2, g, delta)

    def mixers(self, l):
        if self.ext_y is True:
            return
        if self.ext_y == "nsa" or (not self.ext_y and NSA_ENABLED):
            self.nsa(l)
        elif not self.ext_y and l == 0:
            z = self.tmpa[0]
            self.m.memset(z.v, 0.0)
            zb = self.xn[0]
            for h in range(8):
                for g in range(NG):
                    self.m.dma(self.ya_d[h][:, g * GW:(g + 1) * GW], V(z, z.ap.bitcast(BF16))[0:64, 0:GW])
        if not self.ext_y or self.ext_y == "hgrn":
            self.hgrn(l)
        if not self.ext_y or self.ext_y == "mla":
            self.mla(l)

    def dump_stream(self):
        m = self.m
        for c in range(NCH):
            for g in range(NG):
                gs = slice(g * GW, (g + 1) * GW)
                xs = self.xstage[self.cnt_stage % 2]
                self.cnt_stage += 1
                m.dma(xs.v, self.xres[c * 128:(c + 1) * 128, gs])
                ev = m.dma(self.out[c * 128:(c + 1) * 128, gs], xs.v)
                self.final_events.append(ev)

    def build(self):
        self.init_stream()
        if self.stop == "nsa_only":
            try:
                self.nsa(0)
            except StopIteration:
                pass
            self.final_events.append(self.ya_d.w[0] if self.ya_d.w else None)
            self.m.emit([e for e in self.final_events if e])
            return self.nc
        for l in range(self.n_layers):
            last = (l == DEPTH - 1)
            self.ffn(l, 0, 0)
            if self.stop == "ffn1":
                self.dump_stream()
                break
            self.mixers(l)
            if self.stop == "mixers":
                self.dump_stream()
                break
            self.merge(l)
            if self.stop == "mix":
                self.dump_stream()
                break
            self.xattn(l)
            if self.stop == "xa":
                self.dump_stream()
                break
            self.ffn(l, 1, 3, final=last)
            if self.stop == "full" and l == self.n_layers - 1 and not last:
                self.dump_stream()
        self.m.emit(self.final_events)
        return self.nc


def kernel(**inputs):
    shared, per_core, idx, nt = prep_inputs(inputs)
    shapes = {k: v.shape for k, v in shared.items()}
    shapes.update({k: v.shape for k, v in per_core[0].items()})
    prog = Prog(shapes, idx, nt)
    nc = prog.build()
    in_maps = [dict(shared, **pc) for pc in per_core]
    res = run_bass_kernel_spmd(nc, in_maps, core_ids=list(range(len(per_core))))
    out = np.stack([np.ascontiguousarray(r["out"].T) for r in res.results], axis=0)
    return out.astype(np.float32)
```

```python
import contextlib
import os
import numpy as np
import concourse.bass as bass
import concourse.mybir as mybir
from concourse.bass_utils import run_bass_kernel_spmd

F32 = mybir.dt.float32
BF16 = mybir.dt.bfloat16
I32 = mybir.dt.int32
AF = mybir.ActivationFunctionType
ALU = mybir.AluOpType

ENGS = ("pe", "act", "dve", "pool", "sp")
EPOCH = 60000


class V:
    __slots__ = ("buf", "ap")

    def __init__(self, buf, ap):
        self.buf = buf
        self.ap = ap

    def __getitem__(self, k):
        return V(self.buf, self.ap[k])

    def rearrange(self, s, **kw):
        return V(self.buf, self.ap.rearrange(s, **kw))

    def bc(self, shape):
        return V(self.buf, self.ap.to_broadcast(shape))


class Buf:
    def __init__(self, name, ap):
        self.name = name
        self.ap = ap
        self.w = []
        self.r = {}
        self.wpe = False

    def __getitem__(self, k):
        return V(self, self.ap[k])

    @property
    def v(self):
        return V(self, self.ap)


class MK:
    def __init__(self, nc, n_dma_sems=8):
        self.nc = nc
        self.es = contextlib.ExitStack()
        self.q = {e: [] for e in ENGS}
        self.cnt = {e: 0 for e in ENGS}
        self.cursem = {}
        self.waited = {e: {} for e in ENGS}
        self.sems = {}
        self.nsem = 0
        self.pesems = set()
        for e in ("pe", "act", "dve", "pool"):
            self.cursem[e] = self._newsem(e)
        self.pesems.add(self.cursem["pe"])
        self.dsem = {}
        self.dcnt = {}
        for qn in ("sp", "pool"):
            self.dsem[qn] = [self._newsem("d" + qn) for _ in range(n_dma_sems)]
            self.dcnt[qn] = 0
        self.nbuf = 0

    def _newsem(self, tag):
        sid = self.nsem
        self.nsem += 1
        h = self.es.enter_context(self.nc.semaphore(f"s{sid}_{tag}"))
        self.sems[sid] = h
        return sid

    def sb(self, name, shape, dtype=F32):
        self.nbuf += 1
        t = self.nc.alloc_sbuf_tensor(f"{name}_{self.nbuf}", list(shape), dtype)
        return Buf(name, t.ap())

    def ps(self, name, shape, dtype=F32):
        self.nbuf += 1
        t = self.nc.alloc_psum_tensor(f"{name}_{self.nbuf}", list(shape), dtype)
        return Buf(name, t.ap())

    def dram(self, name, shape, dtype=F32, kind="Internal"):
        t = self.nc.dram_tensor(name, list(shape), dtype, kind=kind)
        return Buf(name, t.ap())

    def sub(self, buf, key, name=None):
        return Buf(name or buf.name, buf.ap[key])

    def alias(self, name, ap, olds):
        b = Buf(name, ap)
        for o in olds:
            for ev in o.w:
                b.r[ev[0]] = max(b.r.get(ev[0], 0), ev[1])
            for s, v in o.r.items():
                b.r[s] = max(b.r.get(s, 0), v)
        return b

    def _collect(self, eng, reads, writes, pe_acc=False):
        need = {}

        def add(s, v):
            if need.get(s, 0) < v:
                need[s] = v
        for b in reads:
            for ev in b.w:
                add(*ev)
        for b in writes:
            if not (pe_acc and b.wpe and eng == "pe"):
                for ev in b.w:
                    add(*ev)
            for s, v in b.r.items():
                add(s, v)
        wl = []
        wd = self.waited[eng]
        for s, v in need.items():
            if eng == "pe" and s in self.pesems:
                continue
            if wd.get(s, 0) >= v:
                continue
            wd[s] = v
            wl.append((s, v))
        return wl

    def _record(self, ev, reads, writes, eng):
        for b in writes:
            b.w = [ev]
            b.r = {}
            b.wpe = (eng == "pe")
        for b in reads:
            if b in writes:
                continue
            if b.r.get(ev[0], 0) < ev[1]:
                b.r[ev[0]] = ev[1]

    def op(self, eng, fn, reads, writes, pe_acc=False):
        reads = [x.buf if isinstance(x, V) else x for x in reads]
        writes = [x.buf if isinstance(x, V) else x for x in writes]
        wl = self._collect(eng, reads, writes, pe_acc)
        if self.cnt[eng] >= EPOCH:
            self.cursem[eng] = self._newsem(eng)
            if eng == "pe":
                self.pesems.add(self.cursem[eng])
            self.cnt[eng] = 0
        self.cnt[eng] += 1
        ev = (self.cursem[eng], self.cnt[eng])
        self.q[eng].append((wl, fn, ev[0], 1))
        self._record(ev, reads, writes, eng)
        return ev

    def dma(self, out, in_, queue="sp", **kw):
        eng = queue
        reads = [in_.buf]
        writes = [out.buf]
        i = self.dcnt[queue]
        self.dcnt[queue] += 1
        sems = self.dsem[queue]
        sid = sems[i % len(sems)]
        val = 16 * (i // len(sems) + 1)
        wl = self._collect(eng, reads, writes)
        if val > 16:
            wd = self.waited[eng]
            if wd.get(sid, 0) < val - 16:
                wd[sid] = val - 16
                wl.append((sid, val - 16))
        oap, iap = out.ap, in_.ap
        self.q[eng].append((wl, lambda e: e.dma_start(out=oap, in_=iap, **kw), sid, 16))
        ev = (sid, val)
        self._record(ev, reads, writes, eng)
        return ev

    @staticmethod
    def _a(x):
        return x.ap if isinstance(x, V) else x

    def mm(self, out, lhsT, rhs, start=True, stop=True):
        o, l, r = out.ap, lhsT.ap, rhs.ap
        return self.op("pe", lambda e: e.matmul(o, l, r, start=start, stop=stop),
                       [lhsT, rhs], [out], pe_acc=not start)

    def transpose(self, out, in_, ident):
        o, i, d = out.ap, in_.ap, ident.ap
        return self.op("pe", lambda e: e.transpose(o, i, d), [in_, ident], [out])

    def act(self, out, in_, func, bias=None, scale=None, accum_out=None):
        kw = {}
        reads = [in_]
        writes = [out]
        if bias is not None:
            kw["bias"] = self._a(bias)
            if isinstance(bias, V):
                reads.append(bias)
        if scale is not None:
            kw["scale"] = self._a(scale)
            if isinstance(scale, V):
                reads.append(scale)
        if accum_out is not None:
            kw["accum_out"] = accum_out.ap
            writes.append(accum_out)
        o, i = out.ap, in_.ap
        return self.op("act", lambda e: e.activation(o, i, func, **kw), reads, writes)

    def tt(self, out, in0, in1, op, eng="dve", deps=()):
        o, a, b = out.ap, in0.ap, in1.ap
        return self.op(eng, lambda e: e.tensor_tensor(o, a, b, op), [in0, in1] + list(deps), [out])

    def ts(self, out, in0, s1, op0, s2=None, op1=None, eng="dve", deps=()):
        reads = [in0] + [s for s in (s1, s2) if isinstance(s, V)] + list(deps)
        o, a = out.ap, in0.ap
        a1, a2 = self._a(s1), self._a(s2)
        kw = {}
        if op1 is not None:
            kw["op1"] = op1
        return self.op(eng, lambda e: e.tensor_scalar(o, a, a1, a2, op0, **kw), reads, [out])

    def stt(self, out, in0, scalar, in1, op0, op1, eng="dve"):
        reads = [in0, in1] + ([scalar] if isinstance(scalar, V) else [])
        o, a, b, s = out.ap, in0.ap, in1.ap, self._a(scalar)
        return self.op(eng, lambda e: e.scalar_tensor_tensor(o, a, s, b, op0, op1), reads, [out])

    def copy(self, out, in_, eng="dve"):
        o, i = out.ap, in_.ap
        if eng == "act":
            return self.op(eng, lambda e: e.copy(o, i), [in_], [out])
        return self.op(eng, lambda e: e.tensor_copy(o, i), [in_], [out])

    def memset(self, out, val, eng="dve"):
        o = out.ap
        return self.op(eng, lambda e: e.memset(o, val), [], [out])

    def recip(self, out, in_):
        o, i = out.ap, in_.ap
        return self.op("dve", lambda e: e.reciprocal(o, i), [in_], [out])

    def max8(self, out, in_):
        o, i = out.ap, in_.ap
        return self.op("dve", lambda e: e.max(o, i), [in_], [out])

    def emit(self, final_events):
        nc = self.nc
        sems = self.sems
        q = self.q

        def run(e, items, extra=None):
            for wl, fn, sid, inc in items:
                for s, v in wl:
                    e.wait_ge(sems[s], v)
                ins = fn(e)
                if inc:
                    ins.then_inc(sems[sid], inc)
            if extra:
                for s, v in extra:
                    e.wait_ge(sems[s], v)

        with nc.Block() as block:
            @block.tensor
            def _(e):
                run(e, q["pe"])

            @block.vector
            def _(e):
                run(e, q["dve"])

            @block.scalar
            def _(e):
                run(e, q["act"])

            @block.gpsimd
            def _(e):
                run(e, q["pool"])

            @block.sync
            def _(e):
                run(e, q["sp"], list(final_events))


D = 1024
S = 2048
DEPTH = 4
NCH = 8
NG = 4
GW = 512
DFF = 2816
NJ = 22
ALPHA = (2.0 * DEPTH) ** 0.25
LN_EPS = 1e-5
RMS_EPS = 1e-6
N_IN = 7096
O_Q, O_KC, O_VC, O_KS, O_VS, O_KW, O_VW, O_GATE = 0, 512, 640, 768, 896, 1024, 1152, 1280
O_BQ, O_BF, O_BI, O_BG = 1304, 1816, 2328, 2840
O_CQ, O_CKV, O_KR, O_MERGE = 3352, 3736, 3992, 4024
NEGM = -30000.0
NSA_ENABLED = True


def _fm_tile(W, cols=None):
    K = W.shape[0]
    if cols is None:
        cols = np.arange(W.shape[1])
    cols = np.asarray(cols, dtype=np.int64)
    t = np.zeros((1024, 128), np.float32)
    ok = cols >= 0
    idxs = np.nonzero(ok)[0]
    t[:K, idxs] = W[:, cols[ok]]
    return t.reshape(8, 128, 128).transpose(1, 0, 2).reshape(128, 1024)


def _tm_tile(W, cols):
    K = W.shape[0]
    cols = np.asarray(cols, dtype=np.int64)
    t = np.zeros((1024, 512), np.float32)
    t[:K, :len(cols)] = W[:, cols]
    return t.reshape(8, 128, 512).transpose(1, 0, 2).reshape(128, 4096)


def _rows64_tile(W, c):
    t = np.zeros((128, 8, 128), np.float32)
    t[:64] = W[:, c * 128:(c + 1) * 128].reshape(8, 64, 128).transpose(1, 0, 2)
    return t.reshape(128, 1024)


def layer_tiles(inp, l):
    r = np.arange
    fm, tm = [], []
    win = inp["w_in"][l]
    for i in range(2):
        for j in range(NJ):
            fm.append((f"w1_{i}_{j}", _fm_tile(inp["ffn_w1"][l, i], r(128 * j, 128 * j + 128))))
            fm.append((f"w3_{i}_{j}", _fm_tile(inp["ffn_w3"][l, i], r(128 * j, 128 * j + 128))))
    for b in range(3):
        for c in range(NCH):
            fm.append((f"merge_{b}_{c}", _fm_tile(win, O_MERGE + 1024 * b + 128 * c + r(128))))
    for c in range(NCH):
        fm.append((f"wbr_0_{c}", _rows64_tile(inp["w_branch"][l, 0], c)))
        fm.append((f"wbr_1_{c}", _fm_tile(inp["w_branch"][l, 1], 128 * c + r(128))))
        fm.append((f"wbr_2_{c}", _rows64_tile(inp["w_branch"][l, 2], c)))
        fm.append((f"wout_{c}", _fm_tile(inp["w_out"][l], 128 * c + r(128))))
        fm.append((f"xq_{c}", _fm_tile(inp["xa_wq"][l], 128 * c + r(128))))
        fm.append((f"xk_{c}", _fm_tile(inp["xa_wk"][l], 128 * c + r(128))))
        fm.append((f"xo_{c}", _fm_tile(inp["xa_wo"][l], 128 * c + r(128))))
    for hv in range(2):
        tm.append((f"xv_{hv}", _tm_tile(inp["xa_wv"][l], 512 * hv + r(512))))
    EXTRA_TILES(inp, l, fm, tm)
    return fm, tm


def EXTRA_TILES(inp, l, fm, tm):
    r = np.arange
    win = inp["w_in"][l]
    for k in range(3):
        fm.append((f"cq_{k}", _fm_tile(win, O_CQ + 128 * k + r(128))))
    for k in range(2):
        fm.append((f"ckv_{k}", _fm_tile(win, O_CKV + 128 * k + r(128))))
    pad64 = -np.ones(64, np.int64)
    fm.append(("kr96", _fm_tile(win, np.concatenate([pad64, O_KR + r(32)]))))
    fm.append(("kr96p", _fm_tile(win, np.concatenate([pad64, O_KR + 16 + r(16), O_KR + r(16)]))))
    uq = inp["mla_w_uq"][l]
    ukv = inp["mla_w_ukv"][l]
    for h in range(8):
        fm.append((f"uq_{h}", _fm_tile(uq, 96 * h + r(96))))
        fm.append((f"uqp_{h}", _fm_tile(uq, np.concatenate([pad64, 96 * h + 80 + r(16), 96 * h + 64 + r(16)]))))
        fm.append((f"ukn_{h}", _fm_tile(ukv, 128 * h + r(64))))
    def ropeperm(base, n):
        cols = -np.ones(n, np.int64)
        for j in range(n):
            d = j % 64
            if d < 8:
                cols[j] = base + j + 8
            elif d < 16:
                cols[j] = base + j - 8
        return cols
    for c in range(4):
        fm.append((f"q_{c}", _fm_tile(win, O_Q + 128 * c + r(128))))
        fm.append((f"qp_{c}", _fm_tile(win, ropeperm(O_Q + 128 * c, 128))))
    for g in range(2):
        dup = np.concatenate([64 * g + r(64), 64 * g + r(64)])
        pd = np.concatenate([ropeperm(64 * g, 64), ropeperm(64 * g, 64)])
        fm.append((f"kcd_{g}", _fm_tile(win, O_KC + dup)))
        fm.append((f"vcd_{g}", _fm_tile(win, O_VC + dup)))
        fm.append((f"ksd_{g}", _fm_tile(win, O_KS + dup)))
        fm.append((f"ksdp_{g}", _fm_tile(win, np.where(pd >= 0, O_KS + pd, -1))))
        fm.append((f"kwd_{g}", _fm_tile(win, O_KW + dup)))
        fm.append((f"kwdp_{g}", _fm_tile(win, np.where(pd >= 0, O_KW + pd, -1))))
        tm.append((f"vsw_{g}", _tm_tile(win, np.concatenate([O_VS + 64 * g + r(64), O_VW + 64 * g + r(64)]))))
    fm.append(("gate", _fm_tile(win, O_GATE + r(24))))
    for kv in range(2):
        w1 = inp["nsa_cmp_w1"][l, kv]
        t = w1.reshape(16, 128, 256).transpose(1, 0, 2).reshape(128, 4096)
        tm.append((f"cw1_{kv}", np.ascontiguousarray(t)))
    w2 = inp["nsa_cmp_w2"][l]
    fm.append(("cw2k", _fm_tile(w2[0], np.concatenate([r(64), r(64)]))))
    fm.append(("cw2v", _fm_tile(w2[1], r(64))))
    for h in range(4):
        fm.append((f"bq_{h}", _fm_tile(win, O_BQ + 128 * h + r(128))))
        fm.append((f"bf_{h}", _fm_tile(win, O_BF + 128 * h + r(128))))
        fm.append((f"bg_{h}", _fm_tile(win, O_BG + 128 * h + r(128))))
    tm.append(("bf_tm", _tm_tile(win, O_BF + r(512))))
    tm.append(("bi_tm", _tm_tile(win, O_BI + r(512))))
    tm.append(("ukv_v", _tm_tile(ukv, np.concatenate([128 * h + 64 + r(64) for h in range(8)]))))


def prep_inputs(inputs):
    f = lambda k: np.asarray(inputs[k], dtype=np.float32)
    inp = {k: f(k) for k in inputs if k != "positions"}
    shared = {}
    fms, tms = [], []
    for l in range(DEPTH):
        fm, tm = layer_tiles(inp, l)
        if l == 0:
            idx = {n: i for i, (n, _) in enumerate(fm)}
            idx.update({n: i for i, (n, _) in enumerate(tm)})
            nt = (len(fm), len(tm))
        fms.extend(a for _, a in fm)
        tms.extend(a for _, a in tm)
    shared["wfm"] = np.stack(fms)
    shared["wtm"] = np.stack(tms)
    w2 = inp["ffn_w2"].reshape(DEPTH, 2, NJ, 128, NCH, 128).transpose(0, 1, 4, 3, 2, 5)
    shared["w2"] = np.ascontiguousarray(w2).reshape(DEPTH * 2 * NCH, 128, NJ * 128)
    shared["lng"] = np.ascontiguousarray(inp["ln_g"].reshape(DEPTH, 4, NCH, 128).transpose(3, 0, 1, 2)).reshape(128, DEPTH * 4 * NCH)
    shared["lnb"] = np.ascontiguousarray(inp["ln_b"].reshape(DEPTH, 4, NCH, 128).transpose(3, 0, 1, 2)).reshape(128, DEPTH * 4 * NCH)
    shared["ones32"] = np.ones((128, 128), np.float32)
    EXTRA_SHARED(inp, shared)
    x = inp["x"]
    mem = inp["mem"]
    per_core = []
    for b in range(x.shape[0]):
        per_core.append({"xT": np.ascontiguousarray(x[b].T),
                         "posr": np.ascontiguousarray(np.broadcast_to(np.asarray(inputs["positions"])[b].astype(np.int32)[None, :], (128, S))),
                         "memT": np.ascontiguousarray(mem[b].T.reshape(8, 128, 256).transpose(1, 0, 2)).reshape(128, 2048)})
    return shared, per_core, idx, nt


def EXTRA_SHARED(inp, shared):
    shared["ident"] = np.eye(128, dtype=np.float32)
    k = np.arange(128)[:, None]
    q = np.arange(512)[None, :]
    shared["causneg"] = np.concatenate([np.where(128 * rel + k <= q, 0.0, NEGM) for rel in range(4)], axis=1).astype(np.float32)
    inv96 = np.zeros((128, 1), np.float32)
    sgn96 = np.zeros((128, 1), np.float32)
    fr = (500000.0 ** (-np.arange(16, dtype=np.float32) / 16)).astype(np.float32)
    inv96[64:80, 0] = fr
    inv96[80:96, 0] = fr
    sgn96[64:80, 0] = -1.0
    sgn96[80:96, 0] = 1.0
    shared["ropec"] = np.concatenate([inv96, sgn96], axis=1)
    n = np.arange(128)[:, None]
    t = np.arange(S)[None, :]
    shared["cmpneg"] = np.where((16 * n + 31 <= t) & (n < 127), 0.0, NEGM).astype(np.float32)
    kk_ = np.arange(128)[:, None]
    qq_ = np.arange(512)[None, :]
    wn = []
    for rel in range(-2, 4):
        kp = 128 * rel + kk_
        wn.append(np.where((kp <= qq_) & (kp > qq_ - 256), 0.0, NEGM))
    shared["winneg"] = np.concatenate(wn, axis=1).astype(np.float32)
    es = np.zeros((128, 16 * 128), np.float32)
    for kt in range(16):
        for k_ in range(128):
            es[2 * kt + k_ // 64, kt * 128 + k_] = 1.0
    shared["esel"] = es
    eg = np.zeros((128, 24 * 64), np.float32)
    for j in range(24):
        eg[j, j * 64:(j + 1) * 64] = 1.0
    shared["egate"] = eg
    start = np.arange(127) * 16
    jn = np.arange(32)
    ovl = np.clip(np.minimum(start[:, None] + 32, (jn[None, :] + 1) * 64) - np.maximum(start[:, None], jn[None, :] * 64), 0, None) / 32.0
    o128 = np.zeros((128, 32), np.float32)
    o128[:127] = ovl
    shared["ovl"] = o128
    tt_ = np.arange(S)[:, None]
    valid = jn[None, :] * 64 <= tt_
    cur = tt_ // 64
    forced = (jn[None, :] == 0) | (jn[None, :] == cur) | (jn[None, :] == cur - 1)
    vnf = (valid & ~forced).astype(np.float32)
    addm = np.where(valid & forced, 1e9, np.where(valid, 0.0, -1e9)).astype(np.float32)
    shared["vnf"] = np.ascontiguousarray(vnf.reshape(16, 128, 32).transpose(1, 0, 2)).reshape(128, 512)
    shared["addm"] = np.ascontiguousarray(addm.reshape(16, 128, 32).transpose(1, 0, 2)).reshape(128, 512)
    invn = np.zeros((128, 2), np.float32)
    frn = (500000.0 ** (-np.arange(8, dtype=np.float32) / 8)).astype(np.float32)
    for base in (0, 64):
        invn[base:base + 8, 0] = frn
        invn[base + 8:base + 16, 0] = frn
        invn[base:base + 8, 1] = -1.0
        invn[base + 8:base + 16, 1] = 1.0
    shared["ropen"] = invn
    pe = inp["nsa_cmp_pos"]
    shared["cpe"] = np.ascontiguousarray(pe.reshape(DEPTH, 2, 16, 128).transpose(3, 0, 1, 2)).reshape(128, DEPTH * 2 * 16)
    lbl = inp["hgrn_lb_logits"]
    shared["lblfm"] = np.ascontiguousarray(lbl.reshape(DEPTH, 4, 128).transpose(2, 0, 1)).reshape(128, 16)
    shared["lblrep"] = np.ascontiguousarray(np.broadcast_to(lbl.reshape(1, DEPTH * 512), (128, DEPTH * 512)))
    shared["hng"] = np.ascontiguousarray(inp["hgrn_norm_g"].T)
    ii = np.arange(128)
    shared["ublk"] = ((ii[:, None] <= ii[None, :]) & (ii[:, None] // 64 == ii[None, :] // 64)).astype(np.float32)
    shared["qng"] = np.ascontiguousarray(inp["mla_q_norm_g"].reshape(DEPTH, 3, 128).transpose(2, 0, 1)).reshape(128, DEPTH * 3)
    shared["kvng"] = np.ascontiguousarray(inp["mla_kv_norm_g"].reshape(DEPTH, 2, 128).transpose(2, 0, 1)).reshape(128, DEPTH * 2)


class Prog:
    AW = 23808

    def __init__(self, shapes, idx, nt, n_layers=DEPTH, stop=None, ext_y=False):
        self.idx = idx
        self.nt = nt
        self.n_layers = n_layers
        self.stop = stop
        self.ext_y = ext_y
        nc = bass.Bass("TRN2", target_bir_lowering=False)
        self.nc = nc
        m = MK(nc)
        self.m = m
        self.din = {k: m.dram(k, list(s), I32 if k == "posr" else F32, kind="ExternalInput") for k, s in shapes.items()}
        self.out = m.dram("out", [D, S], F32, kind="ExternalOutput")
        self.xres = m.dram("xres", [D, S], F32)
        yk = lambda nm: "ExternalInput" if (ext_y is True or (ext_y and ext_y != nm)) else ("ExternalOutput" if ext_y == nm else "Internal")
        self.ya_d = m.dram("ya_d", [8, 64, S], BF16, kind=yk("nsa"))
        self.yb_d = m.dram("yb_d", [4, 128, S], BF16, kind=yk("hgrn"))
        self.yc_d = m.dram("yc_d", [8, 64, S], BF16, kind=yk("mla"))
        self.xb_t = m.sb("xb", [128, NCH, S], BF16)
        self.xb = [[m.sub(self.xb_t, (slice(None), c, slice(g * GW, (g + 1) * GW)), f"xb{c}_{g}") for g in range(NG)] for c in range(NCH)]
        self.ones32 = m.sb("ones32", [128, 128])
        self.onesb = m.sb("onesb", [128, 128], BF16)
        self.lng = m.sb("lng", [128, DEPTH * 4 * NCH])
        self.lnb = m.sb("lnb", [128, DEPTH * 4 * NCH])
        m.dma(self.ones32.v, self.din["ones32"].v)
        m.dma(self.lng.v, self.din["lng"].v)
        m.dma(self.lnb.v, self.din["lnb"].v)
        m.dma(self.onesb.v, self.din["ones32"].v, "pool")
        self.pb = [m.ps(f"pb{i}", [128, GW]) for i in range(8)]
        self.nslot = 4
        self.wslot = [m.sb(f"wslot{i}", [128, 8, 128], BF16) for i in range(self.nslot)]
        self.wslot_i = 0
        self.tslot = [m.sb(f"tslot{i}", [128, 8, 512], BF16) for i in range(2)]
        self.tslot_i = 0
        self.xstage = [m.sb(f"xstage{i}", [128, GW]) for i in range(2)]
        self.ybuf = [m.sb(f"y{c}", [128, GW]) for c in range(NCH)]
        self.sq = [m.sb("sq0", [128, GW])] * 2
        self.mean = m.sb("mean", [128, GW])
        self.rstd = m.sb("rstd", [128, GW])
        self.tmpa = [m.sb(f"tmpa{i}", [128, GW]) for i in range(2)]
        self.xn = [m.sb(f"xn{i}", [128, GW]) for i in range(2)]
        self.arena = m.sb("arena", [128, self.AW])
        self.aoff = 0
        self.barrier = {}
        self.cnt_stage = 0
        self.final_events = []
        self.setup_consts()

    def ld_const(self, name, shape, dtype=F32, key=None):
        m = self.m
        b = m.sb(name, shape, dtype)
        m.dma(b.v, self.din[key or name].v, "pool" if dtype != F32 else "sp")
        return b

    def rope_tables(self, inv_sgn, rows, name):
        m = self.m
        C = m.sb(name + "C", [128, S], BF16)
        Sn = m.sb(name + "S", [128, S], BF16)
        TWO_PI = 2.0 * np.pi
        for r0, r1 in rows:
            for g in range(NG):
                gs = slice(g * GW, (g + 1) * GW)
                pi_ = self.tmpa[0]
                pf = self.tmpa[1]
                m.dma(V(pi_, pi_.ap.bitcast(I32))[r0:r1, :], self.din["posr"][r0:r1, gs])
                for which, dst in ((0, Sn), (1, C)):
                    ang = self.xn[0]
                    kk = self.xn[1]
                    m.copy(pf[r0:r1, :], V(pi_, pi_.ap.bitcast(I32))[r0:r1, :])
                    m.ts(ang[r0:r1, :], pf[r0:r1, :], inv_sgn[r0:r1, 0:1], ALU.mult, (np.pi / 2 if which else 0.0), ALU.add)
                    m.ts(kk[r0:r1, :], ang[r0:r1, :], 1.0 / TWO_PI, ALU.mult)
                    ki = self.sq[0]
                    m.copy(V(ki, ki.ap.bitcast(I32))[r0:r1, :], kk[r0:r1, :])
                    m.copy(kk[r0:r1, :], V(ki, ki.ap.bitcast(I32))[r0:r1, :])
                    m.stt(ang[r0:r1, :], kk[r0:r1, :], -TWO_PI, ang[r0:r1, :], ALU.mult, ALU.add)
                    m.ts(kk[r0:r1, :], ang[r0:r1, :], np.pi, ALU.is_gt)
                    m.stt(ang[r0:r1, :], kk[r0:r1, :], -TWO_PI, ang[r0:r1, :], ALU.mult, ALU.add)
                    m.ts(kk[r0:r1, :], ang[r0:r1, :], -np.pi, ALU.is_lt)
                    m.stt(ang[r0:r1, :], kk[r0:r1, :], TWO_PI, ang[r0:r1, :], ALU.mult, ALU.add)
                    m.act(kk[r0:r1, :], ang[r0:r1, :], AF.Sin)
                    if which == 0:
                        m.ts(dst[r0:r1, gs], kk[r0:r1, :], inv_sgn[r0:r1, 1:2], ALU.mult)
                    else:
                        m.copy(dst[r0:r1, gs], kk[r0:r1, :])
        return C, Sn

    def setup_consts(self):
        m = self.m
        self.identb = self.ld_const("identb", [128, 128], BF16, "ident")
        self.causneg = self.ld_const("causneg", [128, 4 * GW], BF16)
        self.ropec96 = self.ld_const("ropec", [128, 2])
        self.qng = self.ld_const("qng", [128, DEPTH * 3])
        self.kvng = self.ld_const("kvng", [128, DEPTH * 2])
        if not self.ext_y or self.ext_y == "mla":
            self.C96, self.S96 = self.rope_tables(self.ropec96, [(64, 96)], "r96")
        if not self.ext_y or self.ext_y == "hgrn":
            self.hgrn_consts()
        if self.ext_y == "nsa" or (not self.ext_y and NSA_ENABLED):
            self.ropen = self.ld_const("ropen", [128, 2])
            self.Cn, self.Sn = self.rope_tables(self.ropen, [(0, 16), (64, 80)], "rn")

    def nsa(self, l):
        m = self.m
        self.phase()
        cv = self.carve
        scale = 64.0 ** -0.5

        def ldc(name, shape, dtype=F32):
            b = cv(name, shape, dtype)
            m.dma(b.v, self.din[name].v, "pool" if dtype != F32 else "sp")
            return b
        cmpneg = ldc("cmpneg", [128, S], BF16)
        winneg = ldc("winneg", [128, 6 * GW], BF16)
        esel = ldc("esel", [128, 16 * 128], BF16)
        egate = ldc("egate", [128, 24 * 64], BF16)
        ovl = ldc("ovl", [128, 32])
        vnf = ldc("vnf", [128, 512])
        addm = ldc("addm", [128, 512])
        cpe = cv("cpe", [128, 32], BF16)
        m.dma(cpe.v, self.din["cpe"][:, l * 32:(l + 1) * 32], "pool")
        sg = cv("sg", [128, S], BF16)
        cbias = cv("cbias", [128, 2])
        slb = cv("nslb", [128, 32], BF16)
        PT = [cv(f"nPT{k}", [128, GW], BF16) for k in range(2)]
        P32 = cv("nP32", [128, GW])
        pn = cv("npn", [128, GW])
        pns = cv("npns", [128, GW])
        rr = cv("nrr", [128, GW])
        t1 = cv("nt1", [128, GW])
        t2 = cv("nt2", [128, GW])
        sc = cv("nsc", [128, 32])
        sl = cv("nsl", [128, 32])
        m8 = cv("nm8", [128, 8])
        hT = [cv(f"nhT{k}", [128, 128], BF16) for k in range(2)]
        kcmp = cv("nkcmp", [128, 128], BF16)
        vcmp = cv("nvcmp", [128, 64])
        yo = [cv(f"nyo{k}", [64, GW], BF16) for k in range(2)]
        yc_ = [cv(f"nyc{k}", [64, GW], BF16) for k in range(2)]
        qT = [cv(f"nq{k}", [128, S], BF16) for k in range(2)]
        qr = [cv(f"nqr{k}", [128, S], BF16) for k in range(2)]
        ksd = cv("nksd", [128, S], BF16)
        kwd = cv("nkwd", [128, S], BF16)
        zk = cv("nzk", [128, S + 16], BF16)
        zv = cv("nzv", [128, S + 16], BF16)
        vsw = [cv(f"nvsw{t}", [128, 128], BF16) for t in range(16)]
        selT = cv("nselT", [128, S], BF16)
        Cn, Sn = self.Cn, self.Sn
        m.memset(zk.v, 0.0)
        m.memset(zv.v, 0.0)
        m.memset(hT[0].v, 0.0)
        m.memset(selT.v, 0.0)
        m.memset(hT[1].v, 0.0)
        ROPE_ROWS = (slice(0, 16), slice(64, 80))
        u = 0
        w = self.load_fm(l, "gate")
        for g4 in range(NG):
            gs = slice(g4 * GW, (g4 + 1) * GW)
            ps = self.pb[g4 % 2]
            for kc in range(NCH):
                m.mm(ps.v, w[:, kc, :], self.xb[kc][g4].v, start=(kc == 0), stop=(kc == NCH - 1))
            m.act(sg[0:32, gs], ps[0:32, :], AF.Sigmoid)
        if os.environ.get("NSA_STOP") == "1":
            return

        def proj(name, g4):
            w_ = self.load_fm(l, name)
            nonlocal u
            ps_ = self.pb[u % 2]
            u += 1
            for kc in range(NCH):
                m.mm(ps_.v, w_[:, kc, :], self.xb[kc][g4].v, start=(kc == 0), stop=(kc == NCH - 1))
            return ps_

        def roped(dst, name, pname, g4):
            gs = slice(g4 * GW, (g4 + 1) * GW)
            ps_ = proj(name, g4)
            if os.environ.get("NSA_STOP") == "2a":
                raise StopIteration
            pp_ = proj(pname, g4)
            if os.environ.get("NSA_STOP") == "2b":
                raise StopIteration
            m.copy(dst[:, gs], ps_.v, "act")
            if os.environ.get("NSA_STOP") == "2c":
                raise StopIteration
            for rws in (ROPE_ROWS if os.environ.get("NSA_ROPE", "1") == "1" else ()):
                m.tt(t1[rws, :], ps_[rws, :], Cn[rws, gs], ALU.mult, deps=[dst])
                m.tt(t2[rws, :], pp_[rws, :], Sn[rws, gs], ALU.mult)
                m.tt(dst[rws, gs], t1[rws, :], t2[rws, :], ALU.add)
            return ps_

        def gate_mul(h, br, R_, O_, rows, qs, dst, add_to=None):
            m.ts(rr[rows, :], R_[rows, :], 1e-30, ALU.max)
            m.recip(rr[rows, :], rr[rows, :])
            pg = self.pb[7]
            j = h * 3 + br
            m.mm(pg[0:64, :], egate[0:32, j * 64:(j + 1) * 64], sg[0:32, qs])
            m.tt(t1[0:64, :], pg[0:64, :], rr[0:64, :], ALU.mult)
            if add_to is None:
                m.tt(dst, O_[0:64, :], t1[0:64, :], ALU.mult)
            else:
                m.tt(t2[0:64, :], O_[0:64, :], t1[0:64, :], ALU.mult)
                m.tt(dst, t2[0:64, :], add_to, ALU.add)

        for g in range(2):
            for g4 in range(NG):
                gs = slice(g4 * GW, (g4 + 1) * GW)
                for k in range(2):
                    c = 2 * g + k
                    ps_ = roped(qr[k], f"q_{c}", f"qp_{c}", g4)
                    m.ts(qT[k][:, gs], ps_.v, 1.0, ALU.mult, deps=[qr[k]])
                    if os.environ.get("NSA_STOP") == "2d":
                        raise StopIteration
                if os.environ.get("NSA_STOP") == "2e":
                    raise StopIteration
                roped(ksd, f"ksd_{g}", f"ksdp_{g}", g4)
                if os.environ.get("NSA_STOP") == "2f":
                    raise StopIteration
                roped(kwd, f"kwd_{g}", f"kwdp_{g}", g4)
                if os.environ.get("NSA_STOP") == "2g":
                    raise StopIteration
                for nm, z in ((f"kcd_{g}", zk), (f"vcd_{g}", zv)):
                    ps_ = proj(nm, g4)
                    lo = g4 * GW
                    m.copy(z[0:64, lo:lo + GW], ps_[0:64, :], "act")
                    m.copy(z[64:128, lo:lo + GW - 1], ps_[64:128, 1:GW], "act")
                    if g4 > 0:
                        m.copy(z[64:128, lo - 1:lo], ps_[64:128, 0:1], "act")
            if os.environ.get("NSA_STOP") == "2":
                return
            wv = self.load_tm(l, f"vsw_{g}")
            for t in range(16):
                ps_ = self.pb[2 + t % 2]
                for kc in range(NCH):
                    m.mm(ps_[:, 0:128], self.xb[kc][t // 4][:, (t % 4) * 128:(t % 4 + 1) * 128], wv[:, kc, 0:128], start=(kc == 0), stop=(kc == NCH - 1))
                m.copy(vsw[t].v, ps_[:, 0:128], "act")
                if os.environ.get("NSA_STOP") == "3":
                    return
            if os.environ.get("NSA_STOP") == "4a":
                raise StopIteration
            for kv, z in ((0, zk), (1, zv)):
                w1 = self.load_tm(l, f"cw1_{kv}")
                w1v = w1.v.rearrange("p k n -> p (k n)").rearrange("p (a c) -> p a c", c=256)
                w2 = self.load_fm(l, "cw2k" if kv == 0 else "cw2v")
                for cc in range(2):
                    ps_ = self.pb[cc]
                    pbz = self.pb[2]
                    ccs = slice(cc * 128, (cc + 1) * 128)
                    for lp in range(16):
                        pc_ = kv * 16 + lp
                        m.mm(pbz[:, 0:1], w1v[:, lp, ccs], cpe[:, pc_:pc_ + 1], start=(lp == 0), stop=(lp == 15))
                    m.copy(cbias[:, cc:cc + 1], pbz[:, 0:1], "dve")
                    if os.environ.get("NSA_STOP") == "4b":
                        raise StopIteration
                    for lp in range(16):
                        m.mm(ps_[:, 0:128], w1v[:, lp, ccs], z[:, 2 * lp:2 * lp + 16 * 127 + 1:16], start=(lp == 0), stop=(lp == 15))
                    xg = t2
                    m.act(xg[:, 0:128], ps_[:, 0:128], AF.Identity, bias=cbias[:, cc:cc + 1])
                    m.act(t1[:, 0:128], xg[:, 0:128], AF.Square)
                    m.ts(t1[:, 0:128], t1[:, 0:128], 0.044715, ALU.mult, 1.0, ALU.add)
                    m.tt(t1[:, 0:128], t1[:, 0:128], xg[:, 0:128], ALU.mult)
                    m.act(t1[:, 0:128], t1[:, 0:128], AF.Sigmoid, scale=1.5957691216057308)
                    m.tt(hT[cc][:, 0:128], t1[:, 0:128], xg[:, 0:128], ALU.mult)
                    if os.environ.get("NSA_STOP") == "4c":
                        raise StopIteration
                ps_ = self.pb[2]
                if kv == 0:
                    for cc in range(2):
                        m.mm(ps_[:, 0:128], w2[:, cc, :], hT[cc][:, 0:128], start=(cc == 0), stop=(cc == 1))
                    m.copy(kcmp[:, 0:128], ps_[:, 0:128], "act")
                    if os.environ.get("NSA_STOP") == "4d":
                        raise StopIteration
                else:
                    for cc in range(2):
                        m.mm(ps_[0:128, 0:64], hT[cc][:, 0:128], w2[:, cc, 0:64], start=(cc == 0), stop=(cc == 1))
                    m.copy(vcmp[0:128, :], ps_[0:128, 0:64], "act")
            if os.environ.get("NSA_STOP") == "4":
                return
            for qg in range(NG):
                qs = slice(qg * GW, (qg + 1) * GW)
                for r_ in range(4):
                    h = 4 * g + r_
                    k = r_ // 2
                    P_ = slice(64 * (r_ % 2), 64 * (r_ % 2) + 64)
                    st = self.pb[2 + u % 2]
                    u += 1
                    m.mm(st[0:128, :], kcmp[P_, 0:128], qT[k][P_, qs], start=True, stop=False)
                    m.mm(st[0:128, :], self.identb[0:128, 0:128], cmpneg[0:128, qs], start=False, stop=True)
                    m.act(P32[0:128, :], st[0:128, :], AF.Exp, scale=scale)
                    O, R = self.pb[4], self.pb[5]
                    m.mm(O[0:64, :], vcmp[0:128, :], P32[0:128, :])
                    m.mm(R.v, self.ones32[0:128, :], P32[0:128, :])
                    y = yc_[(qg * 4 + r_) % 2]
                    gate_mul(h, 0, R, O, slice(0, 128), qs, y.v)
                    m.dma(self.ya_d[h][:, qs], y.v)
                    if r_ == 0:
                        m.tt(pns[0:128, :], P32[0:128, :], rr[0:128, :], ALU.mult)
                    else:
                        m.tt(pn[0:128, :], P32[0:128, :], rr[0:128, :], ALU.mult)
                        m.tt(pns[0:128, :], pns[0:128, :], pn[0:128, :], ALU.add)
                for t4 in range(4):
                    tt_ = qg * 4 + t4
                    pi_ = self.pb[6]
                    m.mm(pi_[:, 0:32], pns[0:128, t4 * 128:(t4 + 1) * 128], ovl[0:128, :])
                    m.tt(sc.v, pi_[:, 0:32], vnf[:, tt_ * 32:(tt_ + 1) * 32], ALU.mult)
                    m.tt(sc.v, sc.v, addm[:, tt_ * 32:(tt_ + 1) * 32], ALU.add)
                    m.max8(m8.v, sc.v)
                    m.ts(sl.v, sc.v, m8[:, 7:8], ALU.is_ge)
                    m.ts(sc.v, sc.v, -0.5e9, ALU.is_gt)
                    m.tt(sl.v, sl.v, sc.v, ALU.mult)
                    m.ts(slb.v, sl.v, -NEGM, ALU.mult, NEGM, ALU.add)
                    ptr = V(self.pb[7], self.pb[7].ap.bitcast(BF16))
                    m.transpose(ptr[0:32, 0:128], slb.v, self.identb.v)
                    m.copy(selT[0:32, tt_ * 128:(tt_ + 1) * 128], ptr[0:32, 0:128], "act")
            if os.environ.get("NSA_STOP") == "5":
                return
            for r_ in range(4):
                h = 4 * g + r_
                k = r_ // 2
                P_ = slice(64 * (r_ % 2), 64 * (r_ % 2) + 64)
                for qg in range(NG):
                    qs = slice(qg * GW, (qg + 1) * GW)
                    ycm = yc_[(qg + r_) % 2]
                    m.dma(ycm.v, self.ya_d[h][:, qs])
                    y = yo[(qg + r_) % 2]
                    for br in (1, 2):
                        O, R = self.pb[4], self.pb[5]
                        kts = list(range(0, 4 * qg + 4)) if br == 1 else list(range(max(0, 4 * qg - 2), 4 * qg + 4))
                        for ii, kt in enumerate(kts):
                            st = self.pb[2 + u % 2]
                            pt = PT[u % 2]
                            u += 1
                            ks_ = slice(kt * 128, (kt + 1) * 128)
                            rel = kt - 4 * qg
                            if br == 1:
                                m.mm(st.v, ksd[P_, ks_], qr[k][P_, qs], start=True, stop=False)
                                diag = rel >= 0
                                m.mm(st.v, esel[:, ks_], selT[:, qs], start=False, stop=not diag)
                                if diag:
                                    m.mm(st.v, self.identb.v, self.causneg[:, rel * GW:(rel + 1) * GW], start=False, stop=True)
                            else:
                                m.mm(st.v, kwd[P_, ks_], qr[k][P_, qs], start=True, stop=False)
                                m.mm(st.v, self.identb.v, winneg[:, (rel + 2) * GW:(rel + 3) * GW], start=False, stop=True)
                            m.act(pt.v, st.v, AF.Exp, scale=scale)
                            vcol = slice(0, 64) if br == 1 else slice(64, 128)
                            m.mm(O[0:64, :], vsw[kt][:, vcol], pt.v, start=(ii == 0), stop=(ii == len(kts) - 1))
                            m.mm(R[0:64, :], self.onesb[:, 0:64], pt.v, start=(ii == 0), stop=(ii == len(kts) - 1))
                        if br == 1:
                            gate_mul(h, 1, R, O, slice(0, 64), qs, pn[0:64, :], add_to=ycm.v)
                        else:
                            gate_mul(h, 2, R, O, slice(0, 64), qs, y.v, add_to=pn[0:64, :])
                    m.dma(self.ya_d[h][:, qs], y.v)

    def hgrn_consts(self):
        m = self.m
        self.hng = self.ld_const("hng", [128, DEPTH])
        self.ublk = self.ld_const("ublk", [128, 128])
        lf = self.ld_const("lblfm", [128, 16])
        e = m.sb("lbe", [128, 16])
        ssum = m.sb("lbs", [128, 4])
        self.lbfm = m.sb("lbfm", [128, 16])
        self.omlfm = m.sb("omlfm", [128, 16])
        m.act(e.v, lf.v, AF.Exp)
        m.tt(ssum.v, e[:, 0:4], e[:, 4:8], ALU.add)
        m.tt(ssum.v, ssum.v, e[:, 8:12], ALU.add)
        m.tt(ssum.v, ssum.v, e[:, 12:16], ALU.add)
        m.recip(ssum.v, ssum.v)
        m.memset(self.lbfm[:, 0:4], 0.0)
        for l in range(1, DEPTH):
            m.tt(e[:, 4 * l:4 * l + 4], e[:, 4 * l:4 * l + 4], ssum.v, ALU.mult)
            m.tt(self.lbfm[:, 4 * l:4 * l + 4], self.lbfm[:, 4 * l - 4:4 * l], e[:, 4 * l:4 * l + 4], ALU.add)
        m.ts(self.omlfm.v, self.lbfm.v, -1.0, ALU.mult, 1.0, ALU.add)

    def hgrn(self, l):
        m = self.m
        self.phase()
        cv = self.carve
        E = cv("lbE", [128, DEPTH * 512])
        lbr = cv("lbr", [128, 512])
        omlr = cv("omlr", [128, 512])
        tq = [cv(f"htq{k}", [128, 512]) for k in range(3)]
        vtm = [cv(f"hv{t}", [128, 512], BF16) for t in range(16)]
        bT = [cv(f"hb{h}", [128, S]) for h in range(4)]
        qi = cv("hqi", [128, S], BF16)
        qt = cv("hqt", [128, S], BF16)
        ks = cv("hks", [128, S], BF16)
        kstm = [cv(f"hkstm{t}", [128, 128], BF16) for t in range(16)]
        eb = cv("heb", [128, 32])
        S32 = cv("hS32", [128, 128])
        Sb = [cv(f"hSb{k}", [128, 128], BF16) for k in range(2)]
        ATm = [cv(f"hATm{k}", [128, 128], BF16) for k in range(2)]
        o32 = cv("ho32", [128, 512])
        yo = [cv(f"hyo{k}", [128, 512], BF16) for k in range(2)]
        m.dma(E.v, self.din["lblrep"].v)
        m.act(E.v, E.v, AF.Exp)
        m.tt(tq[0].v, E[:, 0:512], E[:, 512:1024], ALU.add)
        m.tt(tq[0].v, tq[0].v, E[:, 1024:1536], ALU.add)
        m.tt(tq[0].v, tq[0].v, E[:, 1536:2048], ALU.add)
        m.recip(tq[0].v, tq[0].v)
        m.memset(lbr.v, 0.0)
        for j in range(1, l + 1):
            m.tt(tq[1].v, E[:, 512 * j:512 * j + 512], tq[0].v, ALU.mult)
            m.tt(lbr.v, lbr.v, tq[1].v, ALU.add)
        m.ts(omlr.v, lbr.v, -1.0, ALU.mult, 1.0, ALU.add)
        wz = self.load_tm(l, "bf_tm")
        wi = self.load_tm(l, "bi_tm")
        for t in range(16):
            g, t4 = t // 4, t % 4
            tsl = slice(t4 * 128, (t4 + 1) * 128)
            pz, pi_ = self.pb[0], self.pb[1]
            for kc in range(NCH):
                m.mm(pz.v, self.xb[kc][g][:, tsl], wz[:, kc, :], start=(kc == 0), stop=(kc == NCH - 1))
            for kc in range(NCH):
                m.mm(pi_.v, self.xb[kc][g][:, tsl], wi[:, kc, :], start=(kc == 0), stop=(kc == NCH - 1))
            m.copy(vtm[t].v, pi_.v, "act")
            f = tq[t % 2]
            m.act(f.v, pz.v, AF.Sigmoid)
            m.tt(f.v, f.v, omlr.v, ALU.mult)
            m.tt(f.v, f.v, lbr.v, ALU.add)
            m.ts(f.v, f.v, 1e-20, ALU.max)
            m.act(f.v, f.v, AF.Ln)
            for h in range(4):
                pc = self.pb[2 + h % 2]
                m.mm(pc[:, 0:128], f[:, 128 * h:128 * h + 128], self.ublk.v)
                m.copy(bT[h][:, t * 128:(t + 1) * 128], pc[:, 0:128], "pool" if False else "dve")
        for h in range(4):
            wq = self.load_fm(l, f"bq_{h}")
            wf = self.load_fm(l, f"bf_{h}")
            wg = self.load_fm(l, f"bg_{h}")
            col = 4 * l + h
            for g in range(NG):
                gs = slice(g * GW, (g + 1) * GW)
                pq, pf = self.pb[0], self.pb[1]
                for kc in range(NCH):
                    m.mm(pq.v, wq[:, kc, :], self.xb[kc][g].v, start=(kc == 0), stop=(kc == NCH - 1))
                for kc in range(NCH):
                    m.mm(pf.v, wf[:, kc, :], self.xb[kc][g].v, start=(kc == 0), stop=(kc == NCH - 1))
                b3 = bT[h][:, gs].rearrange("p (c t) -> p c t", t=64)
                d, e1, kk = tq[0], tq[1], tq[2]
                d3 = d.v.rearrange("p (c t) -> p c t", t=64)
                m.tt(d3, b3[:, :, 63:64].bc([128, 8, 64]), b3, ALU.subtract)
                m.act(e1.v, bT[h][:, gs], AF.Exp)
                m.tt(qi[:, gs], pq.v, e1.v, ALU.mult)
                m.act(e1.v, d.v, AF.Exp)
                m.act(kk.v, pf.v, AF.Sigmoid, scale=-1.0)
                m.ts(kk.v, kk.v, self.omlfm[:, col:col + 1], ALU.mult)
                m.tt(ks[:, gs], kk.v, e1.v, ALU.mult)
                m.act(e1.v, d.v, AF.Exp, scale=-1.0)
                m.tt(qt[:, gs], pq.v, e1.v, ALU.mult)
            m.act(eb.v, bT[h].v.rearrange("p (c t) -> p c t", t=64)[:, :, 63], AF.Exp)
            for t in range(16):
                pt_ = V(self.pb[2 + t % 2], self.pb[2 + t % 2].ap.bitcast(BF16))
                m.transpose(pt_[:, 0:128], ks[:, t * 128:(t + 1) * 128], self.identb.v)
                m.copy(kstm[t].v, pt_[:, 0:128], "act")
            m.memset(S32.v, 0.0)
            m.memset(Sb[0].v, 0.0)
            sbi = 0
            for t in range(16):
                g, t4 = t // 4, t % 4
                tsl = slice(t * 128, (t + 1) * 128)
                pa = self.pb[2]
                po = self.pb[3]
                m.mm(pa[:, 0:128], ks[:, tsl], qt[:, tsl])
                am = ATm[t % 2]
                m.tt(am.v, pa[:, 0:128], self.ublk.v, ALU.mult)
                m.mm(po[:, 0:128], vtm[t][:, 128 * h:128 * h + 128], am.v, start=True, stop=False)
                for c2 in range(2):
                    j = 2 * t + c2
                    rows = slice(64 * c2, 64 * c2 + 64)
                    cols = slice(t * 128 + 64 * c2, t * 128 + 64 * c2 + 64)
                    m.mm(po[:, 64 * c2:64 * c2 + 64], Sb[sbi % 2].v, qi[:, cols], start=False, stop=(c2 == 1))
                    pS = self.pb[4 + c2]
                    m.mm(pS[:, 0:128], kstm[t][rows, :], vtm[t][rows, 128 * h:128 * h + 128])
                    m.stt(S32.v, S32.v, eb[:, j:j + 1], pS[:, 0:128], ALU.mult, ALU.add)
                    sbi += 1
                    m.copy(Sb[sbi % 2].v, S32.v, "act")
                m.copy(o32[:, t4 * 128:(t4 + 1) * 128], po[:, 0:128], "act")
                if t4 == 3:
                    gs = slice(g * GW, (g + 1) * GW)
                    sq, rs = tq[0], tq[1]
                    A = self.pb[6]
                    m.act(sq.v, o32.v, AF.Square)
                    m.mm(A.v, self.ones32.v, sq.v)
                    m.ts(rs.v, A.v, 1.0 / 128.0, ALU.mult, RMS_EPS, ALU.add)
                    m.act(rs.v, rs.v, AF.Sqrt)
                    m.recip(rs.v, rs.v)
                    pg = self.pb[7]
                    for kc in range(NCH):
                        m.mm(pg.v, wg[:, kc, :], self.xb[kc][g].v, start=(kc == 0), stop=(kc == NCH - 1))
                    m.act(sq.v, pg.v, AF.Silu)
                    m.tt(rs.v, rs.v, sq.v, ALU.mult)
                    m.stt(rs.v, o32.v, self.hng[:, l:l + 1], rs.v, ALU.mult, ALU.mult)
                    y = yo[g % 2]
                    m.copy(y.v, rs.v, "act")
                    m.dma(self.yb_d[h][:, gs], y.v)

    def mla(self, l):
        m = self.m
        self.phase()
        cv = self.carve
        cqg = [cv(f"cqg{k}", [128, S], BF16) for k in range(3)]
        ckvg = [cv(f"ckvg{k}", [128, S], BF16) for k in range(2)]
        rq = cv("rq", [128, S])
        rkv = cv("rkv", [128, S])
        rkt = cv("rkt", [128, 16])
        vtm = [cv(f"vtm{t}", [128, 512], BF16) for t in range(16)]
        kpe = cv("kpe", [128, S], BF16)
        qh = [cv(f"qh{k}", [128, S], BF16) for k in range(2)]
        kh = [cv(f"kh{k}", [128, S], BF16) for k in range(2)]
        PT = [cv(f"PT{k}", [128, GW], BF16) for k in range(2)]
        rr = cv("rr", [128, GW])
        sqt = [cv(f"sqt{k}", [128, GW]) for k in range(2)]
        t1 = cv("t1", [128, GW])
        t2 = cv("t2", [128, GW])
        yo = [cv(f"yo{k}", [64, GW], BF16) for k in range(2)]
        A = self.pb[6]
        Bt = self.pb[7]
        u = 0
        for (nm, nk, dst, gtab, rdst, dim) in (("cq", 3, cqg, self.qng, rq, 384.0), ("ckv", 2, ckvg, self.kvng, rkv, 256.0)):
            for g in range(NG):
                gs = slice(g * GW, (g + 1) * GW)
                for k in range(nk):
                    w = self.load_fm(l, f"{nm}_{k}")
                    ps = self.pb[u % 2]
                    sq = sqt[u % 2]
                    u += 1
                    for kc in range(NCH):
                        m.mm(ps.v, w[:, kc, :], self.xb[kc][g].v, start=(kc == 0), stop=(kc == NCH - 1))
                    col = l * nk + k
                    m.act(dst[k][:, gs], ps.v, AF.Copy, scale=gtab[:, col:col + 1])
                    m.act(sq.v, ps.v, AF.Square)
                    m.mm(A.v, self.ones32.v, sq.v, start=(k == 0), stop=(k == nk - 1))
                if nm == "ckv":
                    for t4 in range(4):
                        tt_ = g * 4 + t4
                        for k in range(2):
                            m.mm(Bt[:, tt_:tt_ + 1], sqt[(u - 2 + k) % 2][:, t4 * 128:(t4 + 1) * 128], self.ones32[:, 0:1], start=(k == 0), stop=(k == 1))
                m.ts(t1.v, A.v, 1.0 / dim, ALU.mult, RMS_EPS, ALU.add)
                m.act(t1.v, t1.v, AF.Sqrt)
                m.recip(rdst[:, gs], t1.v)
        m.ts(rkt.v, Bt[:, 0:16], 1.0 / 256.0, ALU.mult, RMS_EPS, ALU.add)
        m.act(rkt.v, rkt.v, AF.Sqrt)
        m.recip(rkt.v, rkt.v)
        wv = self.load_tm(l, "ukv_v")
        for t in range(16):
            ps = self.pb[2 + t % 2]
            ts_ = slice(t * 128, (t + 1) * 128)
            for k in range(2):
                m.mm(ps.v, ckvg[k][:, ts_], wv[:, k, :], start=(k == 0), stop=(k == 1))
            m.ts(vtm[t].v, ps.v, rkt[:, t:t + 1], ALU.mult)
        w = self.load_fm(l, "kr96")
        wp = self.load_fm(l, "kr96p")
        R6 = slice(64, 96)
        for g in range(NG):
            gs = slice(g * GW, (g + 1) * GW)
            ps, pp = self.pb[0], self.pb[1]
            for kc in range(NCH):
                m.mm(ps.v, w[:, kc, :], self.xb[kc][g].v, start=(kc == 0), stop=(kc == NCH - 1))
            for kc in range(NCH):
                m.mm(pp.v, wp[:, kc, :], self.xb[kc][g].v, start=(kc == 0), stop=(kc == NCH - 1))
            m.tt(t1[R6, :], ps[R6, :], self.C96[R6, gs], ALU.mult)
            m.tt(t2[R6, :], pp[R6, :], self.S96[R6, gs], ALU.mult)
            m.tt(kpe[R6, gs], t1[R6, :], t2[R6, :], ALU.add)
        scale = 96.0 ** -0.5
        R96 = slice(0, 96)
        R64 = slice(0, 64)
        for h in range(8):
            wq = self.load_fm(l, f"uq_{h}")
            wqp = self.load_fm(l, f"uqp_{h}")
            wk = self.load_fm(l, f"ukn_{h}")
            q_, k_ = qh[h % 2], kh[h % 2]
            for g in range(NG):
                gs = slice(g * GW, (g + 1) * GW)
                ps, pp, pk = self.pb[0], self.pb[1], self.pb[2]
                for k in range(3):
                    m.mm(ps[R96, :], wq[:, k, 0:96], cqg[k][:, gs], start=(k == 0), stop=(k == 2))
                for k in range(3):
                    m.mm(pp[R96, :], wqp[:, k, 0:96], cqg[k][:, gs], start=(k == 0), stop=(k == 2))
                for k in range(2):
                    m.mm(pk[R64, :], wk[:, k, 0:64], ckvg[k][:, gs], start=(k == 0), stop=(k == 1))
                m.copy(t1[R64, :], ps[R64, :], "act")
                m.tt(t1[R6, :], ps[R6, :], self.C96[R6, gs], ALU.mult)
                m.tt(t2[R6, :], pp[R6, :], self.S96[R6, gs], ALU.mult)
                m.tt(t1[R6, :], t1[R6, :], t2[R6, :], ALU.add)
                m.tt(q_[R96, gs], t1[R96, :], rq[R96, gs], ALU.mult)
                m.tt(k_[R64, gs], pk[R64, :], rkv[R64, gs], ALU.mult)
                m.copy(k_[R6, gs], kpe[R6, gs], "act")
            for qg in range(NG):
                qs = slice(qg * GW, (qg + 1) * GW)
                O, R = self.pb[4], self.pb[5]
                nk = 4 * qg + 4
                for kt in range(nk):
                    st = self.pb[2 + u % 2]
                    pt = PT[u % 2]
                    u += 1
                    ks = slice(kt * 128, (kt + 1) * 128)
                    diag = kt >= 4 * qg
                    m.mm(st.v, k_[R96, ks], q_[R96, qs], start=True, stop=not diag)
                    if diag:
                        rel = kt - 4 * qg
                        m.mm(st.v, self.identb.v, self.causneg[:, rel * GW:(rel + 1) * GW], start=False, stop=True)
                    m.act(pt.v, st.v, AF.Exp, scale=scale)
                    m.mm(O[R64, :], vtm[kt][:, 64 * h:64 * h + 64], pt.v, start=(kt == 0), stop=(kt == nk - 1))
                    m.mm(R[R64, :], self.onesb[:, 0:64], pt.v, start=(kt == 0), stop=(kt == nk - 1))
                m.recip(rr[R64, :], R[R64, :])
                y = yo[(h * NG + qg) % 2]
                m.tt(y.v, O[R64, :], rr[R64, :], ALU.mult)
                m.dma(self.yc_d[h][:, qs], y.v)

    def phase(self):
        m = self.m
        self.aoff = 0
        bar = {}
        for e in ("pe", "act", "dve", "pool"):
            if m.cnt[e] > 0:
                bar[m.cursem[e]] = m.cnt[e]
        for qn in ("sp", "pool"):
            n = m.dcnt[qn]
            k = len(m.dsem[qn])
            for j, sid in enumerate(m.dsem[qn]):
                cntj = (n - j + k - 1) // k if n > j else 0
                if cntj > 0:
                    bar[sid] = 16 * cntj
        self.barrier = bar

    def carve(self, name, shape, dtype=F32):
        esz = 4 if dtype in (F32, I32) else 2
        free = 1
        for d_ in shape[1:]:
            free *= d_
        n32 = (free * esz + 3) // 4
        n32 = (n32 + 7) // 8 * 8
        assert self.aoff + n32 <= self.AW, (name, self.aoff, n32)
        ap = self.arena.ap[:, self.aoff:self.aoff + n32]
        self.aoff += n32
        if dtype != F32:
            ap = ap.bitcast(dtype)
        ap = ap[:, 0:free]
        if len(shape) == 3:
            ap = ap.rearrange("p (a b) -> p a b", a=shape[1])
        if shape[0] < 128:
            ap = ap[0:shape[0]]
        b = Buf(name, ap)
        b.r = dict(self.barrier)
        return b

    def load_fm(self, l, name, dst=None):
        m = self.m
        if dst is None:
            dst = self.wslot[self.wslot_i % self.nslot]
            self.wslot_i += 1
        tid = l * self.nt[0] + self.idx[name]
        m.dma(dst.v.rearrange("p k n -> p (k n)"), self.din["wfm"][tid], "pool")
        return dst

    def load_tm(self, l, name):
        m = self.m
        dst = self.tslot[self.tslot_i % 2]
        self.tslot_i += 1
        tid = l * self.nt[1] + self.idx[name]
        m.dma(dst.v.rearrange("p k n -> p (k n)"), self.din["wtm"][tid], "pool")
        return dst

    def resid_ln(self, l, i, g, delta_fn, final=False):
        m = self.m
        gs = slice(g * GW, (g + 1) * GW)
        A, B = self.pb[6], self.pb[7]
        for c in range(NCH):
            xs = self.xstage[self.cnt_stage % 2]
            self.cnt_stage += 1
            m.dma(xs.v, self.xres[c * 128:(c + 1) * 128, gs])
            ps = delta_fn(c)
            y = self.ybuf[c]
            m.stt(y.v, xs.v, ALPHA, ps, ALU.mult, ALU.add)
            sq = self.sq[c % 2]
            m.act(sq.v, y.v, AF.Square)
            m.mm(A.v, self.ones32.v, y.v, start=(c == 0), stop=(c == NCH - 1))
            m.mm(B.v, self.ones32.v, sq.v, start=(c == 0), stop=(c == NCH - 1))
        mean, rstd = self.mean, self.rstd
        t0 = self.tmpa[0]
        m.act(mean.v, A.v, AF.Copy, scale=1.0 / D)
        m.tt(t0.v, mean.v, mean.v, ALU.mult)
        m.stt(t0.v, B.v, 1.0 / D, t0.v, ALU.mult, ALU.subtract)
        m.ts(t0.v, t0.v, LN_EPS, ALU.add)
        m.act(t0.v, t0.v, AF.Sqrt)
        m.recip(rstd.v, t0.v)
        for c in range(NCH):
            y = self.ybuf[c]
            t = self.tmpa[c % 2]
            m.tt(t.v, y.v, mean.v, ALU.subtract)
            m.tt(t.v, t.v, rstd.v, ALU.mult)
            xn = self.xn[c % 2]
            col = (l * 4 + i) * NCH + c
            m.act(xn.v, t.v, AF.Identity, scale=self.lng[:, col:col + 1], bias=self.lnb[:, col:col + 1])
            if final:
                ev = m.dma(self.out[c * 128:(c + 1) * 128, gs], xn.v)
                self.final_events.append(ev)
            else:
                m.dma(self.xres[c * 128:(c + 1) * 128, gs], xn.v)
                m.copy(self.xb[c][g].v, xn.v, "act")

    def init_stream(self):
        m = self.m
        for c in range(NCH):
            for g in range(NG):
                gs = slice(g * GW, (g + 1) * GW)
                xs = self.xstage[self.cnt_stage % 2]
                self.cnt_stage += 1
                m.dma(xs.v, self.din["xT"][c * 128:(c + 1) * 128, gs])
                m.dma(self.xres[c * 128:(c + 1) * 128, gs], xs.v)
                m.copy(self.xb[c][g].v, xs.v, "act")

    def ffn(self, l, i, lni, final=False):
        m = self.m
        self.phase()
        w2res = [self.carve(f"w2res{c}", [128, NJ, 128], BF16) for c in range(NCH)]
        hT = [[self.carve(f"hT{gi}_{j}", [128, GW], BF16) for j in range(NJ)] for gi in range(2)]
        silu = [self.carve(f"silu{k}", [128, GW]) for k in range(2)]
        for c in range(NCH):
            tid = (l * 2 + i) * NCH + c
            m.dma(w2res[c].v.rearrange("p j n -> p (j n)"), self.din["w2"][tid], "pool")
        k = 0
        for half in range(2):
            for j in range(NJ):
                w1 = self.load_fm(l, f"w1_{i}_{j}")
                w3 = self.load_fm(l, f"w3_{i}_{j}")
                for gi in range(2):
                    g = 2 * half + gi
                    p1 = self.pb[(k % 2) * 2]
                    p3 = self.pb[(k % 2) * 2 + 1]
                    for kc in range(NCH):
                        m.mm(p1.v, w1[:, kc, :], self.xb[kc][g].v, start=(kc == 0), stop=(kc == NCH - 1))
                    for kc in range(NCH):
                        m.mm(p3.v, w3[:, kc, :], self.xb[kc][g].v, start=(kc == 0), stop=(kc == NCH - 1))
                    sl = silu[k % 2]
                    k += 1
                    m.act(sl.v, p1.v, AF.Silu)
                    m.stt(hT[gi][j].v, sl.v, 0.5, p3.v, ALU.mult, ALU.mult)
            for gi in range(2):
                g = 2 * half + gi

                def delta(c, gi=gi):
                    ps = self.pb[4 + (c % 2)]
                    for j in range(NJ):
                        m.mm(ps.v, w2res[c][:, j, :], hT[gi][j].v, start=(j == 0), stop=(j == NJ - 1))
                    return ps.v
                self.resid_ln(l, lni, g, delta, final=final)

    def merge(self, l):
        m = self.m
        self.phase()
        wres = [self.carve(f"woutres{c}", [128, 8, 128], BF16) for c in range(NCH)]
        ya = [self.carve(f"ya{h}", [64, GW], BF16) for h in range(8)]
        yb = [self.carve(f"yb{h}", [128, GW], BF16) for h in range(4)]
        yc = [self.carve(f"yc{h}", [64, GW], BF16) for h in range(8)]
        sgt = [self.carve(f"sgt{k}", [128, GW]) for k in range(2)]
        macc = self.carve("macc", [128, GW])
        mtmp = self.carve("mtmp", [128, GW])
        mixed = [self.carve(f"mixed{c}", [128, GW], BF16) for c in range(NCH)]
        for c in range(NCH):
            self.load_fm(l, f"wout_{c}", wres[c])
        k = 0
        for g in range(NG):
            gs = slice(g * GW, (g + 1) * GW)
            for h in range(8):
                m.dma(ya[h].v, self.ya_d[h][:, gs])
                m.dma(yc[h].v, self.yc_d[h][:, gs])
            for h in range(4):
                m.dma(yb[h].v, self.yb_d[h][:, gs])
            for c in range(NCH):
                for b in range(3):
                    wm = self.load_fm(l, f"merge_{b}_{c}")
                    wb = self.load_fm(l, f"wbr_{b}_{c}")
                    psm = self.pb[(k % 2) * 2]
                    psy = self.pb[(k % 2) * 2 + 1]
                    for kc in range(NCH):
                        m.mm(psm.v, wm[:, kc, :], self.xb[kc][g].v, start=(kc == 0), stop=(kc == NCH - 1))
                    if b == 1:
                        for h in range(4):
                            m.mm(psy.v, wb[:, h, :], yb[h].v, start=(h == 0), stop=(h == 3))
                    else:
                        ys = ya if b == 0 else yc
                        for h in range(8):
                            m.mm(psy.v, wb[0:64, h, :], ys[h].v, start=(h == 0), stop=(h == 7))
                    sg = sgt[k % 2]
                    k += 1
                    m.act(sg.v, psm.v, AF.Sigmoid)
                    if b == 0:
                        m.tt(macc.v, sg.v, psy.v, ALU.mult)
                    else:
                        m.tt(mtmp.v, sg.v, psy.v, ALU.mult)
                        m.tt(macc.v, macc.v, mtmp.v, ALU.add)
                m.copy(mixed[c].v, macc.v, "act")

            def delta(c):
                ps = self.pb[4 + (c % 2)]
                for kc in range(NCH):
                    m.mm(ps.v, wres[c][:, kc, :], mixed[kc].v, start=(kc == 0), stop=(kc == NCH - 1))
                return ps.v
            self.resid_ln(l, 1, g, delta)

    def xattn(self, l):
        m = self.m
        self.phase()
        wres = [self.carve(f"wores{c}", [128, 8, 128], BF16) for c in range(NCH)]
        kT = [self.carve(f"xkT{c}", [128, 256], BF16) for c in range(NCH)]
        vtm = [self.carve(f"xv{k}", [128, 1024], BF16) for k in range(2)]
        qT = [self.carve(f"xq{c}", [128, GW], BF16) for c in range(NCH)]
        at = [self.carve(f"xat{c}", [128, GW], BF16) for c in range(NCH)]
        PT = [self.carve(f"xPT{k}", [128, GW], BF16) for k in range(2)]
        rr = self.carve("xrr", [128, GW])
        self.memT = self.carve("memT", [128, 8, 256], BF16)
        m.dma(self.memT.v.rearrange("p k n -> p (k n)"), self.din["memT"].v, "pool")
        for c in range(NCH):
            self.load_fm(l, f"xo_{c}", wres[c])
        for c in range(NCH):
            wk = self.load_fm(l, f"xk_{c}")
            ps = self.pb[c % 2]
            for kc in range(NCH):
                m.mm(ps[:, 0:256], wk[:, kc, :], self.memT[:, kc, :], start=(kc == 0), stop=(kc == NCH - 1))
            m.copy(kT[c].v, ps[:, 0:256], "act")
        for hv in range(2):
            wv = self.load_tm(l, f"xv_{hv}")
            for kt in range(2):
                ps = self.pb[2 + kt]
                for kc in range(NCH):
                    m.mm(ps.v, self.memT[:, kc, kt * 128:(kt + 1) * 128], wv[:, kc, :], start=(kc == 0), stop=(kc == NCH - 1))
                m.copy(vtm[kt][:, hv * 512:(hv + 1) * 512], ps.v, "act")
        u = 0
        for g in range(NG):
            for c in range(NCH):
                wq = self.load_fm(l, f"xq_{c}")
                ps = self.pb[c % 2]
                for kc in range(NCH):
                    m.mm(ps.v, wq[:, kc, :], self.xb[kc][g].v, start=(kc == 0), stop=(kc == NCH - 1))
                m.copy(qT[c].v, ps.v, "act")
            for h in range(4):
                O0, O1, R = self.pb[2], self.pb[3], self.pb[5]
                for kt in range(2):
                    st = self.pb[u % 2]
                    pt = PT[u % 2]
                    u += 1
                    ks = slice(kt * 128, (kt + 1) * 128)
                    m.mm(st.v, kT[2 * h][:, ks], qT[2 * h].v, start=True, stop=False)
                    m.mm(st.v, kT[2 * h + 1][:, ks], qT[2 * h + 1].v, start=False, stop=True)
                    m.act(pt.v, st.v, AF.Exp, scale=1.0 / 16.0)
                    m.mm(O0.v, vtm[kt][:, 256 * h:256 * h + 128], pt.v, start=(kt == 0), stop=(kt == 1))
                    m.mm(O1.v, vtm[kt][:, 256 * h + 128:256 * h + 256], pt.v, start=(kt == 0), stop=(kt == 1))
                    m.mm(R.v, self.onesb.v, pt.v, start=(kt == 0), stop=(kt == 1))
                m.recip(rr.v, R.v)
                m.tt(at[2 * h].v, O0.v, rr.v, ALU.mult)
                m.tt(at[2 * h + 1].v, O1.v, rr.v, ALU.mult)

            def delta(c):
                ps = self.pb[4] if c % 2 == 0 else self.pb[1]
                for kc in range(NCH):
                    m.mm(ps.v, wres[c][:, kc, :], at[kc].v, start=(kc == 0), stop=(kc == NCH - 1))
                return ps.v
            self.resid_ln(l, 2, g, delta)

    def mixers(self, l):
        if self.ext_y is True:
            return
        if self.ext_y == "nsa" or (not self.ext_y and NSA_ENABLED):
            self.nsa(l)
        elif not self.ext_y and l == 0:
            z = self.tmpa[0]
            self.m.memset(z.v, 0.0)
            zb = self.xn[0]
            for h in range(8):
                for g in range(NG):
                    self.m.dma(self.ya_d[h][:, g * GW:(g + 1) * GW], V(z, z.ap.bitcast(BF16))[0:64, 0:GW])
        if not self.ext_y or self.ext_y == "hgrn":
            self.hgrn(l)
        if not self.ext_y or self.ext_y == "mla":
            self.mla(l)

    def dump_stream(self):
        m = self.m
        for c in range(NCH):
            for g in range(NG):
                gs = slice(g * GW, (g + 1) * GW)
                xs = self.xstage[self.cnt_stage % 2]
                self.cnt_stage += 1
                m.dma(xs.v, self.xres[c * 128:(c + 1) * 128, gs])
                ev = m.dma(self.out[c * 128:(c + 1) * 128, gs], xs.v)
                self.final_events.append(ev)

    def build(self):
        self.init_stream()
        if self.stop == "nsa_only":
            try:
                self.nsa(0)
            except StopIteration:
                pass
            self.final_events.append(self.ya_d.w[0] if self.ya_d.w else None)
            self.m.emit([e for e in self.final_events if e])
            return self.nc
        for l in range(self.n_layers):
            last = (l == DEPTH - 1)
            self.ffn(l, 0, 0)
            if self.stop == "ffn1":
                self.dump_stream()
                break
            self.mixers(l)
            if self.stop == "mixers":
                self.dump_stream()
                break
            self.merge(l)
            if self.stop == "mix":
                self.dump_stream()
                break
            self.xattn(l)
            if self.stop == "xa":
                self.dump_stream()
                break
            self.ffn(l, 1, 3, final=last)
            if self.stop == "full" and l == self.n_layers - 1 and not last:
                self.dump_stream()
        self.m.emit(self.final_events)
        return self.nc


def kernel(**inputs):
    shared, per_core, idx, nt = prep_inputs(inputs)
    shapes = {k: v.shape for k, v in shared.items()}
    shapes.update({k: v.shape for k, v in per_core[0].items()})
    prog = Prog(shapes, idx, nt)
    nc = prog.build()
    in_maps = [dict(shared, **pc) for pc in per_core]
    res = run_bass_kernel_spmd(nc, in_maps, core_ids=list(range(len(per_core))))
    out = np.stack([np.ascontiguousarray(r["out"].T) for r in res.results], axis=0)
    return out.astype(np.float32)
```

```python
import contextlib
import os
import numpy as np
import concourse.bass as bass
import concourse.mybir as mybir
from concourse.bass_utils import run_bass_kernel_spmd

F32 = mybir.dt.float32
BF16 = mybir.dt.bfloat16
I32 = mybir.dt.int32
AF = mybir.ActivationFunctionType
ALU = mybir.AluOpType

ENGS = ("pe", "act", "dve", "pool", "sp")
EPOCH = 60000


class V:
    __slots__ = ("buf", "ap")

    def __init__(self, buf, ap):
        self.buf = buf
        self.ap = ap

    def __getitem__(self, k):
        return V(self.buf, self.ap[k])

    def rearrange(self, s, **kw):
        return V(self.buf, self.ap.rearrange(s, **kw))

    def bc(self, shape):
        return V(self.buf, self.ap.to_broadcast(shape))


class Buf:
    def __init__(self, name, ap):
        self.name = name
        self.ap = ap
        self.w = []
        self.r = {}
        self.wpe = False

    def __getitem__(self, k):
        return V(self, self.ap[k])

    @property
    def v(self):
        return V(self, self.ap)


class MK:
    def __init__(self, nc, n_dma_sems=8):
        self.nc = nc
        self.es = contextlib.ExitStack()
        self.q = {e: [] for e in ENGS}
        self.cnt = {e: 0 for e in ENGS}
        self.cursem = {}
        self.waited = {e: {} for e in ENGS}
        self.sems = {}
        self.nsem = 0
        self.pesems = set()
        for e in ("pe", "act", "dve", "pool"):
            self.cursem[e] = self._newsem(e)
        self.pesems.add(self.cursem["pe"])
        self.dsem = {}
        self.dcnt = {}
        for qn in ("sp", "pool"):
            self.dsem[qn] = [self._newsem("d" + qn) for _ in range(n_dma_sems)]
            self.dcnt[qn] = 0
        self.nbuf = 0

    def _newsem(self, tag):
        sid = self.nsem
        self.nsem += 1
        h = self.es.enter_context(self.nc.semaphore(f"s{sid}_{tag}"))
        self.sems[sid] = h
        return sid

    def sb(self, name, shape, dtype=F32):
        self.nbuf += 1
        t = self.nc.alloc_sbuf_tensor(f"{name}_{self.nbuf}", list(shape), dtype)
        return Buf(name, t.ap())

    def ps(self, name, shape, dtype=F32):
        self.nbuf += 1
        t = self.nc.alloc_psum_tensor(f"{name}_{self.nbuf}", list(shape), dtype)
        return Buf(name, t.ap())

    def dram(self, name, shape, dtype=F32, kind="Internal"):
        t = self.nc.dram_tensor(name, list(shape), dtype, kind=kind)
        return Buf(name, t.ap())

    def sub(self, buf, key, name=None):
        return Buf(name or buf.name, buf.ap[key])

    def alias(self, name, ap, olds):
        b = Buf(name, ap)
        for o in olds:
            for ev in o.w:
                b.r[ev[0]] = max(b.r.get(ev[0], 0), ev[1])
            for s, v in o.r.items():
                b.r[s] = max(b.r.get(s, 0), v)
        return b

    def _collect(self, eng, reads, writes, pe_acc=False):
        need = {}

        def add(s, v):
            if need.get(s, 0) < v:
                need[s] = v
        for b in reads:
            for ev in b.w:
                add(*ev)
        for b in writes:
            if not (pe_acc and b.wpe and eng == "pe"):
                for ev in b.w:
                    add(*ev)
            for s, v in b.r.items():
                add(s, v)
        wl = []
        wd = self.waited[eng]
        for s, v in need.items():
            if eng == "pe" and s in self.pesems:
                continue
            if wd.get(s, 0) >= v:
                continue
            wd[s] = v
            wl.append((s, v))
        return wl

    def _record(self, ev, reads, writes, eng):
        for b in writes:
            b.w = [ev]
            b.r = {}
            b.wpe = (eng == "pe")
        for b in reads:
            if b in writes:
                continue
            if b.r.get(ev[0], 0) < ev[1]:
                b.r[ev[0]] = ev[1]

    def op(self, eng, fn, reads, writes, pe_acc=False):
        reads = [x.buf if isinstance(x, V) else x for x in reads]
        writes = [x.buf if isinstance(x, V) else x for x in writes]
        wl = self._collect(eng, reads, writes, pe_acc)
        if self.cnt[eng] >= EPOCH:
            self.cursem[eng] = self._newsem(eng)
            if eng == "pe":
                self.pesems.add(self.cursem[eng])
            self.cnt[eng] = 0
        self.cnt[eng] += 1
        ev = (self.cursem[eng], self.cnt[eng])
        self.q[eng].append((wl, fn, ev[0], 1))
        self._record(ev, reads, writes, eng)
        return ev

    def dma(self, out, in_, queue="sp", **kw):
        eng = queue
        reads = [in_.buf]
        writes = [out.buf]
        i = self.dcnt[queue]
        self.dcnt[queue] += 1
        sems = self.dsem[queue]
        sid = sems[i % len(sems)]
        val = 16 * (i // len(sems) + 1)
        wl = self._collect(eng, reads, writes)
        if val > 16:
            wd = self.waited[eng]
            if wd.get(sid, 0) < val - 16:
                wd[sid] = val - 16
                wl.append((sid, val - 16))
        oap, iap = out.ap, in_.ap
        self.q[eng].append((wl, lambda e: e.dma_start(out=oap, in_=iap, **kw), sid, 16))
        ev = (sid, val)
        self._record(ev, reads, writes, eng)
        return ev

    @staticmethod
    def _a(x):
        return x.ap if isinstance(x, V) else x

    def mm(self, out, lhsT, rhs, start=True, stop=True):
        o, l, r = out.ap, lhsT.ap, rhs.ap
        return self.op("pe", lambda e: e.matmul(o, l, r, start=start, stop=stop),
                       [lhsT, rhs], [out], pe_acc=not start)

    def transpose(self, out, in_, ident):
        o, i, d = out.ap, in_.ap, ident.ap
        return self.op("pe", lambda e: e.transpose(o, i, d), [in_, ident], [out])

    def act(self, out, in_, func, bias=None, scale=None, accum_out=None):
        kw = {}
        reads = [in_]
        writes = [out]
        if bias is not None:
            kw["bias"] = self._a(bias)
            if isinstance(bias, V):
                reads.append(bias)
        if scale is not None:
            kw["scale"] = self._a(scale)
            if isinstance(scale, V):
                reads.append(scale)
        if accum_out is not None:
            kw["accum_out"] = accum_out.ap
            writes.append(accum_out)
        o, i = out.ap, in_.ap
        return self.op("act", lambda e: e.activation(o, i, func, **kw), reads, writes)

    def tt(self, out, in0, in1, op, eng="dve", deps=()):
        o, a, b = out.ap, in0.ap, in1.ap
        return self.op(eng, lambda e: e.tensor_tensor(o, a, b, op), [in0, in1] + list(deps), [out])

    def ts(self, out, in0, s1, op0, s2=None, op1=None, eng="dve", deps=()):
        reads = [in0] + [s for s in (s1, s2) if isinstance(s, V)] + list(deps)
        o, a = out.ap, in0.ap
        a1, a2 = self._a(s1), self._a(s2)
        kw = {}
        if op1 is not None:
            kw["op1"] = op1
        return self.op(eng, lambda e: e.tensor_scalar(o, a, a1, a2, op0, **kw), reads, [out])

    def stt(self, out, in0, scalar, in1, op0, op1, eng="dve"):
        reads = [in0, in1] + ([scalar] if isinstance(scalar, V) else [])
        o, a, b, s = out.ap, in0.ap, in1.ap, self._a(scalar)
        return self.op(eng, lambda e: e.scalar_tensor_tensor(o, a, s, b, op0, op1), reads, [out])

    def copy(self, out, in_, eng="dve"):
        o, i = out.ap, in_.ap
        if eng == "act":
            return self.op(eng, lambda e: e.copy(o, i), [in_], [out])
        return self.op(eng, lambda e: e.tensor_copy(o, i), [in_], [out])

    def memset(self, out, val, eng="dve"):
        o = out.ap
        return self.op(eng, lambda e: e.memset(o, val), [], [out])

    def recip(self, out, in_):
        o, i = out.ap, in_.ap
        return self.op("dve", lambda e: e.reciprocal(o, i), [in_], [out])

    def max8(self, out, in_):
        o, i = out.ap, in_.ap
        return self.op("dve", lambda e: e.max(o, i), [in_], [out])

    def emit(self, final_events):
        nc = self.nc
        sems = self.sems
        q = self.q

        def run(e, items, extra=None):
            for wl, fn, sid, inc in items:
                for s, v in wl:
                    e.wait_ge(sems[s], v)
                ins = fn(e)
                if inc:
                    ins.then_inc(sems[sid], inc)
            if extra:
                for s, v in extra:
                    e.wait_ge(sems[s], v)

        with nc.Block() as block:
            @block.tensor
            def _(e):
                run(e, q["pe"])

            @block.vector
            def _(e):
                run(e, q["dve"])

            @block.scalar
            def _(e):
                run(e, q["act"])

            @block.gpsimd
            def _(e):
                run(e, q["pool"])

            @block.sync
            def _(e):
                run(e, q["sp"], list(final_events))


D = 1024
S = 2048
DEPTH = 4
NCH = 8
NG = 4
GW = 512
DFF = 2816
NJ = 22
ALPHA = (2.0 * DEPTH) ** 0.25
LN_EPS = 1e-5
RMS_EPS = 1e-6
N_IN = 7096
O_Q, O_KC, O_VC, O_KS, O_VS, O_KW, O_VW, O_GATE = 0, 512, 640, 768, 896, 1024, 1152, 1280
O_BQ, O_BF, O_BI, O_BG = 1304, 1816, 2328, 2840
O_CQ, O_CKV, O_KR, O_MERGE = 3352, 3736, 3992, 4024
NEGM = -30000.0
NSA_ENABLED = True


def _fm_tile(W, cols=None):
    K = W.shape[0]
    if cols is None:
        cols = np.arange(W.shape[1])
    cols = np.asarray(cols, dtype=np.int64)
    t = np.zeros((1024, 128), np.float32)
    ok = cols >= 0
    idxs = np.nonzero(ok)[0]
    t[:K, idxs] = W[:, cols[ok]]
    return t.reshape(8, 128, 128).transpose(1, 0, 2).reshape(128, 1024)


def _tm_tile(W, cols):
    K = W.shape[0]
    cols = np.asarray(cols, dtype=np.int64)
    t = np.zeros((1024, 512), np.float32)
    t[:K, :len(cols)] = W[:, cols]
    return t.reshape(8, 128, 512).transpose(1, 0, 2).reshape(128, 4096)


def _rows64_tile(W, c):
    t = np.zeros((128, 8, 128), np.float32)
    t[:64] = W[:, c * 128:(c + 1) * 128].reshape(8, 64, 128).transpose(1, 0, 2)
    return t.reshape(128, 1024)


def layer_tiles(inp, l):
    r = np.arange
    fm, tm = [], []
    win = inp["w_in"][l]
    for i in range(2):
        for j in range(NJ):
            fm.append((f"w1_{i}_{j}", _fm_tile(inp["ffn_w1"][l, i], r(128 * j, 128 * j + 128))))
            fm.append((f"w3_{i}_{j}", _fm_tile(inp["ffn_w3"][l, i], r(128 * j, 128 * j + 128))))
    for b in range(3):
        for c in range(NCH):
            fm.append((f"merge_{b}_{c}", _fm_tile(win, O_MERGE + 1024 * b + 128 * c + r(128))))
    for c in range(NCH):
        fm.append((f"wbr_0_{c}", _rows64_tile(inp["w_branch"][l, 0], c)))
        fm.append((f"wbr_1_{c}", _fm_tile(inp["w_branch"][l, 1], 128 * c + r(128))))
        fm.append((f"wbr_2_{c}", _rows64_tile(inp["w_branch"][l, 2], c)))
        fm.append((f"wout_{c}", _fm_tile(inp["w_out"][l], 128 * c + r(128))))
        fm.append((f"xq_{c}", _fm_tile(inp["xa_wq"][l], 128 * c + r(128))))
        fm.append((f"xk_{c}", _fm_tile(inp["xa_wk"][l], 128 * c + r(128))))
        fm.append((f"xo_{c}", _fm_tile(inp["xa_wo"][l], 128 * c + r(128))))
    for hv in range(2):
        tm.append((f"xv_{hv}", _tm_tile(inp["xa_wv"][l], 512 * hv + r(512))))
    EXTRA_TILES(inp, l, fm, tm)
    return fm, tm


def EXTRA_TILES(inp, l, fm, tm):
    r = np.arange
    win = inp["w_in"][l]
    for k in range(3):
        fm.append((f"cq_{k}", _fm_tile(win, O_CQ + 128 * k + r(128))))
    for k in range(2):
        fm.append((f"ckv_{k}", _fm_tile(win, O_CKV + 128 * k + r(128))))
    pad64 = -np.ones(64, np.int64)
    fm.append(("kr96", _fm_tile(win, np.concatenate([pad64, O_KR + r(32)]))))
    fm.append(("kr96p", _fm_tile(win, np.concatenate([pad64, O_KR + 16 + r(16), O_KR + r(16)]))))
    uq = inp["mla_w_uq"][l]
    ukv = inp["mla_w_ukv"][l]
    for h in range(8):
        fm.append((f"uq_{h}", _fm_tile(uq, 96 * h + r(96))))
        fm.append((f"uqp_{h}", _fm_tile(uq, np.concatenate([pad64, 96 * h + 80 + r(16), 96 * h + 64 + r(16)]))))
        fm.append((f"ukn_{h}", _fm_tile(ukv, 128 * h + r(64))))
    def ropeperm(base, n):
        cols = -np.ones(n, np.int64)
        for j in range(n):
            d = j % 64
            if d < 8:
                cols[j] = base + j + 8
            elif d < 16:
                cols[j] = base + j - 8
        return cols
    for c in range(4):
        fm.append((f"q_{c}", _fm_tile(win, O_Q + 128 * c + r(128))))
        fm.append((f"qp_{c}", _fm_tile(win, ropeperm(O_Q + 128 * c, 128))))
    for g in range(2):
        dup = np.concatenate([64 * g + r(64), 64 * g + r(64)])
        pd = np.concatenate([ropeperm(64 * g, 64), ropeperm(64 * g, 64)])
        fm.append((f"kcd_{g}", _fm_tile(win, O_KC + dup)))
        fm.append((f"vcd_{g}", _fm_tile(win, O_VC + dup)))
        fm.append((f"ksd_{g}", _fm_tile(win, O_KS + dup)))
        fm.append((f"ksdp_{g}", _fm_tile(win, np.where(pd >= 0, O_KS + pd, -1))))
        fm.append((f"kwd_{g}", _fm_tile(win, O_KW + dup)))
        fm.append((f"kwdp_{g}", _fm_tile(win, np.where(pd >= 0, O_KW + pd, -1))))
        tm.append((f"vsw_{g}", _tm_tile(win, np.concatenate([O_VS + 64 * g + r(64), O_VW + 64 * g + r(64)]))))
    fm.append(("gate", _fm_tile(win, O_GATE + r(24))))
    for kv in range(2):
        w1 = inp["nsa_cmp_w1"][l, kv]
        t = w1.reshape(16, 128, 256).transpose(1, 0, 2).reshape(128, 4096)
        tm.append((f"cw1_{kv}", np.ascontiguousarray(t)))
    w2 = inp["nsa_cmp_w2"][l]
    fm.append(("cw2k", _fm_tile(w2[0], np.concatenate([r(64), r(64)]))))
    fm.append(("cw2v", _fm_tile(w2[1], r(64))))
    for h in range(4):
        fm.append((f"bq_{h}", _fm_tile(win, O_BQ + 128 * h + r(128))))
        fm.append((f"bf_{h}", _fm_tile(win, O_BF + 128 * h + r(128))))
        fm.append((f"bg_{h}", _fm_tile(win, O_BG + 128 * h + r(128))))
    tm.append(("bf_tm", _tm_tile(win, O_BF + r(512))))
    tm.append(("bi_tm", _tm_tile(win, O_BI + r(512))))
    tm.append(("ukv_v", _tm_tile(ukv, np.concatenate([128 * h + 64 + r(64) for h in range(8)]))))


def prep_inputs(inputs):
    f = lambda k: np.asarray(inputs[k], dtype=np.float32)
    inp = {k: f(k) for k in inputs if k != "positions"}
    shared = {}
    fms, tms = [], []
    for l in range(DEPTH):
        fm, tm = layer_tiles(inp, l)
        if l == 0:
            idx = {n: i for i, (n, _) in enumerate(fm)}
            idx.update({n: i for i, (n, _) in enumerate(tm)})
            nt = (len(fm), len(tm))
        fms.extend(a for _, a in fm)
        tms.extend(a for _, a in tm)
    shared["wfm"] = np.stack(fms)
    shared["wtm"] = np.stack(tms)
    w2 = inp["ffn_w2"].reshape(DEPTH, 2, NJ, 128, NCH, 128).transpose(0, 1, 4, 3, 2, 5)
    shared["w2"] = np.ascontiguousarray(w2).reshape(DEPTH * 2 * NCH, 128, NJ * 128)
    shared["lng"] = np.ascontiguousarray(inp["ln_g"].reshape(DEPTH, 4, NCH, 128).transpose(3, 0, 1, 2)).reshape(128, DEPTH * 4 * NCH)
    shared["lnb"] = np.ascontiguousarray(inp["ln_b"].reshape(DEPTH, 4, NCH, 128).transpose(3, 0, 1, 2)).reshape(128, DEPTH * 4 * NCH)
    shared["ones32"] = np.ones((128, 128), np.float32)
    EXTRA_SHARED(inp, shared)
    x = inp["x"]
    mem = inp["mem"]
    per_core = []
    for b in range(x.shape[0]):
        per_core.append({"xT": np.ascontiguousarray(x[b].T),
                         "posr": np.ascontiguousarray(np.broadcast_to(np.asarray(inputs["positions"])[b].astype(np.int32)[None, :], (128, S))),
                         "memT": np.ascontiguousarray(mem[b].T.reshape(8, 128, 256).transpose(1, 0, 2)).reshape(128, 2048)})
    return shared, per_core, idx, nt


def EXTRA_SHARED(inp, shared):
    shared["ident"] = np.eye(128, dtype=np.float32)
    k = np.arange(128)[:, None]
    q = np.arange(512)[None, :]
    shared["causneg"] = np.concatenate([np.where(128 * rel + k <= q, 0.0, NEGM) for rel in range(4)], axis=1).astype(np.float32)
    inv96 = np.zeros((128, 1), np.float32)
    sgn96 = np.zeros((128, 1), np.float32)
    fr = (500000.0 ** (-np.arange(16, dtype=np.float32) / 16)).astype(np.float32)
    inv96[64:80, 0] = fr
    inv96[80:96, 0] = fr
    sgn96[64:80, 0] = -1.0
    sgn96[80:96, 0] = 1.0
    shared["ropec"] = np.concatenate([inv96, sgn96], axis=1)
    n = np.arange(128)[:, None]
    t = np.arange(S)[None, :]
    shared["cmpneg"] = np.where((16 * n + 31 <= t) & (n < 127), 0.0, NEGM).astype(np.float32)
    kk_ = np.arange(128)[:, None]
    qq_ = np.arange(512)[None, :]
    wn = []
    for rel in range(-2, 4):
        kp = 128 * rel + kk_
        wn.append(np.where((kp <= qq_) & (kp > qq_ - 256), 0.0, NEGM))
    shared["winneg"] = np.concatenate(wn, axis=1).astype(np.float32)
    es = np.zeros((128, 16 * 128), np.float32)
    for kt in range(16):
        for k_ in range(128):
            es[2 * kt + k_ // 64, kt * 128 + k_] = 1.0
    shared["esel"] = es
    eg = np.zeros((128, 24 * 64), np.float32)
    for j in range(24):
        eg[j, j * 64:(j + 1) * 64] = 1.0
    shared["egate"] = eg
    start = np.arange(127) * 16
    jn = np.arange(32)
    ovl = np.clip(np.minimum(start[:, None] + 32, (jn[None, :] + 1) * 64) - np.maximum(start[:, None], jn[None, :] * 64), 0, None) / 32.0
    o128 = np.zeros((128, 32), np.float32)
    o128[:127] = ovl
    shared["ovl"] = o128
    tt_ = np.arange(S)[:, None]
    valid = jn[None, :] * 64 <= tt_
    cur = tt_ // 64
    forced = (jn[None, :] == 0) | (jn[None, :] == cur) | (jn[None, :] == cur - 1)
    vnf = (valid & ~forced).astype(np.float32)
    addm = np.where(valid & forced, 1e9, np.where(valid, 0.0, -1e9)).astype(np.float32)
    shared["vnf"] = np.ascontiguousarray(vnf.reshape(16, 128, 32).transpose(1, 0, 2)).reshape(128, 512)
    shared["addm"] = np.ascontiguousarray(addm.reshape(16, 128, 32).transpose(1, 0, 2)).reshape(128, 512)
    invn = np.zeros((128, 2), np.float32)
    frn = (500000.0 ** (-np.arange(8, dtype=np.float32) / 8)).astype(np.float32)
    for base in (0, 64):
        invn[base:base + 8, 0] = frn
        invn[base + 8:base + 16, 0] = frn
        invn[base:base + 8, 1] = -1.0
        invn[base + 8:base + 16, 1] = 1.0
    shared["ropen"] = invn
    pe = inp["nsa_cmp_pos"]
    shared["cpe"] = np.ascontiguousarray(pe.reshape(DEPTH, 2, 16, 128).transpose(3, 0, 1, 2)).reshape(128, DEPTH * 2 * 16)
    lbl = inp["hgrn_lb_logits"]
    shared["lblfm"] = np.ascontiguousarray(lbl.reshape(DEPTH, 4, 128).transpose(2, 0, 1)).reshape(128, 16)
    shared["lblrep"] = np.ascontiguousarray(np.broadcast_to(lbl.reshape(1, DEPTH * 512), (128, DEPTH * 512)))
    shared["hng"] = np.ascontiguousarray(inp["hgrn_norm_g"].T)
    ii = np.arange(128)
    shared["ublk"] = ((ii[:, None] <= ii[None, :]) & (ii[:, None] // 64 == ii[None, :] // 64)).astype(np.float32)
    shared["qng"] = np.ascontiguousarray(inp["mla_q_norm_g"].reshape(DEPTH, 3, 128).transpose(2, 0, 1)).reshape(128, DEPTH * 3)
    shared["kvng"] = np.ascontiguousarray(inp["mla_kv_norm_g"].reshape(DEPTH, 2, 128).transpose(2, 0, 1)).reshape(128, DEPTH * 2)


class Prog:
    AW = 23808

    def __init__(self, shapes, idx, nt, n_layers=DEPTH, stop=None, ext_y=False):
        self.idx = idx
        self.nt = nt
        self.n_layers = n_layers
        self.stop = stop
        self.ext_y = ext_y
        nc = bass.Bass("TRN2", target_bir_lowering=False)
        self.nc = nc
        m = MK(nc)
        self.m = m
        self.din = {k: m.dram(k, list(s), I32 if k == "posr" else F32, kind="ExternalInput") for k, s in shapes.items()}
        self.out = m.dram("out", [D, S], F32, kind="ExternalOutput")
        self.xres = m.dram("xres", [D, S], F32)
        yk = lambda nm: "ExternalInput" if (ext_y is True or (ext_y and ext_y != nm)) else ("ExternalOutput" if ext_y == nm else "Internal")
        self.ya_d = m.dram("ya_d", [8, 64, S], BF16, kind=yk("nsa"))
        self.yb_d = m.dram("yb_d", [4, 128, S], BF16, kind=yk("hgrn"))
        self.yc_d = m.dram("yc_d", [8, 64, S], BF16, kind=yk("mla"))
        self.xb_t = m.sb("xb", [128, NCH, S], BF16)
        self.xb = [[m.sub(self.xb_t, (slice(None), c, slice(g * GW, (g + 1) * GW)), f"xb{c}_{g}") for g in range(NG)] for c in range(NCH)]
        self.ones32 = m.sb("ones32", [128, 128])
        self.onesb = m.sb("onesb", [128, 128], BF16)
        self.lng = m.sb("lng", [128, DEPTH * 4 * NCH])
        self.lnb = m.sb("lnb", [128, DEPTH * 4 * NCH])
        m.dma(self.ones32.v, self.din["ones32"].v)
        m.dma(self.lng.v, self.din["lng"].v)
        m.dma(self.lnb.v, self.din["lnb"].v)
        m.dma(self.onesb.v, self.din["ones32"].v, "pool")
        self.pb = [m.ps(f"pb{i}", [128, GW]) for i in range(8)]
        self.nslot = 4
        self.wslot = [m.sb(f"wslot{i}", [128, 8, 128], BF16) for i in range(self.nslot)]
        self.wslot_i = 0
        self.tslot = [m.sb(f"tslot{i}", [128, 8, 512], BF16) for i in range(2)]
        self.tslot_i = 0
        self.xstage = [m.sb(f"xstage{i}", [128, GW]) for i in range(2)]
        self.ybuf = [m.sb(f"y{c}", [128, GW]) for c in range(NCH)]
        self.sq = [m.sb("sq0", [128, GW])] * 2
        self.mean = m.sb("mean", [128, GW])
        self.rstd = m.sb("rstd", [128, GW])
        self.tmpa = [m.sb(f"tmpa{i}", [128, GW]) for i in range(2)]
        self.xn = [m.sb(f"xn{i}", [128, GW]) for i in range(2)]
        self.arena = m.sb("arena", [128, self.AW])
        self.aoff = 0
        self.barrier = {}
        self.cnt_stage = 0
        self.final_events = []
        self.setup_consts()

    def ld_const(self, name, shape, dtype=F32, key=None):
        m = self.m
        b = m.sb(name, shape, dtype)
        m.dma(b.v, self.din[key or name].v, "pool" if dtype != F32 else "sp")
        return b

    def rope_tables(self, inv_sgn, rows, name):
        m = self.m
        C = m.sb(name + "C", [128, S], BF16)
        Sn = m.sb(name + "S", [128, S], BF16)
        TWO_PI = 2.0 * np.pi
        for r0, r1 in rows:
            for g in range(NG):
                gs = slice(g * GW, (g + 1) * GW)
                pi_ = self.tmpa[0]
                pf = self.tmpa[1]
                m.dma(V(pi_, pi_.ap.bitcast(I32))[r0:r1, :], self.din["posr"][r0:r1, gs])
                for which, dst in ((0, Sn), (1, C)):
                    ang = self.xn[0]
                    kk = self.xn[1]
                    m.copy(pf[r0:r1, :], V(pi_, pi_.ap.bitcast(I32))[r0:r1, :])
                    m.ts(ang[r0:r1, :], pf[r0:r1, :], inv_sgn[r0:r1, 0:1], ALU.mult, (np.pi / 2 if which else 0.0), ALU.add)
                    m.ts(kk[r0:r1, :], ang[r0:r1, :], 1.0 / TWO_PI, ALU.mult)
                    ki = self.sq[0]
                    m.copy(V(ki, ki.ap.bitcast(I32))[r0:r1, :], kk[r0:r1, :])
                    m.copy(kk[r0:r1, :], V(ki, ki.ap.bitcast(I32))[r0:r1, :])
                    m.stt(ang[r0:r1, :], kk[r0:r1, :], -TWO_PI, ang[r0:r1, :], ALU.mult, ALU.add)
                    m.ts(kk[r0:r1, :], ang[r0:r1, :], np.pi, ALU.is_gt)
                    m.stt(ang[r0:r1, :], kk[r0:r1, :], -TWO_PI, ang[r0:r1, :], ALU.mult, ALU.add)
                    m.ts(kk[r0:r1, :], ang[r0:r1, :], -np.pi, ALU.is_lt)
                    m.stt(ang[r0:r1, :], kk[r0:r1, :], TWO_PI, ang[r0:r1, :], ALU.mult, ALU.add)
                    m.act(kk[r0:r1, :], ang[r0:r1, :], AF.Sin)
                    if which == 0:
                        m.ts(dst[r0:r1, gs], kk[r0:r1, :], inv_sgn[r0:r1, 1:2], ALU.mult)
                    else:
                        m.copy(dst[r0:r1, gs], kk[r0:r1, :])
        return C, Sn

    def setup_consts(self):
        m = self.m
        self.identb = self.ld_const("identb", [128, 128], BF16, "ident")
        self.causneg = self.ld_const("causneg", [128, 4 * GW], BF16)
        self.ropec96 = self.ld_const("ropec", [128, 2])
        self.qng = self.ld_const("qng", [128, DEPTH * 3])
        self.kvng = self.ld_const("kvng", [128, DEPTH * 2])
        if not self.ext_y or self.ext_y == "mla":
            self.C96, self.S96 = self.rope_tables(self.ropec96, [(64, 96)], "r96")
        if not self.ext_y or self.ext_y == "hgrn":
            self.hgrn_consts()
        if self.ext_y == "nsa" or (not self.ext_y and NSA_ENABLED):
            self.ropen = self.ld_const("ropen", [128, 2])
            self.Cn, self.Sn = self.rope_tables(self.ropen, [(0, 16), (64, 80)], "rn")

    def nsa(self, l):
        m = self.m
        self.phase()
        cv = self.carve
        scale = 64.0 ** -0.5

        def ldc(name, shape, dtype=F32):
            b = cv(name, shape, dtype)
            m.dma(b.v, self.din[name].v, "pool" if dtype != F32 else "sp")
            return b
        cmpneg = ldc("cmpneg", [128, S], BF16)
        winneg = ldc("winneg", [128, 6 * GW], BF16)
        esel = ldc("esel", [128, 16 * 128], BF16)
        egate = ldc("egate", [128, 24 * 64], BF16)
        ovl = ldc("ovl", [128, 32])
        vnf = ldc("vnf", [128, 512])
        addm = ldc("addm", [128, 512])
        cpe = cv("cpe", [128, 32], BF16)
        m.dma(cpe.v, self.din["cpe"][:, l * 32:(l + 1) * 32], "pool")
        sg = cv("sg", [128, S], BF16)
        cbias = cv("cbias", [128, 2])
        slb = cv("nslb", [128, 32], BF16)
        PT = [cv(f"nPT{k}", [128, GW], BF16) for k in range(2)]
        P32 = cv("nP32", [128, GW])
        pn = cv("npn", [128, GW])
        pns = cv("npns", [128, GW])
        rr = cv("nrr", [128, GW])
        t1 = cv("nt1", [128, GW])
        t2 = cv("nt2", [128, GW])
        sc = cv("nsc", [128, 32])
        sl = cv("nsl", [128, 32])
        m8 = cv("nm8", [128, 8])
        hT = [cv(f"nhT{k}", [128, 128], BF16) for k in range(2)]
        kcmp = cv("nkcmp", [128, 128], BF16)
        vcmp = cv("nvcmp", [128, 64])
        yo = [cv(f"nyo{k}", [64, GW], BF16) for k in range(2)]
        yc_ = [cv(f"nyc{k}", [64, GW], BF16) for k in range(2)]
        qT = [cv(f"nq{k}", [128, S], BF16) for k in range(2)]
        qr = [cv(f"nqr{k}", [128, S], BF16) for k in range(2)]
        ksd = cv("nksd", [128, S], BF16)
        kwd = cv("nkwd", [128, S], BF16)
        zk = cv("nzk", [128, S + 16], BF16)
        zv = cv("nzv", [128, S + 16], BF16)
        vsw = [cv(f"nvsw{t}", [128, 128], BF16) for t in range(16)]
        selT = cv("nselT", [128, S], BF16)
        Cn, Sn = self.Cn, self.Sn
        m.memset(zk.v, 0.0)
        m.memset(zv.v, 0.0)
        m.memset(hT[0].v, 0.0)
        m.memset(selT.v, 0.0)
        m.memset(hT[1].v, 0.0)
        ROPE_ROWS = (slice(0, 16), slice(64, 80))
        u = 0
        w = self.load_fm(l, "gate")
        for g4 in range(NG):
            gs = slice(g4 * GW, (g4 + 1) * GW)
            ps = self.pb[g4 % 2]
            for kc in range(NCH):
                m.mm(ps.v, w[:, kc, :], self.xb[kc][g4].v, start=(kc == 0), stop=(kc == NCH - 1))
            m.act(sg[0:32, gs], ps[0:32, :], AF.Sigmoid)
        if os.environ.get("NSA_STOP") == "1":
            return

        def proj(name, g4):
            w_ = self.load_fm(l, name)
            nonlocal u
            ps_ = self.pb[u % 2]
            u += 1
            for kc in range(NCH):
                m.mm(ps_.v, w_[:, kc, :], self.xb[kc][g4].v, start=(kc == 0), stop=(kc == NCH - 1))
            return ps_

        def roped(dst, name, pname, g4):
            gs = slice(g4 * GW, (g4 + 1) * GW)
            ps_ = proj(name, g4)
            if os.environ.get("NSA_STOP") == "2a":
                raise StopIteration
            pp_ = proj(pname, g4)
            if os.environ.get("NSA_STOP") == "2b":
                raise StopIteration
            m.copy(dst[:, gs], ps_.v, "act")
            if os.environ.get("NSA_STOP") == "2c":
                raise StopIteration
            for rws in (ROPE_ROWS if os.environ.get("NSA_ROPE", "1") == "1" else ()):
                m.tt(t1[rws, :], ps_[rws, :], Cn[rws, gs], ALU.mult, deps=[dst])
                m.tt(t2[rws, :], pp_[rws, :], Sn[rws, gs], ALU.mult)
                m.tt(dst[rws, gs], t1[rws, :], t2[rws, :], ALU.add)
            return ps_

        def gate_mul(h, br, R_, O_, rows, qs, dst, add_to=None):
            m.ts(rr[rows, :], R_[rows, :], 1e-30, ALU.max)
            m.recip(rr[rows, :], rr[rows, :])
            pg = self.pb[7]
            j = h * 3 + br
            m.mm(pg[0:64, :], egate[0:32, j * 64:(j + 1) * 64], sg[0:32, qs])
            m.tt(t1[0:64, :], pg[0:64, :], rr[0:64, :], ALU.mult)
            if add_to is None:
                m.tt(dst, O_[0:64, :], t1[0:64, :], ALU.mult)
            else:
                m.tt(t2[0:64, :], O_[0:64, :], t1[0:64, :], ALU.mult)
                m.tt(dst, t2[0:64, :], add_to, ALU.add)

        for g in range(2):
            for g4 in range(NG):
                gs = slice(g4 * GW, (g4 + 1) * GW)
                for k in range(2):
                    c = 2 * g + k
                    ps_ = roped(qr[k], f"q_{c}", f"qp_{c}", g4)
                    m.ts(qT[k][:, gs], ps_.v, 1.0, ALU.mult, deps=[qr[k]])
                    if os.environ.get("NSA_STOP") == "2d":
                        raise StopIteration
                if os.environ.get("NSA_STOP") == "2e":
                    raise StopIteration
                roped(ksd, f"ksd_{g}", f"ksdp_{g}", g4)
                if os.environ.get("NSA_STOP") == "2f":
                    raise StopIteration
                roped(kwd, f"kwd_{g}", f"kwdp_{g}", g4)
                if os.environ.get("NSA_STOP") == "2g":
                    raise StopIteration
                for nm, z in ((f"kcd_{g}", zk), (f"vcd_{g}", zv)):
                    ps_ = proj(nm, g4)
                    lo = g4 * GW
                    m.copy(z[0:64, lo:lo + GW], ps_[0:64, :], "act")
                    m.copy(z[64:128, lo:lo + GW - 1], ps_[64:128, 1:GW], "act")
                    if g4 > 0:
                        m.copy(z[64:128, lo - 1:lo], ps_[64:128, 0:1], "act")
            if os.environ.get("NSA_STOP") == "2":
                return
            wv = self.load_tm(l, f"vsw_{g}")
            for t in range(16):
                ps_ = self.pb[2 + t % 2]
                for kc in range(NCH):
                    m.mm(ps_[:, 0:128], self.xb[kc][t // 4][:, (t % 4) * 128:(t % 4 + 1) * 128], wv[:, kc, 0:128], start=(kc == 0), stop=(kc == NCH - 1))
                m.copy(vsw[t].v, ps_[:, 0:128], "act")
                if os.environ.get("NSA_STOP") == "3":
                    return
            if os.environ.get("NSA_STOP") == "4a":
                raise StopIteration
            for kv, z in ((0, zk), (1, zv)):
                w1 = self.load_tm(l, f"cw1_{kv}")
                w1v = w1.v.rearrange("p k n -> p (k n)").rearrange("p (a c) -> p a c", c=256)
                w2 = self.load_fm(l, "cw2k" if kv == 0 else "cw2v")
                for cc in range(2):
                    ps_ = self.pb[cc]
                    pbz = self.pb[2]
                    ccs = slice(cc * 128, (cc + 1) * 128)
                    for lp in range(16):
                        pc_ = kv * 16 + lp
                        m.mm(pbz[:, 0:1], w1v[:, lp, ccs], cpe[:, pc_:pc_ + 1], start=(lp == 0), stop=(lp == 15))
                    m.copy(cbias[:, cc:cc + 1], pbz[:, 0:1], "dve")
                    if os.environ.get("NSA_STOP") == "4b":
                        raise StopIteration
                    for lp in range(16):
                        m.mm(ps_[:, 0:128], w1v[:, lp, ccs], z[:, 2 * lp:2 * lp + 16 * 127 + 1:16], start=(lp == 0), stop=(lp == 15))
                    xg = t2
                    m.act(xg[:, 0:128], ps_[:, 0:128], AF.Identity, bias=cbias[:, cc:cc + 1])
                    m.act(t1[:, 0:128], xg[:, 0:128], AF.Square)
                    m.ts(t1[:, 0:128], t1[:, 0:128], 0.044715, ALU.mult, 1.0, ALU.add)
                    m.tt(t1[:, 0:128], t1[:, 0:128], xg[:, 0:128], ALU.mult)
                    m.act(t1[:, 0:128], t1[:, 0:128], AF.Sigmoid, scale=1.5957691216057308)
                    m.tt(hT[cc][:, 0:128], t1[:, 0:128], xg[:, 0:128], ALU.mult)
                    if os.environ.get("NSA_STOP") == "4c":
                        raise StopIteration
                ps_ = self.pb[2]
                if kv == 0:
                    for cc in range(2):
                        m.mm(ps_[:, 0:128], w2[:, cc, :], hT[cc][:, 0:128], start=(cc == 0), stop=(cc == 1))
                    m.copy(kcmp[:, 0:128], ps_[:, 0:128], "act")
                    if os.environ.get("NSA_STOP") == "4d":
                        raise StopIteration
                else:
                    for cc in range(2):
                        m.mm(ps_[0:128, 0:64], hT[cc][:, 0:128], w2[:, cc, 0:64], start=(cc == 0), stop=(cc == 1))
                    m.copy(vcmp[0:128, :], ps_[0:128, 0:64], "act")
            if os.environ.get("NSA_STOP") == "4":
                return
            for qg in range(NG):
                qs = slice(qg * GW, (qg + 1) * GW)
                for r_ in range(4):
                    h = 4 * g + r_
                    k = r_ // 2
                    P_ = slice(64 * (r_ % 2), 64 * (r_ % 2) + 64)
                    st = self.pb[2 + u % 2]
                    u += 1
                    m.mm(st[0:128, :], kcmp[P_, 0:128], qT[k][P_, qs], start=True, stop=False)
                    m.mm(st[0:128, :], self.identb[0:128, 0:128], cmpneg[0:128, qs], start=False, stop=True)
                    m.act(P32[0:128, :], st[0:128, :], AF.Exp, scale=scale)
                    O, R = self.pb[4], self.pb[5]
                    m.mm(O[0:64, :], vcmp[0:128, :], P32[0:128, :])
                    m.mm(R.v, self.ones32[0:128, :], P32[0:128, :])
                    y = yc_[(qg * 4 + r_) % 2]
                    gate_mul(h, 0, R, O, slice(0, 128), qs, y.v)
                    m.dma(self.ya_d[h][:, qs], y.v)
                    if r_ == 0:
                        m.tt(pns[0:128, :], P32[0:128, :], rr[0:128, :], ALU.mult)
                    else:
                        m.tt(pn[0:128, :], P32[0:128, :], rr[0:128, :], ALU.mult)
                        m.tt(pns[0:128, :], pns[0:128, :], pn[0:128, :], ALU.add)
                for t4 in range(4):
                    tt_ = qg * 4 + t4
                    pi_ = self.pb[6]
                    m.mm(pi_[:, 0:32], pns[0:128, t4 * 128:(t4 + 1) * 128], ovl[0:128, :])
                    m.tt(sc.v, pi_[:, 0:32], vnf[:, tt_ * 32:(tt_ + 1) * 32], ALU.mult)
                    m.tt(sc.v, sc.v, addm[:, tt_ * 32:(tt_ + 1) * 32], ALU.add)
                    m.max8(m8.v, sc.v)
                    m.ts(sl.v, sc.v, m8[:, 7:8], ALU.is_ge)
                    m.ts(sc.v, sc.v, -0.5e9, ALU.is_gt)
                    m.tt(sl.v, sl.v, sc.v, ALU.mult)
                    m.ts(slb.v, sl.v, -NEGM, ALU.mult, NEGM, ALU.add)
                    ptr = V(self.pb[7], self.pb[7].ap.bitcast(BF16))
                    m.transpose(ptr[0:32, 0:128], slb.v, self.identb.v)
                    m.copy(selT[0:32, tt_ * 128:(tt_ + 1) * 128], ptr[0:32, 0:128], "act")
            if os.environ.get("NSA_STOP") == "5":
                return
            for r_ in range(4):
                h = 4 * g + r_
                k = r_ // 2
                P_ = slice(64 * (r_ % 2), 64 * (r_ % 2) + 64)
                for qg in range(NG):
                    qs = slice(qg * GW, (qg + 1) * GW)
                    ycm = yc_[(qg + r_) % 2]
                    m.dma(ycm.v, self.ya_d[h][:, qs])
                    y = yo[(qg + r_) % 2]
                    for br in (1, 2):
                        O, R = self.pb[4], self.pb[5]
                        kts = list(range(0, 4 * qg + 4)) if br == 1 else list(range(max(0, 4 * qg - 2), 4 * qg + 4))
                        for ii, kt in enumerate(kts):
                            st = self.pb[2 + u % 2]
                            pt = PT[u % 2]
                            u += 1
                            ks_ = slice(kt * 128, (kt + 1) * 128)
                            rel = kt - 4 * qg
                            if br == 1:
                                m.mm(st.v, ksd[P_, ks_], qr[k][P_, qs], start=True, stop=False)
                                diag = rel >= 0
                                m.mm(st.v, esel[:, ks_], selT[:, qs], start=False, stop=not diag)
                                if diag:
                                    m.mm(st.v, self.identb.v, self.causneg[:, rel * GW:(rel + 1) * GW], start=False, stop=True)
                            else:
                                m.mm(st.v, kwd[P_, ks_], qr[k][P_, qs], start=True, stop=False)
                                m.mm(st.v, self.identb.v, winneg[:, (rel + 2) * GW:(rel + 3) * GW], start=False, stop=True)
                            m.act(pt.v, st.v, AF.Exp, scale=scale)
                            vcol = slice(0, 64) if br == 1 else slice(64, 128)
                            m.mm(O[0:64, :], vsw[kt][:, vcol], pt.v, start=(ii == 0), stop=(ii == len(kts) - 1))
                            m.mm(R[0:64, :], self.onesb[:, 0:64], pt.v, start=(ii == 0), stop=(ii == len(kts) - 1))
                        if br == 1:
                            gate_mul(h, 1, R, O, slice(0, 64), qs, pn[0:64, :], add_to=ycm.v)
                        else:
                            gate_mul(h, 2, R, O, slice(0, 64), qs, y.v, add_to=pn[0:64, :])
                    m.dma(self.ya_d[h][:, qs], y.v)

    def hgrn_consts(self):
        m = self.m
        self.hng = self.ld_const("hng", [128, DEPTH])
        self.ublk = self.ld_const("ublk", [128, 128])
        lf = self.ld_const("lblfm", [128, 16])
        e = m.sb("lbe", [128, 16])
        ssum = m.sb("lbs", [128, 4])
        self.lbfm = m.sb("lbfm", [128, 16])
        self.omlfm = m.sb("omlfm", [128, 16])
        m.act(e.v, lf.v, AF.Exp)
        m.tt(ssum.v, e[:, 0:4], e[:, 4:8], ALU.add)
        m.tt(ssum.v, ssum.v, e[:, 8:12], ALU.add)
        m.tt(ssum.v, ssum.v, e[:, 12:16], ALU.add)
        m.recip(ssum.v, ssum.v)
        m.memset(self.lbfm[:, 0:4], 0.0)
        for l in range(1, DEPTH):
            m.tt(e[:, 4 * l:4 * l + 4], e[:, 4 * l:4 * l + 4], ssum.v, ALU.mult)
            m.tt(self.lbfm[:, 4 * l:4 * l + 4], self.lbfm[:, 4 * l - 4:4 * l], e[:, 4 * l:4 * l + 4], ALU.add)
        m.ts(self.omlfm.v, self.lbfm.v, -1.0, ALU.mult, 1.0, ALU.add)

    def hgrn(self, l):
        m = self.m
        self.phase()
        cv = self.carve
        E = cv("lbE", [128, DEPTH * 512])
        lbr = cv("lbr", [128, 512])
        omlr = cv("omlr", [128, 512])
        tq = [cv(f"htq{k}", [128, 512]) for k in range(3)]
        vtm = [cv(f"hv{t}", [128, 512], BF16) for t in range(16)]
        bT = [cv(f"hb{h}", [128, S]) for h in range(4)]
        qi = cv("hqi", [128, S], BF16)
        qt = cv("hqt", [128, S], BF16)
        ks = cv("hks", [128, S], BF16)
        kstm = [cv(f"hkstm{t}", [128, 128], BF16) for t in range(16)]
        eb = cv("heb", [128, 32])
        S32 = cv("hS32", [128, 128])
        Sb = [cv(f"hSb{k}", [128, 128], BF16) for k in range(2)]
        ATm = [cv(f"hATm{k}", [128, 128], BF16) for k in range(2)]
        o32 = cv("ho32", [128, 512])
        yo = [cv(f"hyo{k}", [128, 512], BF16) for k in range(2)]
        m.dma(E.v, self.din["lblrep"].v)
        m.act(E.v, E.v, AF.Exp)
        m.tt(tq[0].v, E[:, 0:512], E[:, 512:1024], ALU.add)
        m.tt(tq[0].v, tq[0].v, E[:, 1024:1536], ALU.add)
        m.tt(tq[0].v, tq[0].v, E[:, 1536:2048], ALU.add)
        m.recip(tq[0].v, tq[0].v)
        m.memset(lbr.v, 0.0)
        for j in range(1, l + 1):
            m.tt(tq[1].v, E[:, 512 * j:512 * j + 512], tq[0].v, ALU.mult)
            m.tt(lbr.v, lbr.v, tq[1].v, ALU.add)
        m.ts(omlr.v, lbr.v, -1.0, ALU.mult, 1.0, ALU.add)
        wz = self.load_tm(l, "bf_tm")
        wi = self.load_tm(l, "bi_tm")
        for t in range(16):
            g, t4 = t // 4, t % 4
            tsl = slice(t4 * 128, (t4 + 1) * 128)
            pz, pi_ = self.pb[0], self.pb[1]
            for kc in range(NCH):
                m.mm(pz.v, self.xb[kc][g][:, tsl], wz[:, kc, :], start=(kc == 0), stop=(kc == NCH - 1))
            for kc in range(NCH):
                m.mm(pi_.v, self.xb[kc][g][:, tsl], wi[:, kc, :], start=(kc == 0), stop=(kc == NCH - 1))
            m.copy(vtm[t].v, pi_.v, "act")
            f = tq[t % 2]
            m.act(f.v, pz.v, AF.Sigmoid)
            m.tt(f.v, f.v, omlr.v, ALU.mult)
            m.tt(f.v, f.v, lbr.v, ALU.add)
            m.ts(f.v, f.v, 1e-20, ALU.max)
            m.act(f.v, f.v, AF.Ln)
            for h in range(4):
                pc = self.pb[2 + h % 2]
                m.mm(pc[:, 0:128], f[:, 128 * h:128 * h + 128], self.ublk.v)
                m.copy(bT[h][:, t * 128:(t + 1) * 128], pc[:, 0:128], "pool" if False else "dve")
        for h in range(4):
            wq = self.load_fm(l, f"bq_{h}")
            wf = self.load_fm(l, f"bf_{h}")
            wg = self.load_fm(l, f"bg_{h}")
            col = 4 * l + h
            for g in range(NG):
                gs = slice(g * GW, (g + 1) * GW)
                pq, pf = self.pb[0], self.pb[1]
                for kc in range(NCH):
                    m.mm(pq.v, wq[:, kc, :], self.xb[kc][g].v, start=(kc == 0), stop=(kc == NCH - 1))
                for kc in range(NCH):
                    m.mm(pf.v, wf[:, kc, :], self.xb[kc][g].v, start=(kc == 0), stop=(kc == NCH - 1))
                b3 = bT[h][:, gs].rearrange("p (c t) -> p c t", t=64)
                d, e1, kk = tq[0], tq[1], tq[2]
                d3 = d.v.rearrange("p (c t) -> p c t", t=64)
                m.tt(d3, b3[:, :, 63:64].bc([128, 8, 64]), b3, ALU.subtract)
                m.act(e1.v, bT[h][:, gs], AF.Exp)
                m.tt(qi[:, gs], pq.v, e1.v, ALU.mult)
                m.act(e1.v, d.v, AF.Exp)
                m.act(kk.v, pf.v, AF.Sigmoid, scale=-1.0)
                m.ts(kk.v, kk.v, self.omlfm[:, col:col + 1], ALU.mult)
                m.tt(ks[:, gs], kk.v, e1.v, ALU.mult)
                m.act(e1.v, d.v, AF.Exp, scale=-1.0)
                m.tt(qt[:, gs], pq.v, e1.v, ALU.mult)
            m.act(eb.v, bT[h].v.rearrange("p (c t) -> p c t", t=64)[:, :, 63], AF.Exp)
            for t in range(16):
                pt_ = V(self.pb[2 + t % 2], self.pb[2 + t % 2].ap.bitcast(BF16))
                m.transpose(pt_[:, 0:128], ks[:, t * 128:(t + 1) * 128], self.identb.v)
                m.copy(kstm[t].v, pt_[:, 0:128], "act")
            m.memset(S32.v, 0.0)
            m.memset(Sb[0].v, 0.0)
            sbi = 0
            for t in range(16):
                g, t4 = t // 4, t % 4
                tsl = slice(t * 128, (t + 1) * 128)
                pa = self.pb[2]
                po = self.pb[3]
                m.mm(pa[:, 0:128], ks[:, tsl], qt[:, tsl])
                am = ATm[t % 2]
                m.tt(am.v, pa[:, 0:128], self.ublk.v, ALU.mult)
                m.mm(po[:, 0:128], vtm[t][:, 128 * h:128 * h + 128], am.v, start=True, stop=False)
                for c2 in range(2):
                    j = 2 * t + c2
                    rows = slice(64 * c2, 64 * c2 + 64)
                    cols = slice(t * 128 + 64 * c2, t * 128 + 64 * c2 + 64)
                    m.mm(po[:, 64 * c2:64 * c2 + 64], Sb[sbi % 2].v, qi[:, cols], start=False, stop=(c2 == 1))
                    pS = self.pb[4 + c2]
                    m.mm(pS[:, 0:128], kstm[t][rows, :], vtm[t][rows, 128 * h:128 * h + 128])
                    sbi += 1
                    m.stt(Sb[sbi % 2].v, S32.v, eb[:, j:j + 1], pS[:, 0:128], ALU.mult, ALU.add)
                    m.stt(S32.v, S32.v, eb[:, j:j + 1], pS[:, 0:128], ALU.mult, ALU.add)
                m.copy(o32[:, t4 * 128:(t4 + 1) * 128], po[:, 0:128], "act")
                if t4 == 3:
                    gs = slice(g * GW, (g + 1) * GW)
                    sq, rs = tq[0], tq[1]
                    A = self.pb[6]
                    m.act(sq.v, o32.v, AF.Square)
                    m.mm(A.v, self.ones32.v, sq.v)
                    m.ts(rs.v, A.v, 1.0 / 128.0, ALU.mult, RMS_EPS, ALU.add)
                    m.act(rs.v, rs.v, AF.Sqrt)
                    m.recip(rs.v, rs.v)
                    pg = self.pb[7]
                    for kc in range(NCH):
                        m.mm(pg.v, wg[:, kc, :], self.xb[kc][g].v, start=(kc == 0), stop=(kc == NCH - 1))
                    m.act(sq.v, pg.v, AF.Silu)
                    m.tt(rs.v, rs.v, sq.v, ALU.mult)
                    m.stt(rs.v, o32.v, self.hng[:, l:l + 1], rs.v, ALU.mult, ALU.mult)
                    y = yo[g % 2]
                    m.copy(y.v, rs.v, "act")
                    m.dma(self.yb_d[h][:, gs], y.v)

    def mla(self, l):
        m = self.m
        self.phase()
        cv = self.carve
        cqg = [cv(f"cqg{k}", [128, S], BF16) for k in range(3)]
        ckvg = [cv(f"ckvg{k}", [128, S], BF16) for k in range(2)]
        rq = cv("rq", [128, S])
        rkv = cv("rkv", [128, S])
        rkt = cv("rkt", [128, 16])
        vtm = [cv(f"vtm{t}", [128, 512], BF16) for t in range(16)]
        kpe = cv("kpe", [128, S], BF16)
        qh = [cv(f"qh{k}", [128, S], BF16) for k in range(2)]
        kh = [cv(f"kh{k}", [128, S], BF16) for k in range(2)]
        PT = [cv(f"PT{k}", [128, GW], BF16) for k in range(2)]
        rr = cv("rr", [128, GW])
        sqt = [cv(f"sqt{k}", [128, GW]) for k in range(2)]
        t1 = cv("t1", [128, GW])
        t2 = cv("t2", [128, GW])
        yo = [cv(f"yo{k}", [64, GW], BF16) for k in range(2)]
        A = self.pb[6]
        Bt = self.pb[7]
        u = 0
        for (nm, nk, dst, gtab, rdst, dim) in (("cq", 3, cqg, self.qng, rq, 384.0), ("ckv", 2, ckvg, self.kvng, rkv, 256.0)):
            for g in range(NG):
                gs = slice(g * GW, (g + 1) * GW)
                for k in range(nk):
                    w = self.load_fm(l, f"{nm}_{k}")
                    ps = self.pb[u % 2]
                    sq = sqt[u % 2]
                    u += 1
                    for kc in range(NCH):
                        m.mm(ps.v, w[:, kc, :], self.xb[kc][g].v, start=(kc == 0), stop=(kc == NCH - 1))
                    col = l * nk + k
                    m.act(dst[k][:, gs], ps.v, AF.Copy, scale=gtab[:, col:col + 1])
                    m.act(sq.v, ps.v, AF.Square)
                    m.mm(A.v, self.ones32.v, sq.v, start=(k == 0), stop=(k == nk - 1))
                if nm == "ckv":
                    for t4 in range(4):
                        tt_ = g * 4 + t4
                        for k in range(2):
                            m.mm(Bt[:, tt_:tt_ + 1], sqt[(u - 2 + k) % 2][:, t4 * 128:(t4 + 1) * 128], self.ones32[:, 0:1], start=(k == 0), stop=(k == 1))
                m.ts(t1.v, A.v, 1.0 / dim, ALU.mult, RMS_EPS, ALU.add)
                m.act(t1.v, t1.v, AF.Sqrt)
                m.recip(rdst[:, gs], t1.v)
        m.ts(rkt.v, Bt[:, 0:16], 1.0 / 256.0, ALU.mult, RMS_EPS, ALU.add)
        m.act(rkt.v, rkt.v, AF.Sqrt)
        m.recip(rkt.v, rkt.v)
        wv = self.load_tm(l, "ukv_v")
        for t in range(16):
            ps = self.pb[2 + t % 2]
            ts_ = slice(t * 128, (t + 1) * 128)
            for k in range(2):
                m.mm(ps.v, ckvg[k][:, ts_], wv[:, k, :], start=(k == 0), stop=(k == 1))
            m.ts(vtm[t].v, ps.v, rkt[:, t:t + 1], ALU.mult)
        w = self.load_fm(l, "kr96")
        wp = self.load_fm(l, "kr96p")
        R6 = slice(64, 96)
        for g in range(NG):
            gs = slice(g * GW, (g + 1) * GW)
            ps, pp = self.pb[0], self.pb[1]
            for kc in range(NCH):
                m.mm(ps.v, w[:, kc, :], self.xb[kc][g].v, start=(kc == 0), stop=(kc == NCH - 1))
            for kc in range(NCH):
                m.mm(pp.v, wp[:, kc, :], self.xb[kc][g].v, start=(kc == 0), stop=(kc == NCH - 1))
            m.tt(t1[R6, :], ps[R6, :], self.C96[R6, gs], ALU.mult)
            m.tt(t2[R6, :], pp[R6, :], self.S96[R6, gs], ALU.mult)
            m.tt(kpe[R6, gs], t1[R6, :], t2[R6, :], ALU.add)
        scale = 96.0 ** -0.5
        R96 = slice(0, 96)
        R64 = slice(0, 64)
        for h in range(8):
            wq = self.load_fm(l, f"uq_{h}")
            wqp = self.load_fm(l, f"uqp_{h}")
            wk = self.load_fm(l, f"ukn_{h}")
            q_, k_ = qh[h % 2], kh[h % 2]
            for g in range(NG):
                gs = slice(g * GW, (g + 1) * GW)
                ps, pp, pk = self.pb[0], self.pb[1], self.pb[2]
                for k in range(3):
                    m.mm(ps[R96, :], wq[:, k, 0:96], cqg[k][:, gs], start=(k == 0), stop=(k == 2))
                for k in range(3):
                    m.mm(pp[R96, :], wqp[:, k, 0:96], cqg[k][:, gs], start=(k == 0), stop=(k == 2))
                for k in range(2):
                    m.mm(pk[R64, :], wk[:, k, 0:64], ckvg[k][:, gs], start=(k == 0), stop=(k == 1))
                m.copy(t1[R64, :], ps[R64, :], "act")
                m.tt(t1[R6, :], ps[R6, :], self.C96[R6, gs], ALU.mult)
                m.tt(t2[R6, :], pp[R6, :], self.S96[R6, gs], ALU.mult)
                m.tt(t1[R6, :], t1[R6, :], t2[R6, :], ALU.add)
                m.tt(q_[R96, gs], t1[R96, :], rq[R96, gs], ALU.mult)
                m.tt(k_[R64, gs], pk[R64, :], rkv[R64, gs], ALU.mult)
                m.copy(k_[R6, gs], kpe[R6, gs], "act")
            for qg in range(NG):
                qs = slice(qg * GW, (qg + 1) * GW)
                O, R = self.pb[4], self.pb[5]
                nk = 4 * qg + 4
                for kt in range(nk):
                    st = self.pb[2 + u % 2]
                    pt = PT[u % 2]
                    u += 1
                    ks = slice(kt * 128, (kt + 1) * 128)
                    diag = kt >= 4 * qg
                    m.mm(st.v, k_[R96, ks], q_[R96, qs], start=True, stop=not diag)
                    if diag:
                        rel = kt - 4 * qg
                        m.mm(st.v, self.identb.v, self.causneg[:, rel * GW:(rel + 1) * GW], start=False, stop=True)
                    m.act(pt.v, st.v, AF.Exp, scale=scale)
                    m.mm(O[R64, :], vtm[kt][:, 64 * h:64 * h + 64], pt.v, start=(kt == 0), stop=(kt == nk - 1))
                    m.mm(R[R64, :], self.onesb[:, 0:64], pt.v, start=(kt == 0), stop=(kt == nk - 1))
                m.recip(rr[R64, :], R[R64, :])
                y = yo[(h * NG + qg) % 2]
                m.tt(y.v, O[R64, :], rr[R64, :], ALU.mult)
                m.dma(self.yc_d[h][:, qs], y.v)

    def phase(self):
        m = self.m
        self.aoff = 0
        bar = {}
        for e in ("pe", "act", "dve", "pool"):
            if m.cnt[e] > 0:
                bar[m.cursem[e]] = m.cnt[e]
        for qn in ("sp", "pool"):
            n = m.dcnt[qn]
            k = len(m.dsem[qn])
            for j, sid in enumerate(m.dsem[qn]):
                cntj = (n - j + k - 1) // k if n > j else 0
                if cntj > 0:
                    bar[sid] = 16 * cntj
        self.barrier = bar

    def carve(self, name, shape, dtype=F32):
        esz = 4 if dtype in (F32, I32) else 2
        free = 1
        for d_ in shape[1:]:
            free *= d_
        n32 = (free * esz + 3) // 4
        n32 = (n32 + 7) // 8 * 8
        assert self.aoff + n32 <= self.AW, (name, self.aoff, n32)
        ap = self.arena.ap[:, self.aoff:self.aoff + n32]
        self.aoff += n32
        if dtype != F32:
            ap = ap.bitcast(dtype)
        ap = ap[:, 0:free]
        if len(shape) == 3:
            ap = ap.rearrange("p (a b) -> p a b", a=shape[1])
        if shape[0] < 128:
            ap = ap[0:shape[0]]
        b = Buf(name, ap)
        b.r = dict(self.barrier)
        return b

    def load_fm(self, l, name, dst=None):
        m = self.m
        if dst is None:
            dst = self.wslot[self.wslot_i % self.nslot]
            self.wslot_i += 1
        tid = l * self.nt[0] + self.idx[name]
        m.dma(dst.v.rearrange("p k n -> p (k n)"), self.din["wfm"][tid], "pool")
        return dst

    def load_tm(self, l, name):
        m = self.m
        dst = self.tslot[self.tslot_i % 2]
        self.tslot_i += 1
        tid = l * self.nt[1] + self.idx[name]
        m.dma(dst.v.rearrange("p k n -> p (k n)"), self.din["wtm"][tid], "pool")
        return dst

    def resid_ln(self, l, i, g, delta_fn, final=False):
        m = self.m
        gs = slice(g * GW, (g + 1) * GW)
        A, B = self.pb[6], self.pb[7]
        for c in range(NCH):
            xs = self.xstage[self.cnt_stage % 2]
            self.cnt_stage += 1
            m.dma(xs.v, self.xres[c * 128:(c + 1) * 128, gs])
            ps = delta_fn(c)
            y = self.ybuf[c]
            m.stt(y.v, xs.v, ALPHA, ps, ALU.mult, ALU.add)
            sq = self.sq[c % 2]
            m.act(sq.v, y.v, AF.Square)
            m.mm(A.v, self.ones32.v, y.v, start=(c == 0), stop=(c == NCH - 1))
            m.mm(B.v, self.ones32.v, sq.v, start=(c == 0), stop=(c == NCH - 1))
        mean, rstd = self.mean, self.rstd
        t0 = self.tmpa[0]
        m.act(mean.v, A.v, AF.Copy, scale=1.0 / D)
        m.tt(t0.v, mean.v, mean.v, ALU.mult)
        m.stt(t0.v, B.v, 1.0 / D, t0.v, ALU.mult, ALU.subtract)
        m.ts(t0.v, t0.v, LN_EPS, ALU.add)
        m.act(t0.v, t0.v, AF.Sqrt)
        m.recip(rstd.v, t0.v)
        for c in range(NCH):
            y = self.ybuf[c]
            t = self.tmpa[c % 2]
            m.tt(t.v, y.v, mean.v, ALU.subtract)
            m.tt(t.v, t.v, rstd.v, ALU.mult)
            xn = self.xn[c % 2]
            col = (l * 4 + i) * NCH + c
            m.act(xn.v, t.v, AF.Identity, scale=self.lng[:, col:col + 1], bias=self.lnb[:, col:col + 1])
            if final:
                ev = m.dma(self.out[c * 128:(c + 1) * 128, gs], xn.v)
                self.final_events.append(ev)
            else:
                m.dma(self.xres[c * 128:(c + 1) * 128, gs], xn.v)
                m.copy(self.xb[c][g].v, xn.v, "act")

    def init_stream(self):
        m = self.m
        for c in range(NCH):
            for g in range(NG):
                gs = slice(g * GW, (g + 1) * GW)
                xs = self.xstage[self.cnt_stage % 2]
                self.cnt_stage += 1
                m.dma(xs.v, self.din["xT"][c * 128:(c + 1) * 128, gs])
                m.dma(self.xres[c * 128:(c + 1) * 128, gs], xs.v)
                m.copy(self.xb[c][g].v, xs.v, "act")

    def ffn(self, l, i, lni, final=False):
        m = self.m
        self.phase()
        w2res = [self.carve(f"w2res{c}", [128, NJ, 128], BF16) for c in range(NCH)]
        hT = [[self.carve(f"hT{gi}_{j}", [128, GW], BF16) for j in range(NJ)] for gi in range(2)]
        silu = [self.carve(f"silu{k}", [128, GW]) for k in range(2)]
        for c in range(NCH):
            tid = (l * 2 + i) * NCH + c
            m.dma(w2res[c].v.rearrange("p j n -> p (j n)"), self.din["w2"][tid], "pool")
        k = 0
        for half in range(2):
            for j in range(NJ):
                w1 = self.load_fm(l, f"w1_{i}_{j}")
                w3 = self.load_fm(l, f"w3_{i}_{j}")
                for gi in range(2):
                    g = 2 * half + gi
                    p1 = self.pb[(k % 2) * 2]
                    p3 = self.pb[(k % 2) * 2 + 1]
                    for kc in range(NCH):
                        m.mm(p1.v, w1[:, kc, :], self.xb[kc][g].v, start=(kc == 0), stop=(kc == NCH - 1))
                    for kc in range(NCH):
                        m.mm(p3.v, w3[:, kc, :], self.xb[kc][g].v, start=(kc == 0), stop=(kc == NCH - 1))
                    sl = silu[k % 2]
                    k += 1
                    m.act(sl.v, p1.v, AF.Silu)
                    m.stt(hT[gi][j].v, sl.v, 0.5, p3.v, ALU.mult, ALU.mult)
            for gi in range(2):
                g = 2 * half + gi

                def delta(c, gi=gi):
                    ps = self.pb[4 + (c % 2)]
                    for j in range(NJ):
                        m.mm(ps.v, w2res[c][:, j, :], hT[gi][j].v, start=(j == 0), stop=(j == NJ - 1))
                    return ps.v
                self.resid_ln(l, lni, g, delta, final=final)

    def merge(self, l):
        m = self.m
        self.phase()
        wres = [self.carve(f"woutres{c}", [128, 8, 128], BF16) for c in range(NCH)]
        ya = [self.carve(f"ya{h}", [64, GW], BF16) for h in range(8)]
        yb = [self.carve(f"yb{h}", [128, GW], BF16) for h in range(4)]
        yc = [self.carve(f"yc{h}", [64, GW], BF16) for h in range(8)]
        sgt = [self.carve(f"sgt{k}", [128, GW]) for k in range(2)]
        macc = self.carve("macc", [128, GW])
        mtmp = self.carve("mtmp", [128, GW])
        mixed = [self.carve(f"mixed{c}", [128, GW], BF16) for c in range(NCH)]
        for c in range(NCH):
            self.load_fm(l, f"wout_{c}", wres[c])
        k = 0
        for g in range(NG):
            gs = slice(g * GW, (g + 1) * GW)
            for h in range(8):
                m.dma(ya[h].v, self.ya_d[h][:, gs])
                m.dma(yc[h].v, self.yc_d[h][:, gs])
            for h in range(4):
                m.dma(yb[h].v, self.yb_d[h][:, gs])
            for c in range(NCH):
                for b in range(3):
                    wm = self.load_fm(l, f"merge_{b}_{c}")
                    wb = self.load_fm(l, f"wbr_{b}_{c}")
                    psm = self.pb[(k % 2) * 2]
                    psy = self.pb[(k % 2) * 2 + 1]
                    for kc in range(NCH):
                        m.mm(psm.v, wm[:, kc, :], self.xb[kc][g].v, start=(kc == 0), stop=(kc == NCH - 1))
                    if b == 1:
                        for h in range(4):
                            m.mm(psy.v, wb[:, h, :], yb[h].v, start=(h == 0), stop=(h == 3))
                    else:
                        ys = ya if b == 0 else yc
                        for h in range(8):
                            m.mm(psy.v, wb[0:64, h, :], ys[h].v, start=(h == 0), stop=(h == 7))
                    sg = sgt[k % 2]
                    k += 1
                    m.act(sg.v, psm.v, AF.Sigmoid)
                    if b == 0:
                        m.tt(macc.v, sg.v, psy.v, ALU.mult)
                    else:
                        m.tt(mtmp.v, sg.v, psy.v, ALU.mult)
                        m.tt(macc.v, macc.v, mtmp.v, ALU.add)
                m.copy(mixed[c].v, macc.v, "act")

            def delta(c):
                ps = self.pb[4 + (c % 2)]
                for kc in range(NCH):
                    m.mm(ps.v, wres[c][:, kc, :], mixed[kc].v, start=(kc == 0), stop=(kc == NCH - 1))
                return ps.v
            self.resid_ln(l, 1, g, delta)

    def xattn(self, l):
        m = self.m
        self.phase()
        wres = [self.carve(f"wores{c}", [128, 8, 128], BF16) for c in range(NCH)]
        kT = [self.carve(f"xkT{c}", [128, 256], BF16) for c in range(NCH)]
        vtm = [self.carve(f"xv{k}", [128, 1024], BF16) for k in range(2)]
        qT = [self.carve(f"xq{c}", [128, GW], BF16) for c in range(NCH)]
        at = [self.carve(f"xat{c}", [128, GW], BF16) for c in range(NCH)]
        PT = [self.carve(f"xPT{k}", [128, GW], BF16) for k in range(2)]
        rr = self.carve("xrr", [128, GW])
        self.memT = self.carve("memT", [128, 8, 256], BF16)
        m.dma(self.memT.v.rearrange("p k n -> p (k n)"), self.din["memT"].v, "pool")
        for c in range(NCH):
            self.load_fm(l, f"xo_{c}", wres[c])
        for c in range(NCH):
            wk = self.load_fm(l, f"xk_{c}")
            ps = self.pb[c % 2]
            for kc in range(NCH):
                m.mm(ps[:, 0:256], wk[:, kc, :], self.memT[:, kc, :], start=(kc == 0), stop=(kc == NCH - 1))
            m.copy(kT[c].v, ps[:, 0:256], "act")
        for hv in range(2):
            wv = self.load_tm(l, f"xv_{hv}")
            for kt in range(2):
                ps = self.pb[2 + kt]
                for kc in range(NCH):
                    m.mm(ps.v, self.memT[:, kc, kt * 128:(kt + 1) * 128], wv[:, kc, :], start=(kc == 0), stop=(kc == NCH - 1))
                m.copy(vtm[kt][:, hv * 512:(hv + 1) * 512], ps.v, "act")
        u = 0
        for g in range(NG):
            for c in range(NCH):
                wq = self.load_fm(l, f"xq_{c}")
                ps = self.pb[c % 2]
                for kc in range(NCH):
                    m.mm(ps.v, wq[:, kc, :], self.xb[kc][g].v, start=(kc == 0), stop=(kc == NCH - 1))
                m.copy(qT[c].v, ps.v, "act")
            for h in range(4):
                O0, O1, R = self.pb[2], self.pb[3], self.pb[5]
                for kt in range(2):
                    st = self.pb[u % 2]
                    pt = PT[u % 2]
                    u += 1
                    ks = slice(kt * 128, (kt + 1) * 128)
                    m.mm(st.v, kT[2 * h][:, ks], qT[2 * h].v, start=True, stop=False)
                    m.mm(st.v, kT[2 * h + 1][:, ks], qT[2 * h + 1].v, start=False, stop=True)
                    m.act(pt.v, st.v, AF.Exp, scale=1.0 / 16.0)
                    m.mm(O0.v, vtm[kt][:, 256 * h:256 * h + 128], pt.v, start=(kt == 0), stop=(kt == 1))
                    m.mm(O1.v, vtm[kt][:, 256 * h + 128:256 * h + 256], pt.v, start=(kt == 0), stop=(kt == 1))
                    m.mm(R.v, self.onesb.v, pt.v, start=(kt == 0), stop=(kt == 1))
                m.recip(rr.v, R.v)
                m.tt(at[2 * h].v, O0.v, rr.v, ALU.mult)
                m.tt(at[2 * h + 1].v, O1.v, rr.v, ALU.mult)

            def delta(c):
                ps = self.pb[4] if c % 2 == 0 else self.pb[1]
                for kc in range(NCH):
                    m.mm(ps.v, wres[c][:, kc, :], at[kc].v, start=(kc == 0), stop=(kc == NCH - 1))
                return ps.v
            self.resid_ln(l, 2, g, delta)

    def mixers(self, l):
        if self.ext_y is True:
            return
        if self.ext_y == "nsa" or (not self.ext_y and NSA_ENABLED):
            self.nsa(l)
        elif not self.ext_y and l == 0:
            z = self.tmpa[0]
            self.m.memset(z.v, 0.0)
            zb = self.xn[0]
            for h in range(8):
                for g in range(NG):
                    self.m.dma(self.ya_d[h][:, g * GW:(g + 1) * GW], V(z, z.ap.bitcast(BF16))[0:64, 0:GW])
        if not self.ext_y or self.ext_y == "hgrn":
            self.hgrn(l)
        if not self.ext_y or self.ext_y == "mla":
            self.mla(l)

    def dump_stream(self):
        m = self.m
        for c in range(NCH):
            for g in range(NG):
                gs = slice(g * GW, (g + 1) * GW)
                xs = self.xstage[self.cnt_stage % 2]
                self.cnt_stage += 1
                m.dma(xs.v, self.xres[c * 128:(c + 1) * 128, gs])
                ev = m.dma(self.out[c * 128:(c + 1) * 128, gs], xs.v)
                self.final_events.append(ev)

    def build(self):
        self.init_stream()
        if self.stop == "nsa_only":
            try:
                self.nsa(0)
            except StopIteration:
                pass
            self.final_events.append(self.ya_d.w[0] if self.ya_d.w else None)
            self.m.emit([e for e in self.final_events if e])
            return self.nc
        for l in range(self.n_layers):
            last = (l == DEPTH - 1)
            self.ffn(l, 0, 0)
            if self.stop == "ffn1":
                self.dump_stream()
                break
            self.mixers(l)
            if self.stop == "mixers":
                self.dump_stream()
                break
            self.merge(l)
            if self.stop == "mix":
                self.dump_stream()
                break
            self.xattn(l)
            if self.stop == "xa":
                self.dump_stream()
                break
            self.ffn(l, 1, 3, final=last)
            if self.stop == "full" and l == self.n_layers - 1 and not last:
                self.dump_stream()
        self.m.emit(self.final_events)
        return self.nc


def kernel(**inputs):
    shared, per_core, idx, nt = prep_inputs(inputs)
    shapes = {k: v.shape for k, v in shared.items()}
    shapes.update({k: v.shape for k, v in per_core[0].items()})
    prog = Prog(shapes, idx, nt)
    nc = prog.build()
    in_maps = [dict(shared, **pc) for pc in per_core]
    res = run_bass_kernel_spmd(nc, in_maps, core_ids=list(range(len(per_core))))
    out = np.stack([np.ascontiguousarray(r["out"].T) for r in res.results], axis=0)
    return out.astype(np.float32)
```
